# Optimizing a Trainium2 kernel written in Bass

```python
import math
import jax, jax.numpy as jnp
from jax import lax
import numpy as np

D_MODEL = 2048
BATCH = 16
SEQ = 2048
DEPTH = 4

N_A_LAYERS = DEPTH // 2
N_B_LAYERS = DEPTH - N_A_LAYERS
HEAD_DIM = 128
N_HEADS = D_MODEL // HEAD_DIM
N_KV_A = 4
GROUP_A = N_HEADS // N_KV_A
IDX_HEADS = 16
IDX_DIM = 64
INDEX_TOPK = 256
PLE_DIM = 256
ROPE_THETA = 10000.0
A_BLOCK = 64
B_BLOCK = 128
LN_EPS = 1e-5
DN_ALPHA = (2 * DEPTH) ** 0.25
DN_BETA = (8 * DEPTH) ** -0.25
ATTN_W = N_HEADS * HEAD_DIM
KV_W_A = N_KV_A * HEAD_DIM
A_SPLITS = (ATTN_W, KV_W_A, KV_W_A, ATTN_W, IDX_HEADS * IDX_DIM, IDX_HEADS, IDX_DIM)
A_IN_WIDTH = sum(A_SPLITS)
NEG_BIG = -1e30

kernel_name = "yoco_dsa_stickbreaking_hybrid"


def _split(h, sizes):
    idx, acc = [], 0
    for s in sizes[:-1]:
        acc += s
        idx.append(acc)
    return jnp.split(h, idx, axis=-1)


def _layer_norm(x, g, b):
    xf = x.astype(jnp.float32)
    mu = jnp.mean(xf, axis=-1, keepdims=True)
    var = jnp.mean(jnp.square(xf - mu), axis=-1, keepdims=True)
    y = (xf - mu) * lax.rsqrt(var + LN_EPS) * g.astype(jnp.float32) + b.astype(jnp.float32)
    return y.astype(x.dtype)


def _rope(x, pos):
    half = x.shape[-1] // 2
    inv = ROPE_THETA ** (-jnp.arange(half, dtype=jnp.float32) / half)
    ang = pos.astype(jnp.float32)[..., None] * inv
    cos = jnp.cos(ang)[:, :, None, :]
    sin = jnp.sin(ang)[:, :, None, :]
    xf = x.astype(jnp.float32)
    x1, x2 = xf[..., :half], xf[..., half:]
    return jnp.concatenate([x1 * cos - x2 * sin, x2 * cos + x1 * sin], axis=-1).astype(x.dtype)


def _to_blocks(a, blk):
    b, s = a.shape[0], a.shape[1]
    return a.reshape(b, s // blk, blk, *a.shape[2:]).swapaxes(0, 1)


def _dsa_mixer(x, pos, w_in, w_out):
    B, S, _ = x.shape
    h = x @ w_in
    q, k, v, g, qi, wi, ki = _split(h, A_SPLITS)
    q = _rope(q.reshape(B, S, N_HEADS, HEAD_DIM), pos)
    k = _rope(k.reshape(B, S, N_KV_A, HEAD_DIM), pos)
    v = v.reshape(B, S, N_KV_A, HEAD_DIM)
    qi = _rope(qi.reshape(B, S, IDX_HEADS, IDX_DIM), pos)
    ki = _rope(ki.reshape(B, S, 1, IDX_DIM), pos)[:, :, 0]
    wi = wi.astype(jnp.float32) * (IDX_HEADS ** -0.5 * IDX_DIM ** -0.5)
    k_top = min(INDEX_TOPK, S // 4)
    n_blk = S // A_BLOCK
    key_pos = jnp.arange(S)
    scale = HEAD_DIM ** -0.5

    def block(args):
        i, qb, qib, wib = args
        t = i * A_BLOCK + jnp.arange(A_BLOCK)
        dots = jnp.einsum('bqhd,bsd->bqhs', qib, ki, preferred_element_type=jnp.float32)
        score = jnp.einsum('bqhs,bqh->bqs', jax.nn.relu(dots), wib)
        causal = key_pos[None, :] <= t[:, None]
        score = jnp.where(causal[None], score, NEG_BIG)
        _, sel = lax.top_k(score, k_top)
        valid = sel <= t[None, :, None]
        ks = jax.vmap(lambda kb, ib: kb[ib])(k, sel)
        vs = jax.vmap(lambda vb, ib: vb[ib])(v, sel)
        qg = qb.reshape(B, A_BLOCK, N_KV_A, GROUP_A, HEAD_DIM)
        s = jnp.einsum('bqgrd,bqkgd->bqgrk', qg, ks, preferred_element_type=jnp.float32) * scale
        s = jnp.where(valid[:, :, None, None, :], s, -jnp.inf)
        pr = jax.nn.softmax(s, axis=-1).astype(vs.dtype)
        o = jnp.einsum('bqgrk,bqkgd->bqgrd', pr, vs)
        return o.reshape(B, A_BLOCK, ATTN_W)

    o = lax.map(block, (jnp.arange(n_blk), _to_blocks(q, A_BLOCK),
                        _to_blocks(qi, A_BLOCK), _to_blocks(wi, A_BLOCK)))
    o = o.swapaxes(0, 1).reshape(B, S, ATTN_W)
    return (o * jax.nn.silu(g)) @ w_out


def _stick_breaking_mixer(x, k, v, w_q, w_out):
    B, S, _ = x.shape
    q, g = _split(x @ w_q, (ATTN_W, ATTN_W))
    q = q.reshape(B, S, N_HEADS, HEAD_DIM)
    n_blk = S // B_BLOCK
    key_pos = jnp.arange(S)
    scale = HEAD_DIM ** -0.5

    def block(args):
        i, qb = args
        t = i * B_BLOCK + jnp.arange(B_BLOCK)
        z = jnp.einsum('bqhd,bshd->bhqs', qb, k, preferred_element_type=jnp.float32) * scale
        strict = (key_pos[None, :] < t[:, None])[None, None]
        log_keep = jnp.where(strict, jax.nn.log_sigmoid(-z), 0.0)
        between = lax.cumsum(log_keep, axis=3, reverse=True) - log_keep
        a = jnp.where(strict, jnp.exp(jax.nn.log_sigmoid(z) + between), 0.0)
        o = jnp.einsum('bhqs,bshd->bqhd', a.astype(v.dtype), v)
        return o.reshape(B, B_BLOCK, ATTN_W)

    o = lax.map(block, (jnp.arange(n_blk), _to_blocks(q, B_BLOCK)))
    o = o.swapaxes(0, 1).reshape(B, S, ATTN_W)
    return (o * jax.nn.silu(g)) @ w_out


def setup_inputs(seed: int = 0) -> dict:
    key = jax.random.key(seed)
    ks = jax.random.split(key, 14)
    f32 = jnp.float32
    d_in = D_MODEL ** -0.5
    x = jax.random.normal(ks[0], (BATCH, SEQ, D_MODEL), f32)
    p = jax.random.normal(ks[1], (DEPTH, BATCH, SEQ, PLE_DIM), f32)
    offs = jax.random.randint(ks[2], (BATCH, 1), 0, 4096, dtype=jnp.int32)
    positions = (offs + jnp.arange(SEQ, dtype=jnp.int32)[None, :]).astype(jnp.int32)
    w_in_a = jax.random.normal(ks[3], (N_A_LAYERS, D_MODEL, A_IN_WIDTH), f32) * d_in
    w_out_a = jax.random.normal(ks[4], (N_A_LAYERS, ATTN_W, D_MODEL), f32) * (ATTN_W ** -0.5 * DN_BETA)
    w_q_b = jax.random.normal(ks[5], (N_B_LAYERS, D_MODEL, 2 * ATTN_W), f32) * d_in
    w_kv_b = jax.random.normal(ks[6], (D_MODEL, 2 * ATTN_W), f32) * d_in
    w_out_b = jax.random.normal(ks[7], (N_B_LAYERS, ATTN_W, D_MODEL), f32) * (ATTN_W ** -0.5 * DN_BETA)
    ln_g = 1.0 + 0.02 * jax.random.normal(ks[8], (DEPTH, D_MODEL), f32)
    ln_b = 0.02 * jax.random.normal(ks[9], (DEPTH, D_MODEL), f32)
    w_ple = jax.random.normal(ks[10], (DEPTH, PLE_DIM, D_MODEL), f32) * PLE_DIM ** -0.5
    w_ple_gate = jax.random.normal(ks[11], (DEPTH, D_MODEL, D_MODEL), f32) * d_in
    return {"x": x, "p": p, "positions": positions,
            "w_in_a": w_in_a, "w_out_a": w_out_a,
            "w_q_b": w_q_b, "w_kv_b": w_kv_b, "w_out_b": w_out_b,
            "ln_g": ln_g, "ln_b": ln_b, "w_ple": w_ple, "w_ple_gate": w_ple_gate}


def reference(x, p, positions, w_in_a, w_out_a, w_q_b, w_kv_b, w_out_b,
              ln_g, ln_b, w_ple, w_ple_gate):
    B, S, _ = x.shape
    k_b = None
    v_b = None
    for i in range(DEPTH):
        if i < N_A_LAYERS:
            y = _dsa_mixer(x, positions, w_in_a[i], w_out_a[i])
        else:
            j = i - N_A_LAYERS
            y = _stick_breaking_mixer(x, k_b, v_b, w_q_b[j], w_out_b[j])
        x = _layer_norm(DN_ALPHA * x + y, ln_g[i], ln_b[i])
        x = x + (p[i] @ w_ple[i]) * jax.nn.sigmoid(x @ w_ple_gate[i])
        if i == N_A_LAYERS - 1:
            kb, vb = _split(x @ w_kv_b, (ATTN_W, ATTN_W))
            k_b = kb.reshape(B, S, N_HEADS, HEAD_DIM)
            v_b = vb.reshape(B, S, N_HEADS, HEAD_DIM)
    return x
```

```python
import math
from contextlib import ExitStack

import numpy as np
import concourse.bass as bass
import concourse.mybir as mybir
from concourse.bass_utils import run_bass_kernel_spmd

F32 = mybir.dt.float32
BF16 = mybir.dt.bfloat16
I32 = mybir.dt.int32
AF = mybir.ActivationFunctionType
ALU = mybir.AluOpType

S = 2048
D = 2048
NT = 16
H = 16
DH = 128
A_IN = 6224
ALPHA = float((2 * 4) ** 0.25)
SCALE = float(DH ** -0.5)
NEG = -30000.0
PI = math.pi


class Buf:
    __slots__ = ("t", "w", "r", "name")

    def __init__(self, t, name=""):
        self.t = t
        self.w = None
        self.r = {}
        self.name = name

    def __getitem__(self, idx):
        return self.t[idx]


class Eng:
    def __init__(self, fw, name, eng, sem, self_sync=True):
        self.fw = fw
        self.name = name
        self.eng = eng
        self.sem = sem
        self.count = 0
        self.seen = {}
        self.self_sync = self_sync

    def wait(self, tok):
        if tok is None:
            return
        key, val = tok
        if val <= 0:
            return
        if key == self.name and not self.self_sync:
            return
        if self.seen.get(key, 0) >= val:
            return
        self.seen[key] = val
        self.eng.wait_ge(self.fw.sems[key], val)
        self.fw.nwaits += 1


class FW:
    def __init__(self, nc, ndma=32):
        self.nc = nc
        self.sems = {}
        self.nwaits = 0
        self.nops = 0
        self.engs = {}
        self.dma_keys = []
        self.dma_val = {}
        self.dma_rr = 0
        self.ndma = ndma

    def setup(self, stack):
        nc = self.nc
        for name, eng, ss in (("pe", nc.tensor, False), ("act", nc.scalar, True),
                              ("dve", nc.vector, True), ("pool", nc.gpsimd, True),
                              ("sp", nc.sync, False)):
            sem = stack.enter_context(nc.semaphore("s_" + name))
            self.sems[name] = sem
            self.engs[name] = Eng(self, name, eng, sem, ss)
        self.rings = {"sp": [], "pool": []}
        self.ring_rr = {"sp": 0, "pool": 0}
        for qn, pre in (("sp", "d"), ("pool", "g")):
            for i in range(self.ndma // 2):
                k = "%s%d" % (pre, i)
                self.sems[k] = stack.enter_context(nc.semaphore("s_" + k))
                self.dma_keys.append(k)
                self.rings[qn].append(k)
                self.dma_val[k] = 0
        self.pe = self.engs["pe"]
        self.act = self.engs["act"]
        self.dve = self.engs["dve"]
        self.pool = self.engs["pool"]
        self.sp = self.engs["sp"]

    def _deps(self, E, reads, writes):
        for t in reads:
            E.wait(t.w)
        for t in writes:
            E.wait(t.w)
            for k, v in t.r.items():
                E.wait((k, v))

    def op(self, E, fn, reads, writes):
        self._deps(E, reads, writes)
        ins = fn()
        self.nops += 1
        E.count += 1
        ins.then_inc(E.sem, 1)
        tok = (E.name, E.count)
        for t in reads:
            if t.r.get(E.name, 0) < E.count:
                t.r[E.name] = E.count
        for t in writes:
            t.w = tok
            t.r = {}
        return tok

    def dma(self, Q, out_b, in_b, out_ap, in_ap, **kw):
        outs = out_b if isinstance(out_b, (list, tuple)) else [out_b]
        ins_ = in_b if isinstance(in_b, (list, tuple)) else [in_b]
        self._deps(Q, ins_, outs)
        ring = self.rings[Q.name]
        k = ring[self.ring_rr[Q.name]]
        self.ring_rr[Q.name] = (self.ring_rr[Q.name] + 1) % len(ring)
        Q.wait((k, self.dma_val[k]))
        ins = Q.eng.dma_start(out=out_ap, in_=in_ap, **kw)
        self.dma_val[k] += 16
        ins.then_inc(self.sems[k], 16)
        tok = (k, self.dma_val[k])
        for b in ins_:
            b.r[k] = self.dma_val[k]
        for b in outs:
            b.w = tok
            b.r = {}
        self.nops += 1
        return tok

    def barrier(self):
        for E in self.engs.values():
            for E2 in self.engs.values():
                if E2 is not E:
                    E.wait((E2.name, E2.count))
            for k in self.dma_keys:
                E.wait((k, self.dma_val[k]))

    def finish(self, bufs):
        for b in bufs:
            self.sp.wait(b.w)


class StopBuild(Exception):
    pass


class Prog:
    def stage(self, name):
        if self.stop == name:
            self.stopped = True
        return self.stopped

    def __init__(self, nseq=2, nlayers=4, debug=False, stop=None):
        self.stop = stop
        self.stopped = False
        self.nseq = nseq
        self.nlayers = nlayers
        self.debug = debug
        self.nc = bass.Bass("TRN2", target_bir_lowering=False)
        self.rr = 0

    def sb(self, st, name, shape, dt):
        self.uid += 1
        return Buf(st.enter_context(self.nc.sbuf_tensor("%s_%d" % (name, self.uid), shape, dt)), name)

    def din(self, name, shape, dt):
        return self.nc.dram_tensor(name, shape, dt, kind="ExternalInput").ap()

    def dscr(self, name, shape, dt):
        kind = "ExternalOutput" if self.debug else "Internal"
        return self.nc.dram_tensor(name, shape, dt, kind=kind).ap()

    def evac_eng(self):
        self.rr += 1
        return self.fw.act if (self.rr & 1) else self.fw.dve

    def copy(self, E, out_ap, in_ap, reads, writes):
        nc = self.nc
        if E is self.fw.act:
            return self.fw.op(E, lambda: nc.scalar.copy(out=out_ap, in_=in_ap), reads, writes)
        elif E is self.fw.dve:
            return self.fw.op(E, lambda: nc.vector.tensor_copy(out=out_ap, in_=in_ap), reads, writes)
        else:
            return self.fw.op(E, lambda: nc.gpsimd.tensor_copy(out=out_ap, in_=in_ap), reads, writes)

    def load_w(self, wb, src2d, ncols):
        kc = src2d.shape[0] // 128
        self.fw.dma(self.fw.pool, wb, self.wdram, wb[:, 0:kc, 0:ncols],
                    src2d.rearrange("(kc p) n -> p kc n", p=128))

    def build(self):
        nc = self.nc
        ns = self.nseq
        self.uid = 0
        self.x_d = self.din("x", [ns, S, D], F32)
        self.p_d = self.din("p", [4, ns, S, 256], F32)
        self.pos_d = self.din("pos", [ns, S], I32)
        self.w_in_a = self.din("w_in_a", [2, D, A_IN], F32)
        self.w_out_a = self.din("w_out_a", [2, D, D], F32)
        self.w_q_b = self.din("w_q_b", [2, D, 2 * D], F32)
        self.w_kv_b = self.din("w_kv_b", [D, 2 * D], F32)
        self.w_out_b = self.din("w_out_b", [2, D, D], F32)
        self.ln_g = self.din("ln_g", [4, D], F32)
        self.ln_b = self.din("ln_b", [4, D], F32)
        self.w_ple = self.din("w_ple", [4, 256, D], F32)
        self.w_gate = self.din("w_ple_gate", [4, D, D], F32)
        self.invf_d = self.din("invf", [128, 2], F32)
        self.out_d = nc.dram_tensor("out", [ns, S, D], F32, kind="ExternalOutput").ap()
        self.xres = [self.dscr("xres%d" % i, [S, D], F32) for i in range(2)]
        self.qT_s = self.dscr("qT_s", [H, 128, S], BF16)
        self.gT_s = self.dscr("gT_s", [H, 128, S], BF16)
        self.biasT_s = self.dscr("biasT_s", [NT, 128, S], BF16)
        self.kTb_s = self.dscr("kTb_s", [H, 128, S], BF16)
        self.vb_s = self.dscr("vb_s", [H, 128, NT, 128], BF16)
        self.wdram = Buf(None, "weights")
        self.xin_b = Buf(None, "xin")
        self.xres_b = [[Buf(None) for _ in range(NT)] for _ in range(2)]
        self.out_b = [[Buf(None) for _ in range(NT)] for _ in range(ns)]
        self.qT_b = [[Buf(None) for _ in range(4)] for _ in range(H)]
        self.gT_b = [[Buf(None) for _ in range(4)] for _ in range(H)]
        self.biasT_b = [[Buf(None) for _ in range(NT)] for _ in range(NT)]
        self.kTb_b = [[Buf(None) for _ in range(4)] for _ in range(H)]
        self.vb_b = [Buf(None) for _ in range(H)]

        with ExitStack() as st:
            self.fw = fw = FW(nc)
            fw.setup(st)
            pzt = st.enter_context(nc.psum_tensor("pz", [128, 2048], F32))
            self.pzt = pzt
            self.pb = [Buf(pzt[:, i * 512:(i + 1) * 512], "pz%d" % i) for i in range(4)]
            for i in range(2):
                t = st.enter_context(nc.psum_tensor("pb%d" % (4 + i), [128, 512], F32))
                self.pb.append(Buf(t, "pb%d" % (4 + i)))
            self.ptb = [Buf(st.enter_context(nc.psum_tensor("ptb%d" % i, [128, 512], BF16)), "ptb%d" % i)
                        for i in range(2)]
            self.identf = self.sb(st, "identf", [128, 128], F32)
            self.ident = self.sb(st, "ident", [128, 128], BF16)
            self.ones_bf = self.sb(st, "ones_bf", [128, 128], BF16)
            self.invf = self.sb(st, "invf", [128, 2], F32)
            self.cst = self.sb(st, "cst", [128, 4], F32)
            self.big = self.sb(st, "big", [128, NT, S], BF16)
            idf = self.identf
            self.reg_neg = nc.gpsimd.to_reg(-1e30)
            self.reg_zero = nc.gpsimd.to_reg(0.0)
            fw.op(fw.pool, lambda: nc.gpsimd.memset(idf[:], 1.0), [], [idf])
            fw.op(fw.pool, lambda: nc.gpsimd.affine_select(out=idf[:], in_=idf[:], pattern=[[-1, 128]],
                                                           compare_op=ALU.is_equal, fill=0.0, base=0,
                                                           channel_multiplier=1), [idf], [idf])
            self.copy(fw.dve, self.ident[:], idf[:], [idf], [self.ident])
            fw.op(fw.dve, lambda: nc.vector.memset(self.ones_bf[:], 1.0), [], [self.ones_bf])
            fw.op(fw.dve, lambda: nc.vector.memset(self.cst[:, 0:1], PI), [], [self.cst])
            fw.op(fw.dve, lambda: nc.vector.memset(self.cst[:, 1:2], -PI), [], [self.cst])
            fw.op(fw.dve, lambda: nc.vector.memset(self.cst[:, 2:3], 0.0), [], [self.cst])
            fw.op(fw.dve, lambda: nc.vector.memset(self.cst[:, 3:4], 1.0), [], [self.cst])
            fw.dma(fw.sp, self.invf, self.wdram, self.invf[:], self.invf_d[:, :])

            for s in range(ns):
                if not self.stopped:
                    self.run_seq(s)
            outs = [b for row in self.out_b for b in row]
            if self.nlayers < 4:
                outs += [b for row in self.xres_b for b in row]
            fw.barrier()
            fw.finish(outs)
        return nc

    def run_seq(self, s):
        fw = self.fw
        for l in range(self.nlayers):
            self.tag = "s%dL%d_" % (s, l)
            if self.stopped:
                return
            if l == 0:
                src = self.x_d[s]
                src_b = [self.xin_b] * NT
            else:
                src = self.xres[(l - 1) % 2]
                src_b = self.xres_b[(l - 1) % 2]
            if l == 3:
                dst = self.out_d[s]
                dst_b = self.out_b[s]
            else:
                dst = self.xres[l % 2]
                dst_b = self.xres_b[l % 2]
            if True:
                aoT = self.big
                aoT_b = [[Buf(aoT.t) for _ in range(NT)] for _ in range(H)]
                if l < 2:
                    self.layer_a(s, l, src, src_b, aoT, aoT_b)
                else:
                    self.layer_b(s, l, src, src_b, aoT, aoT_b)
                fw.barrier()
                if self.stopped:
                    return
                w_out = self.w_out_a[l] if l < 2 else self.w_out_b[l - 2]
                self.epilogue(s, l, src, src_b, dst, dst_b, aoT, aoT_b, w_out)
                fw.barrier()

    def phase_T(self, st, src, src_b):
        nc, fw = self.nc, self.fw
        xT = self.big
        xT_b = [Buf(xT.t) for _ in range(NT)]
        with ExitStack() as st2:
            st2.enter_context(self.nc.named_scope(self.tag + "T"))
            xb = [self.sb(st2, "T_xb%d" % i, [128, D], BF16) for i in range(2)]
            for i in range(NT):
                b = xb[i % 2]
                fw.dma(fw.pool, b, src_b[i], b[:], src[i * 128:(i + 1) * 128, :])
                for c4 in range(4):
                    pt = self.ptb[(i * 4 + c4) % 2]

                    def f(pt=pt, b=b, c4=c4):
                        for c in range(4):
                            k = c4 * 4 + c
                            ins = nc.tensor.transpose(pt[:, c * 128:(c + 1) * 128], b[:, k * 128:(k + 1) * 128],
                                                      self.ident[:])
                        return ins
                    fw.op(fw.pe, f, [b, self.ident], [pt])
                    self.copy(self.evac_eng(), xT[:, c4 * 4:(c4 + 1) * 4, i * 128:(i + 1) * 128],
                              pt[:].rearrange("p (c t) -> p c t", c=4), [pt], [xT_b[i]])
            fw.barrier()
        self.stage("T")
        return xT, xT_b

    def rope_tables(self, st, s, col, period):
        nc, fw = self.nc, self.fw
        cosT = self.sb(st, "cosT", [128, S], F32)
        sinT = self.sb(st, "sinT", [128, S], F32)
        with ExitStack() as st2:
            posi = self.sb(st2, "posi", [128, S], I32)
            ang = self.sb(st2, "ang", [128, S], F32)
            fw.dma(fw.sp, posi, self.wdram, posi[:], self.pos_d[s:s + 1, :].partition_broadcast(128))
            self.copy(fw.dve, ang[:], posi[:], [posi], [ang])
            fw.op(fw.dve, lambda: nc.vector.tensor_scalar(out=ang[:], in0=ang[:], scalar1=self.invf[:, col:col + 1],
                                                          scalar2=None, op0=ALU.mult), [ang, self.invf], [ang])
            tmp = posi
            tmpf = tmp[:].bitcast(F32)
            C1 = 6.28125
            C2 = 2 * PI - 6.28125
            TS = nc.vector.tensor_scalar
            STT = nc.vector.scalar_tensor_tensor
            fw.op(fw.dve, lambda: TS(out=posi[:], in0=ang[:], scalar1=1.0 / (2 * PI), scalar2=None, op0=ALU.mult),
                  [ang], [posi])
            self.copy(fw.dve, cosT[:], posi[:], [posi], [cosT])
            fw.op(fw.dve, lambda: STT(out=ang[:], in0=cosT[:], scalar=-C1, in1=ang[:], op0=ALU.mult, op1=ALU.add),
                  [cosT, ang], [ang])
            fw.op(fw.dve, lambda: STT(out=ang[:], in0=cosT[:], scalar=-C2, in1=ang[:], op0=ALU.mult, op1=ALU.add),
                  [cosT, ang], [ang])
            fw.op(fw.dve, lambda: TS(out=sinT[:], in0=ang[:], scalar1=PI, scalar2=-2 * PI, op0=ALU.is_gt, op1=ALU.mult),
                  [ang], [sinT])
            fw.op(fw.dve, lambda: nc.vector.tensor_tensor(out=ang[:], in0=ang[:], in1=sinT[:], op=ALU.add),
                  [ang, sinT], [ang])
            fw.op(fw.dve, lambda: TS(out=sinT[:], in0=ang[:], scalar1=-PI, scalar2=2 * PI, op0=ALU.is_lt, op1=ALU.mult),
                  [ang], [sinT])
            fw.op(fw.dve, lambda: nc.vector.tensor_tensor(out=ang[:], in0=ang[:], in1=sinT[:], op=ALU.add),
                  [ang, sinT], [ang])
            fw.op(fw.dve, lambda: TS(out=ang[:], in0=ang[:], scalar1=-PI, scalar2=PI, op0=ALU.max, op1=ALU.min),
                  [ang], [ang])
            fw.op(fw.dve, lambda: TS(out=tmpf, in0=ang[:], scalar1=0.5 * PI, scalar2=None, op0=ALU.add), [ang], [tmp])
            fw.op(fw.dve, lambda: TS(out=cosT[:], in0=tmpf, scalar1=PI, scalar2=-2 * PI, op0=ALU.is_gt, op1=ALU.mult),
                  [tmp], [cosT])
            fw.op(fw.dve, lambda: nc.vector.tensor_tensor(out=tmpf, in0=tmpf, in1=cosT[:], op=ALU.add), [tmp, cosT], [tmp])
            fw.op(fw.dve, lambda: TS(out=tmpf, in0=tmpf, scalar1=-PI, scalar2=PI, op0=ALU.max, op1=ALU.min), [tmp], [tmp])
            half = period // 2
            for g0 in range(0, 128, period):
                fw.op(fw.act, lambda g0=g0: nc.scalar.activation(out=sinT[g0:g0 + half, :], in_=ang[g0:g0 + half, :],
                                                                 func=AF.Sin, scale=1.0), [ang], [sinT])
                fw.op(fw.act, lambda g0=g0: nc.scalar.activation(out=sinT[g0 + half:g0 + period, :],
                                                                 in_=ang[g0 + half:g0 + period, :],
                                                                 func=AF.Sin, scale=-1.0), [ang], [sinT])
            fw.op(fw.act, lambda: nc.scalar.activation(out=cosT[:], in_=tmpf, func=AF.Sin, scale=1.0), [tmp], [cosT])
        return cosT, sinT

    def rope_post(self, ps, raw, t1, t2, cosT, sinT, tb, period, out_ap, out_bufs):
        nc, fw = self.nc, self.fw
        cs = slice(tb * 512, (tb + 1) * 512)
        self.copy(fw.act, raw[:], ps[:], [ps], [raw])
        fw.op(fw.dve, lambda: nc.vector.tensor_tensor(out=t1[:], in0=raw[:], in1=cosT[:, cs], op=ALU.mult),
              [raw, cosT], [t1])
        half = period // 2
        for g0 in range(0, 128, period):
            a0, a1, a2 = g0, g0 + half, g0 + period
            fw.op(fw.dve, lambda a0=a0, a1=a1, a2=a2: nc.vector.tensor_tensor(
                out=t2[a0:a1, :], in0=raw[a1:a2, :], in1=sinT[a1:a2, cs], op=ALU.mult), [raw, sinT], [t2])
            fw.op(fw.dve, lambda a0=a0, a1=a1, a2=a2: nc.vector.tensor_tensor(
                out=t2[a1:a2, :], in0=raw[a0:a1, :], in1=sinT[a0:a1, cs], op=ALU.mult), [raw, sinT], [t2])
        fw.op(fw.dve, lambda: nc.vector.tensor_tensor(out=out_ap, in0=t1[:], in1=t2[:], op=ALU.add),
              [t1, t2], out_bufs)

    def fm_chunk(self, ps, wb, c, xT, xT_b, tb):
        nc, fw = self.nc, self.fw

        def f():
            for k in range(NT):
                ins = nc.tensor.matmul(ps[:], lhsT=wb[:, k, c * 128:(c + 1) * 128],
                                       rhs=xT[:, k, tb * 512:(tb + 1) * 512], start=(k == 0), stop=(k == NT - 1))
            return ins
        fw.op(fw.pe, f, [wb] + xT_b[tb * 4:(tb + 1) * 4], [ps])

    def layer_a(self, s, l, src, src_b, aoT, aoT_b):
        nc, fw = self.nc, self.fw
        W = self.w_in_a[l]
        with ExitStack() as st_kv:
            kT = self.sb(st_kv, "kT", [128, 4, S], BF16)
            kT_b = [[Buf(kT.t) for _ in range(4)] for _ in range(4)]
            v = self.sb(st_kv, "v", [128, NT, 512], BF16)
            v_b = [Buf(v.t) for _ in range(NT)]
            with ExitStack() as st_idx:
                qiT = self.sb(st_idx, "qiT", [128, 8, S], BF16)
                qiT_b = [[Buf(qiT.t) for _ in range(4)] for _ in range(8)]
                kiT = self.sb(st_idx, "kiT", [128, S], BF16)
                kiT_b = [Buf(kiT.t) for _ in range(4)]
                wi = self.sb(st_idx, "wi", [128, NT, 16], F32)
                wi_b = [Buf(wi.t) for _ in range(NT)]
                with ExitStack() as st:
                    xT, xT_b = self.phase_T(st, src, src_b)
                    if self.stopped:
                        return
                    wb = [self.sb(st, "wb%d" % i, [128, NT, 256], BF16) for i in range(2)]
                    raw = [self.sb(st, "raw%d" % i, [128, 512], F32) for i in range(2)]
                    t1 = [self.sb(st, "t1%d" % i, [128, 512], F32) for i in range(2)]
                    t2 = [self.sb(st, "t2%d" % i, [128, 512], F32) for i in range(2)]
                    stg = [self.sb(st, "stg%d" % i, [128, 512], BF16) for i in range(4)]
                    st.enter_context(nc.named_scope(self.tag + "projA"))
                    gi = 0
                    pi = 0
                    si = 0
                    ri = 0
                    with ExitStack() as st_r:
                        cosT, sinT = self.rope_tables(st_r, s, 0, 128)
                        for grp in range(10):
                            col0 = grp * 256
                            w = wb[gi % 2]; gi += 1
                            self.load_w(w, W[:, col0:col0 + 256], 256)
                            for c in range(2):
                                ch = grp * 2 + c
                                for tb in range(4):
                                    ps = self.pb[pi % 4]; pi += 1
                                    self.fm_chunk(ps, w, c, xT, xT_b, tb)
                                    r = ri % 2; ri += 1
                                    if ch < 16:
                                        sg = stg[si % 4]; si += 1
                                        self.rope_post(ps, raw[r], t1[r], t2[r], cosT, sinT, tb, 128, sg[:], [sg])
                                        fw.dma(fw.sp, self.qT_b[ch][tb], sg, self.qT_s[ch, :, tb * 512:(tb + 1) * 512], sg[:])
                                    else:
                                        g = ch - 16
                                        self.rope_post(ps, raw[r], t1[r], t2[r], cosT, sinT, tb, 128,
                                                       kT[:, g, tb * 512:(tb + 1) * 512], [kT_b[g][tb]])
                        fw.barrier()
                        if self.stage("qk"):
                            return
                    with ExitStack() as st_r:
                        cosT, sinT = self.rope_tables(st_r, s, 1, 64)
                        for grp in range(4):
                            col0 = 5120 + grp * 256
                            w = wb[gi % 2]; gi += 1
                            self.load_w(w, W[:, col0:col0 + 256], 256)
                            for c in range(2):
                                ch = grp * 2 + c
                                for tb in range(4):
                                    ps = self.pb[pi % 4]; pi += 1
                                    self.fm_chunk(ps, w, c, xT, xT_b, tb)
                                    r = ri % 2; ri += 1
                                    self.rope_post(ps, raw[r], t1[r], t2[r], cosT, sinT, tb, 64,
                                                   qiT[:, ch, tb * 512:(tb + 1) * 512], [qiT_b[ch][tb]])
                        w = wb[gi % 2]; gi += 1
                        Wk = W[:, 6160:6224].rearrange("(kc p) n -> p kc n", p=128)
                        fw.dma(fw.pool, w, self.wdram, w[:, :, 0:64], Wk)
                        fw.dma(fw.pool, w, self.wdram, w[:, :, 64:128], Wk)
                        fw.dma(fw.pool, w, self.wdram, w[:, :, 128:144],
                               W[:, 6144:6160].rearrange("(kc p) n -> p kc n", p=128))
                        for tb in range(4):
                            ps = self.pb[pi % 4]; pi += 1
                            self.fm_chunk(ps, w, 0, xT, xT_b, tb)
                            r = ri % 2; ri += 1
                            self.rope_post(ps, raw[r], t1[r], t2[r], cosT, sinT, tb, 64,
                                           kiT[:, tb * 512:(tb + 1) * 512], [kiT_b[tb]])
                        for i in range(NT):
                            ps = self.pb[pi % 4]; pi += 1

                            def f(ps=ps, i=i, w=w):
                                for k in range(NT):
                                    ins = nc.tensor.matmul(ps[:, 0:16], lhsT=xT[:, k, i * 128:(i + 1) * 128],
                                                           rhs=w[:, k, 128:144], start=(k == 0), stop=(k == NT - 1))
                                return ins
                            fw.op(fw.pe, f, [w, xT_b[i]], [ps])
                            fw.op(fw.act, lambda ps=ps, i=i: nc.scalar.mul(out=wi[:, i, :], in_=ps[:, 0:16], mul=1.0 / 32.0),
                                  [ps], [wi_b[i]])
                        fw.barrier()
                    self.xT_keep = (xT, xT_b)
                    fw.barrier()
                if self.stage("proj"):
                    return
                self.indexer(qiT, qiT_b, kiT, kiT_b, wi, wi_b, W, v, v_b)
                fw.barrier()
                if self.stage("idx"):
                    return
            self.attn_a(kT, kT_b, v, v_b, aoT, aoT_b)
            fw.barrier()
            if self.stage("attn"):
                return

    def indexer(self, qiT, qiT_b, kiT, kiT_b, wi, wi_b, W, v, v_b):
        nc, fw = self.nc, self.fw
        with ExitStack() as st:
            st.enter_context(nc.named_scope(self.tag + "idx"))
            score = [self.sb(st, "score%d" % i, [128, S], F32) for i in range(2)]
            work = self.sb(st, "work", [128, S], F32)
            m8 = self.sb(st, "m8", [128, 8], F32)
            brow = [self.sb(st, "brow%d" % i, [128, S], BF16) for i in range(2)]
            rl = [self.sb(st, "rl%d" % i, [128, 512], BF16) for i in range(4)]
            diag = [self.sb(st, "diag%d" % i, [128, 16, 128], BF16) for i in range(2)]
            bst = [self.sb(st, "bst%d" % i, [128, 4, 128], BF16) for i in range(2)]
            cnt = {"ri": 0, "pi": 0, "bi": 0, "ti": 0, "di": 0, "si": 0}
            xT, xT_b = self.xT_keep
            wb2 = [self.sb(st, "iwb%d" % i, [128, NT, 128], BF16) for i in range(2)]
            stg2 = [self.sb(st, "istg%d" % i, [128, 512], BF16) for i in range(2)]
            extra = []

            def g_unit(ch, tb):
                w = wb2[ch % 2]
                if tb == 0:
                    self.load_w(w, W[:, 3072 + ch * 128:3072 + (ch + 1) * 128], 128)
                ps = self.pb[cnt["di"] % 4]; cnt["di"] += 1
                self.fm_chunk(ps, w, 0, xT, xT_b, tb)
                sg = stg2[cnt["si"] % 2]; cnt["si"] += 1
                fw.op(fw.act, lambda: nc.scalar.activation(out=sg[:], in_=ps[:], func=AF.Silu), [ps], [sg])
                fw.dma(fw.sp, self.gT_b[ch][tb], sg, self.gT_s[ch, :, tb * 512:(tb + 1) * 512], sg[:])

            def v_unit(vg, tq):
                w = wb2[vg % 2]
                if tq == 0:
                    self.load_w(w, W[:, 2560 + vg * 128:2560 + (vg + 1) * 128], 128)
                ps = self.pb[cnt["di"] % 4]; cnt["di"] += 1

                def f():
                    for tt in range(4):
                        it_ = tq * 4 + tt
                        for k in range(NT):
                            ins = nc.tensor.matmul(ps[:, tt * 128:(tt + 1) * 128], lhsT=xT[:, k, it_ * 128:(it_ + 1) * 128],
                                                   rhs=w[:, k, 0:128], start=(k == 0), stop=(k == NT - 1))
                    return ins
                fw.op(fw.pe, f, [w] + xT_b[tq * 4:tq * 4 + 4], [ps])
                self.copy(fw.act, v[:, tq * 4:tq * 4 + 4, vg * 128:(vg + 1) * 128],
                          ps[:].rearrange("p (c t) -> p c t", c=4), [ps], v_b[tq * 4:tq * 4 + 4])

            for vg in range(4):
                for tq in range(4):
                    extra.append(lambda vg=vg, tq=tq: v_unit(vg, tq))
            for ch in range(16):
                for tb in range(4):
                    extra.append(lambda ch=ch, tb=tb: g_unit(ch, tb))

            def score_tile(i):
                L = (i + 1) * 128
                dg = diag[i % 2]
                sc = score[i % 2]
                for h in range(16):
                    fw.op(fw.act, lambda h=h: nc.scalar.mul(out=dg[:, h, :], in_=self.identf[:], mul=wi[:, i, h:h + 1]),
                          [self.identf, wi_b[i]], [dg])
                nkb = (L + 511) // 512
                seq = [(kb, h) for kb in range(nkb) for h in range(16)]
                pscs = {}
                inflight = {}

                def issue_dots(q):
                    kb, h = seq[q]
                    n = min(512, L - kb * 512)
                    ks = slice(kb * 512, kb * 512 + n)
                    c, half = h // 2, h % 2
                    pd = self.pb[cnt["di"] % 4]; cnt["di"] += 1
                    p0 = half * 64
                    fw.op(fw.pe, lambda: nc.tensor.matmul(
                        pd[:, 0:n], lhsT=qiT[p0:p0 + 64, c, i * 128:(i + 1) * 128], rhs=kiT[p0:p0 + 64, ks],
                        start=True, stop=True), [qiT_b[c][i // 4], kiT_b[kb]], [pd])
                    r = rl[cnt["ri"] % 4]; cnt["ri"] += 1
                    fw.op(fw.act, lambda: nc.scalar.activation(out=r[:, 0:n], in_=pd[:, 0:n], func=AF.Relu), [pd], [r])
                    inflight[q] = r

                def issue_acc(q):
                    kb, h = seq[q]
                    n = min(512, L - kb * 512)
                    ks = slice(kb * 512, kb * 512 + n)
                    if h == 0:
                        pscs[kb] = self.pb[4 + (cnt["pi"] % 2)]; cnt["pi"] += 1
                    psc = pscs[kb]
                    r = inflight.pop(q)
                    fw.op(fw.pe, lambda: nc.tensor.matmul(
                        psc[:, 0:n], lhsT=dg[:, h, :], rhs=r[:, 0:n], start=(h == 0), stop=(h == 15)),
                        [dg, r], [psc])
                    if h == 15:
                        self.copy(fw.act, sc[:, ks], psc[:, 0:n], [psc], [sc])

                LA = 3
                for q in range(min(LA, len(seq))):
                    issue_dots(q)
                for q in range(len(seq)):
                    if q + LA < len(seq):
                        issue_dots(q + LA)
                    issue_acc(q)
                fw.op(fw.pool, lambda: nc.gpsimd.affine_select(
                    out=sc[:, i * 128:(i + 1) * 128], in_=sc[:, i * 128:(i + 1) * 128], pattern=[[-1, 128]],
                    compare_op=ALU.is_ge, fill=self.reg_neg, base=0, channel_multiplier=1), [sc], [sc])

            def topk_tile(i):
                L = (i + 1) * 128
                sc = score[i % 2]
                br = brow[i % 2]
                if i >= 2:
                    cur = sc
                    for it in range(32):
                        fw.op(fw.dve, lambda: nc.vector.max(out=m8[:], in_=cur[:, 0:L]), [cur], [m8])
                        if it < 31:
                            fw.op(fw.dve, lambda: nc.vector.match_replace(
                                out=work[:, 0:L], in_to_replace=m8[:], in_values=cur[:, 0:L], imm_value=-1e30),
                                [cur, m8], [work])
                            cur = work
                    fw.op(fw.dve, lambda: nc.vector.tensor_scalar(
                        out=br[:, 0:L], in0=sc[:, 0:L], scalar1=m8[:, 7:8], scalar2=None, op0=ALU.is_ge),
                        [sc, m8], [br])
                else:
                    fw.op(fw.dve, lambda: nc.vector.tensor_scalar(
                        out=br[:, 0:L], in0=sc[:, 0:L], scalar1=-1e29, scalar2=None, op0=ALU.is_ge),
                        [sc], [br])
                for j4 in range(0, i + 1, 4):
                    nj = min(4, i + 1 - j4)
                    pt = self.ptb[cnt["ti"] % 2]; cnt["ti"] += 1

                    def f():
                        for jj in range(nj):
                            j = j4 + jj
                            ins = nc.tensor.transpose(pt[:, jj * 128:(jj + 1) * 128], br[:, j * 128:(j + 1) * 128],
                                                      self.ident[:])
                        return ins
                    fw.op(fw.pe, f, [br, self.ident], [pt])
                    b = bst[cnt["bi"] % 2]; cnt["bi"] += 1
                    self.copy(fw.act, b[:, 0:nj, :], pt[:, 0:nj * 128].rearrange("p (c t) -> p c t", c=nj), [pt], [b])
                    fw.dma(fw.sp, [self.biasT_b[j4 + jj][i] for jj in range(nj)], b,
                           self.biasT_s[j4:j4 + nj, :, i * 128:(i + 1) * 128].rearrange("j p t -> p j t"),
                           b[:, 0:nj, :])

            score_tile(0)
            for i in range(NT):
                if i + 1 < NT:
                    score_tile(i + 1)
                ne = (len(extra) + (NT - 1 - i)) // (NT - i)
                for _ in range(ne):
                    extra.pop(0)()
                topk_tile(i)
            assert not extra

    def attn_a(self, kT, kT_b, v, v_b, aoT, aoT_b):
        nc, fw = self.nc, self.fw
        with ExitStack() as st:
            st.enter_context(nc.named_scope(self.tag + "attnA"))
            qblk = self.sb(st, "qblk", [128, H, 512], BF16)
            gblk = self.sb(st, "gblk", [128, H, 512], BF16)
            bblk = self.sb(st, "bblk", [128, NT, 512], BF16)
            qblk_b = [Buf(qblk.t) for _ in range(H)]
            gblk_b = [Buf(gblk.t) for _ in range(H)]
            bblk_b = [Buf(bblk.t) for _ in range(NT)]
            pT = [self.sb(st, "pT%d" % i, [128, 512], BF16) for i in range(8)]
            rden = self.sb(st, "rden", [128, 512], F32)
            o = self.sb(st, "o", [128, 512], F32)
            pi = 0
            for b in range(4):
                qs = slice(b * 512, (b + 1) * 512)
                nj = 4 * b + 4
                for h in range(H):
                    fw.dma(fw.sp, qblk_b[h], self.qT_b[h][b], qblk[:, h, :], self.qT_s[h, :, qs])
                    fw.dma(fw.sp, gblk_b[h], self.gT_b[h][b], gblk[:, h, :], self.gT_s[h, :, qs])
                for j in range(nj):
                    t0 = max(0, j - 4 * b)
                    fw.dma(fw.sp, bblk_b[j], [self.biasT_b[j][i] for i in range(4 * b + t0, 4 * b + 4)],
                           bblk[:, j, t0 * 128:512], self.biasT_s[j, :, b * 512 + t0 * 128:(b + 1) * 512])
                for h in range(H):
                    g = h // 4
                    pout = self.pb[4]
                    pden = self.pb[5]

                    def issue_s(j):
                        q0 = max(0, j - 4 * b) * 128
                        ps = self.pb[j % 4]

                        def f():
                            return nc.tensor.matmul(ps[:, q0:512], lhsT=kT[:, g, j * 128:(j + 1) * 128],
                                                    rhs=qblk[:, h, q0:512], start=True, stop=True)
                        fw.op(fw.pe, f, [kT_b[g][j // 4], qblk_b[h]], [ps])
                        pt = pT[j % 8]
                        fw.op(fw.act, lambda: nc.scalar.activation(out=pt[:, q0:512], in_=ps[:, q0:512], func=AF.Exp,
                                                                   scale=SCALE), [ps], [pt])
                        fw.op(fw.dve, lambda: nc.vector.tensor_tensor(out=pt[:, q0:512], in0=pt[:, q0:512],
                                                                      in1=bblk[:, j, q0:512], op=ALU.mult),
                              [pt, bblk_b[j]], [pt])

                    def issue_av(j):
                        q0 = max(0, j - 4 * b) * 128
                        pt = pT[j % 8]
                        fw.op(fw.pe, lambda: nc.tensor.matmul(pout[:, q0:512], lhsT=v[:, j, g * 128:(g + 1) * 128],
                                                              rhs=pt[:, q0:512], start=(j == 0), stop=(j == nj - 1)),
                              [v_b[j], pt], [pout])
                        fw.op(fw.pe, lambda: nc.tensor.matmul(pden[:, q0:512], lhsT=self.ones_bf[:],
                                                              rhs=pt[:, q0:512], start=(j == 0), stop=(j == nj - 1)),
                              [self.ones_bf, pt], [pden])
                    LA = 3
                    for j in range(min(LA, nj)):
                        issue_s(j)
                    for j in range(nj):
                        if j + LA < nj:
                            issue_s(j + LA)
                        issue_av(j)
                    fw.op(fw.act, lambda: nc.scalar.activation(out=rden[:], in_=pden[:], func=AF.Ln), [pden], [rden])
                    fw.op(fw.act, lambda: nc.scalar.activation(out=rden[:], in_=rden[:], func=AF.Exp, scale=-1.0),
                          [rden], [rden])
                    fw.op(fw.dve, lambda: nc.vector.tensor_tensor(out=o[:], in0=pout[:], in1=rden[:], op=ALU.mult),
                          [pout, rden], [o])
                    fw.op(fw.pool, lambda h=h: nc.gpsimd.tensor_tensor(out=aoT[:, h, qs], in0=o[:], in1=gblk[:, h, :],
                                                                       op=ALU.mult),
                          [o, gblk_b[h]], aoT_b[h][4 * b:4 * b + 4])

    def proj_fm_store(self, W, col0, nchunks, xT, xT_b, wb, stg, dst_s, dst_b, act_func, gi0=0):
        nc, fw = self.nc, self.fw
        gi = gi0
        pi = 0
        si = 0
        for grp in range(nchunks // 2):
            w = wb[gi % 2]; gi += 1
            self.load_w(w, W[:, col0 + grp * 256:col0 + grp * 256 + 256], 256)
            for c in range(2):
                ch = grp * 2 + c
                for tb in range(4):
                    ps = self.pb[pi % 4]; pi += 1
                    self.fm_chunk(ps, w, c, xT, xT_b, tb)
                    sg = stg[si % 4]; si += 1
                    if act_func is None:
                        self.copy(self.evac_eng(), sg[:], ps[:], [ps], [sg])
                    else:
                        fw.op(fw.act, lambda ps=ps, sg=sg: nc.scalar.activation(out=sg[:], in_=ps[:], func=act_func),
                              [ps], [sg])
                    fw.dma(fw.sp, dst_b[ch][tb], sg, dst_s[ch, :, tb * 512:(tb + 1) * 512], sg[:])
        return gi

    def layer_b(self, s, l, src, src_b, aoT, aoT_b):
        nc, fw = self.nc, self.fw
        jl = l - 2
        with ExitStack() as st:
            xT, xT_b = self.phase_T(st, src, src_b)
            wb = [self.sb(st, "wb%d" % i, [128, NT, 256], BF16) for i in range(2)]
            stg = [self.sb(st, "stg%d" % i, [128, 512], BF16) for i in range(4)]
            st.enter_context(nc.named_scope(self.tag + "projB"))
            gi = 0
            if l == 2:
                gi = self.proj_fm_store(self.w_kv_b, 0, 16, xT, xT_b, wb, stg, self.kTb_s, self.kTb_b, None, gi)
                vst = self.sb(st, "vst", [128, NT, 256], BF16)
                pi = 0
                for grp in range(8):
                    w = wb[gi % 2]; gi += 1
                    self.load_w(w, self.w_kv_b[:, 2048 + grp * 256:2048 + grp * 256 + 256], 256)
                    for i in range(NT):
                        ps = self.pb[pi % 4]; pi += 1

                        def f(ps=ps, i=i, w=w):
                            for k in range(NT):
                                ins = nc.tensor.matmul(ps[:, 0:256], lhsT=xT[:, k, i * 128:(i + 1) * 128],
                                                       rhs=w[:, k, 0:256], start=(k == 0), stop=(k == NT - 1))
                            return ins
                        fw.op(fw.pe, f, [w, xT_b[i]], [ps])
                        self.copy(self.evac_eng(), vst[:, i, :], ps[:, 0:256], [ps], [vst])
                    for c in range(2):
                        hh = grp * 2 + c
                        fw.dma(fw.sp, self.vb_b[hh], vst, self.vb_s[hh], vst[:, :, c * 128:(c + 1) * 128])
            W = self.w_q_b[jl]
            gi = self.proj_fm_store(W, 0, 16, xT, xT_b, wb, stg, self.qT_s, self.qT_b, None, gi)
            gi = self.proj_fm_store(W, 2048, 16, xT, xT_b, wb, stg, self.gT_s, self.gT_b, AF.Silu, gi)
            fw.barrier()
        self.attn_b(aoT, aoT_b)
        fw.barrier()

    def attn_b(self, aoT, aoT_b):
        nc, fw = self.nc, self.fw
        with ExitStack() as st:
            st.enter_context(nc.named_scope(self.tag + "attnB"))
            kh = [self.sb(st, "kh%d" % i, [128, S], BF16) for i in range(2)]
            vh = [self.sb(st, "vh%d" % i, [128, NT, 128], BF16) for i in range(2)]
            qh = [self.sb(st, "qh%d" % i, [128, S], BF16) for i in range(2)]
            gh = [self.sb(st, "gh%d" % i, [128, S], BF16) for i in range(2)]
            Eb = [self.sb(st, "Eb%d" % i, [128, S], F32) for i in range(3)]
            Sb = [self.sb(st, "Sb%d" % i, [128, S], F32) for i in range(3)]
            Cb = [self.sb(st, "Cb%d" % i, [128, S + 1], F32) for i in range(2)]
            ntot = [self.sb(st, "ntot%d" % i, [128, 1], F32) for i in range(2)]
            ones = self.sb(st, "ones", [128, S], F32)
            Ab = [self.sb(st, "Ab%d" % i, [128, S], BF16) for i in range(2)]
            AT = [self.sb(st, "AT%d" % i, [128, NT, 128], BF16) for i in range(2)]
            fw.op(fw.dve, lambda: nc.vector.memset(ones[:], 1.0), [], [ones])
            for c in Cb:
                fw.op(fw.dve, lambda c=c: nc.vector.memset(c[:, 0:1], 0.0), [], [c])
            iters = [(h, i) for h in range(H) for i in range(NT)]
            N = len(iters)
            pz = self.pzt
            st8 = {"cc": 0, "ti": 0}

            def loadhead(h):
                k_, v_, q_, g_ = kh[h % 2], vh[h % 2], qh[h % 2], gh[h % 2]
                fw.dma(fw.sp, k_, self.kTb_b[h], k_[:], self.kTb_s[h])
                fw.dma(fw.sp, q_, self.qT_b[h], q_[:], self.qT_s[h])
                fw.dma(fw.sp, v_, self.vb_b[h], v_[:], self.vb_s[h])
                fw.dma(fw.sp, g_, self.gT_b[h], g_[:], self.gT_s[h])

            def stageA(n):
                h, i = iters[n]
                if n == 0:
                    loadhead(0)
                if i == 8 and h + 1 < H:
                    loadhead(h + 1)
                k_, q_ = kh[h % 2], qh[h % 2]
                L = (i + 1) * 128
                e, sp = Eb[n % 3], Sb[n % 3]
                for c0 in range(0, L, 1024):
                    nn = min(1024, L - c0)
                    cc = st8["cc"]; st8["cc"] += 1
                    base = (cc % 2) * 1024
                    zb = self.pb[2 * (cc % 2):2 * (cc % 2) + (nn + 511) // 512]

                    def fz():
                        for s0 in range(0, nn, 512):
                            m = min(512, nn - s0)
                            ins = nc.tensor.matmul(pz[:, base + s0:base + s0 + m], lhsT=q_[:, i * 128:(i + 1) * 128],
                                                   rhs=k_[:, c0 + s0:c0 + s0 + m], start=True, stop=True)
                        return ins
                    fw.op(fw.pe, fz, [q_, k_], zb)
                    fw.op(fw.act, lambda: nc.scalar.activation(out=e[:, c0:c0 + nn], in_=pz[:, base:base + nn],
                                                               func=AF.Exp, scale=SCALE), zb, [e])
                fw.op(fw.act, lambda: nc.scalar.activation(out=sp[:, 0:L], in_=e[:, 0:L], func=AF.Ln,
                                                           bias=self.cst[:, 3:4]), [e, self.cst], [sp])
                d0 = i * 128
                fw.op(fw.pool, lambda: nc.gpsimd.affine_select(
                    out=sp[:, d0:d0 + 128], in_=sp[:, d0:d0 + 128], pattern=[[-1, 128]], compare_op=ALU.is_gt,
                    fill=self.reg_zero, base=0, channel_multiplier=1), [sp], [sp])

            def stageB(n):
                h, i = iters[n]
                L = (i + 1) * 128
                sp, cb, nt = Sb[n % 3], Cb[n % 2], ntot[n % 2]
                fw.op(fw.dve, lambda: nc.vector.tensor_tensor_scan(
                    out=cb[:, 1:L + 1], data0=ones[:, 0:L], data1=sp[:, 0:L], initial=0.0, op0=ALU.mult, op1=ALU.add),
                    [ones, sp], [cb])
                fw.op(fw.dve, lambda: nc.vector.tensor_scalar(out=nt[:], in0=cb[:, L:L + 1], scalar1=-1.0,
                                                              scalar2=None, op0=ALU.mult), [cb], [nt])
                fw.op(fw.act, lambda: nc.scalar.activation(out=sp[:, 0:L], in_=cb[:, 0:L], func=AF.Exp,
                                                           bias=nt[:, 0:1]), [cb, nt], [sp])

            def stageB2(n):
                h, i = iters[n]
                L = (i + 1) * 128
                e, sp, A = Eb[n % 3], Sb[n % 3], Ab[n % 2]
                fw.op(fw.dve, lambda: nc.vector.tensor_tensor(out=A[:, 0:L], in0=e[:, 0:L], in1=sp[:, 0:L], op=ALU.mult),
                      [e, sp], [A])
                d0 = i * 128
                fw.op(fw.pool, lambda: nc.gpsimd.affine_select(
                    out=A[:, d0:d0 + 128], in_=A[:, d0:d0 + 128], pattern=[[-1, 128]], compare_op=ALU.is_gt,
                    fill=self.reg_zero, base=0, channel_multiplier=1), [A], [A])

            def stageC(n):
                h, i = iters[n]
                A, at = Ab[n % 2], AT[n % 2]
                for j4 in range(0, i + 1, 4):
                    nj = min(4, i + 1 - j4)
                    pt = self.ptb[st8["ti"] % 2]; st8["ti"] += 1

                    def f():
                        for jj in range(nj):
                            j = j4 + jj
                            ins = nc.tensor.transpose(pt[:, jj * 128:(jj + 1) * 128], A[:, j * 128:(j + 1) * 128],
                                                      self.ident[:])
                        return ins
                    fw.op(fw.pe, f, [A, self.ident], [pt])
                    self.copy(fw.act if (st8["ti"] % 3 == 0) else fw.dve, at[:, j4:j4 + nj, :],
                              pt[:, 0:nj * 128].rearrange("p (c t) -> p c t", c=nj), [pt], [at])

            def stageC2(n):
                h, i = iters[n]
                v_, g_ = vh[h % 2], gh[h % 2]
                at = AT[n % 2]
                po = self.pb[4 + (n % 2)]

                def fo():
                    for j in range(i + 1):
                        ins = nc.tensor.matmul(po[:, 0:128], lhsT=v_[:, j, :], rhs=at[:, j, :],
                                               start=(j == 0), stop=(j == i))
                    return ins
                fw.op(fw.pe, fo, [v_, at], [po])
                fw.op(fw.dve, lambda: nc.vector.tensor_tensor(out=aoT[:, h, i * 128:(i + 1) * 128], in0=po[:, 0:128],
                                                              in1=g_[:, i * 128:(i + 1) * 128], op=ALU.mult),
                      [po, g_], [aoT_b[h][i]])

            for t in range(N + 4):
                if t < N:
                    stageA(t)
                if 0 <= t - 1 < N:
                    stageB(t - 1)
                if 0 <= t - 2 < N:
                    stageB2(t - 2)
                if 0 <= t - 3 < N:
                    stageC(t - 3)
                if 0 <= t - 4 < N:
                    stageC2(t - 4)

    def epilogue(self, s, l, src, src_b, dst, dst_b, aoT, aoT_b, w_out):
        nc, fw = self.nc, self.fw
        with ExitStack() as st:
            st.enter_context(nc.named_scope(self.tag + "epi"))
            wb = [self.sb(st, "ewb%d" % i, [128, NT, 512], BF16) for i in range(2)]
            wp = [self.sb(st, "ewp%d" % i, [128, 2, 512], BF16) for i in range(2)]
            z = [self.sb(st, "z%d" % i, [128, D], F32) for i in range(4)]
            gB = self.sb(st, "gB", [128, D], F32)
            bB = self.sb(st, "bB", [128, D], F32)
            xlnT = self.sb(st, "xlnT", [128, NT, 512], BF16)
            xlnT_b = [Buf(xlnT.t) for _ in range(4)]
            pT = self.sb(st, "ppT", [128, 2, 512], BF16)
            pT_b = [Buf(pT.t) for _ in range(4)]
            pbf = [self.sb(st, "pbf%d" % i, [128, 256], BF16) for i in range(2)]
            xb = [self.sb(st, "exb%d" % i, [128, D], BF16) for i in range(2)]
            sig = [self.sb(st, "sig%d" % i, [128, 512], F32) for i in range(2)]
            tmp = [self.sb(st, "tmp%d" % i, [128, 512], F32) for i in range(2)]
            stats = self.sb(st, "stats", [128, 4, 4, 6], F32)
            mv = self.sb(st, "mv", [128, 4, 2], F32)
            rstd = self.sb(st, "rstd", [128, 4, 1], F32)
            nbias = self.sb(st, "nbias", [128, 1], F32)
            fw.dma(fw.sp, gB, self.wdram, gB[:], self.ln_g[l:l + 1, :].partition_broadcast(128))
            fw.dma(fw.sp, bB, self.wdram, bB[:], self.ln_b[l:l + 1, :].partition_broadcast(128))
            gi = 0
            pi = 0
            ti = 0
            for b in range(4):
                for t in range(4):
                    i = 4 * b + t
                    fw.dma(fw.sp, z[t], src_b[i], z[t][:], src[i * 128:(i + 1) * 128, :])
                for t in range(4):
                    i = 4 * b + t
                    pb_ = pbf[t % 2]
                    fw.dma(fw.pool, pb_, self.wdram, pb_[:], self.p_d[l, s, i * 128:(i + 1) * 128, :])
                    pt = self.ptb[ti % 2]; ti += 1

                    def f2(pt=pt, pb_=pb_):
                        for c in range(2):
                            ins = nc.tensor.transpose(pt[:, c * 128:(c + 1) * 128], pb_[:, c * 128:(c + 1) * 128],
                                                      self.ident[:])
                        return ins
                    fw.op(fw.pe, f2, [pb_, self.ident], [pt])
                    self.copy(fw.act, pT[:, :, t * 128:(t + 1) * 128],
                              pt[:, 0:256].rearrange("p (c t) -> p c t", c=2), [pt], [pT_b[t]])
                for n in range(4):
                    w = wb[gi % 2]; gi += 1
                    ns_ = slice(n * 512, (n + 1) * 512)
                    self.load_w(w, w_out[:, ns_], 512)
                    for t in range(4):
                        i = 4 * b + t
                        ps = self.pb[pi % 4]; pi += 1

                        def f(ps=ps, w=w, i=i):
                            for k in range(NT):
                                ins = nc.tensor.matmul(ps[:], lhsT=aoT[:, k, i * 128:(i + 1) * 128], rhs=w[:, k, :],
                                                       start=(k == 0), stop=(k == NT - 1))
                            return ins
                        fw.op(fw.pe, f, [w] + [aoT_b[k][i] for k in range(H)], [ps])
                        fw.op(fw.dve, lambda ps=ps, t=t, ns_=ns_: nc.vector.scalar_tensor_tensor(
                            out=z[t][:, ns_], in0=z[t][:, ns_], scalar=ALPHA, in1=ps[:], op0=ALU.mult, op1=ALU.add),
                            [z[t], ps], [z[t]])
                        fw.op(fw.dve, lambda t=t, n=n, ns_=ns_: nc.vector.bn_stats(out=stats[:, t, n, :], in_=z[t][:, ns_]),
                              [z[t]], [stats])
                w3 = wb[gi % 2]; gi += 1
                self.load_w(w3, self.w_gate[l][:, 0:512], 512)
                self.load_w(wp[0], self.w_ple[l][:, 0:512], 512)
                for t in range(4):
                    fw.op(fw.dve, lambda t=t: nc.vector.bn_aggr(out=mv[:, t, :], in_=stats[:, t].rearrange("p a b -> p (a b)")),
                          [stats], [mv])
                fw.op(fw.dve, lambda: nc.vector.tensor_scalar(out=rstd[:], in0=mv[:, :, 1:2], scalar1=1e-5, scalar2=None,
                                                              op0=ALU.add), [mv], [rstd])
                fw.op(fw.act, lambda: nc.scalar.sqrt(out=rstd[:], in_=rstd[:]), [rstd], [rstd])
                fw.op(fw.dve, lambda: nc.vector.reciprocal(out=rstd[:], in_=rstd[:]), [rstd], [rstd])
                for t in range(4):
                    i = 4 * b + t
                    zt = z[t]
                    fw.op(fw.dve, lambda zt=zt, t=t: nc.vector.scalar_tensor_tensor(
                        out=zt[:], in0=zt[:], scalar=mv[:, t, 0:1], in1=gB[:], op0=ALU.subtract, op1=ALU.mult),
                        [zt, mv, gB], [zt])
                    fw.op(fw.dve, lambda zt=zt, t=t: nc.vector.scalar_tensor_tensor(
                        out=zt[:], in0=zt[:], scalar=rstd[:, t, 0:1], in1=bB[:], op0=ALU.mult, op1=ALU.add),
                        [zt, rstd, bB], [zt])
                    x_ = xb[t % 2]
                    self.copy(fw.act, x_[:], zt[:], [zt], [x_])
                    for c4 in range(4):
                        pt = self.ptb[ti % 2]; ti += 1

                        def f(pt=pt, x_=x_, c4=c4):
                            for c in range(4):
                                k = c4 * 4 + c
                                ins = nc.tensor.transpose(pt[:, c * 128:(c + 1) * 128], x_[:, k * 128:(k + 1) * 128],
                                                          self.ident[:])
                            return ins
                        fw.op(fw.pe, f, [x_, self.ident], [pt])
                        self.copy(self.evac_eng(), xlnT[:, c4 * 4:(c4 + 1) * 4, t * 128:(t + 1) * 128],
                                  pt[:].rearrange("p (c t) -> p c t", c=4), [pt], [xlnT_b[t]])
                for n in range(4):
                    ns_ = slice(n * 512, (n + 1) * 512)
                    w2 = wp[n % 2]
                    if n == 0:
                        w = w3
                    else:
                        w = wb[gi % 2]; gi += 1
                        self.load_w(w, self.w_gate[l][:, ns_], 512)
                        self.load_w(w2, self.w_ple[l][:, ns_], 512)
                    for t in range(4):
                        zt = z[t]
                        ps = self.pb[pi % 4]; pi += 1
                        ps2 = self.pb[4 + (pi % 2)]

                        def f(ps=ps, w=w, t=t):
                            for k in range(NT):
                                ins = nc.tensor.matmul(ps[:], lhsT=xlnT[:, k, t * 128:(t + 1) * 128], rhs=w[:, k, :],
                                                       start=(k == 0), stop=(k == NT - 1))
                            return ins
                        fw.op(fw.pe, f, [w, xlnT_b[t]], [ps])

                        def f3(ps2=ps2, w2=w2, t=t):
                            for k in range(2):
                                ins = nc.tensor.matmul(ps2[:], lhsT=pT[:, k, t * 128:(t + 1) * 128], rhs=w2[:, k, :],
                                                       start=(k == 0), stop=(k == 1))
                            return ins
                        fw.op(fw.pe, f3, [w2, pT_b[t]], [ps2])
                        sg = sig[(n * 4 + t) % 2]
                        tm = tmp[(n * 4 + t) % 2]
                        fw.op(fw.act, lambda ps=ps, sg=sg: nc.scalar.activation(out=sg[:], in_=ps[:], func=AF.Sigmoid),
                              [ps], [sg])
                        fw.op(fw.dve, lambda ps2=ps2, sg=sg, tm=tm: nc.vector.tensor_tensor(out=tm[:], in0=ps2[:], in1=sg[:],
                                                                                            op=ALU.mult), [ps2, sg], [tm])
                        fw.op(fw.dve, lambda zt=zt, tm=tm, ns_=ns_: nc.vector.tensor_tensor(out=zt[:, ns_], in0=zt[:, ns_],
                                                                                            in1=tm[:], op=ALU.add),
                              [zt, tm], [zt])
                for t in range(4):
                    i = 4 * b + t
                    fw.dma(fw.sp, dst_b[i], z[t], dst[i * 128:(i + 1) * 128, :], z[t][:])


_CACHE = {}


def _inv_freq_table():
    p = np.arange(128)
    c0 = (np.float32(10000.0) ** (-(p % 64).astype(np.float32) / np.float32(64))).astype(np.float32)
    c1 = (np.float32(10000.0) ** (-(p % 32).astype(np.float32) / np.float32(32))).astype(np.float32)
    return np.ascontiguousarray(np.stack([c0, c1], axis=1).astype(np.float32))


def kernel(x, p, positions, w_in_a, w_out_a, w_q_b, w_kv_b, w_out_b, ln_g, ln_b, w_ple, w_ple_gate):
    n = 8
    ns = 2
    if "nc" not in _CACHE:
        _CACHE["nc"] = Prog(nseq=ns, nlayers=4).build()
    nc = _CACHE["nc"]
    f32 = lambda a: np.ascontiguousarray(np.asarray(a), dtype=np.float32)
    x = f32(x); p = f32(p)
    pos = np.ascontiguousarray(np.asarray(positions), dtype=np.int32)
    shared = {"w_in_a": f32(w_in_a), "w_out_a": f32(w_out_a), "w_q_b": f32(w_q_b), "w_kv_b": f32(w_kv_b),
              "w_out_b": f32(w_out_b), "ln_g": f32(ln_g), "ln_b": f32(ln_b), "w_ple": f32(w_ple),
              "w_ple_gate": f32(w_ple_gate), "invf": _inv_freq_table()}
    in_maps = []
    for c in range(n):
        m = dict(shared)
        m["x"] = np.ascontiguousarray(x[c * ns:(c + 1) * ns])
        m["p"] = np.ascontiguousarray(p[:, c * ns:(c + 1) * ns])
        m["pos"] = np.ascontiguousarray(pos[c * ns:(c + 1) * ns])
        in_maps.append(m)
    res = run_bass_kernel_spmd(nc, in_maps, core_ids=list(range(n)))
    return np.concatenate([np.asarray(r["out"], dtype=np.float32) for r in res.results], axis=0)
```

```python
import math
from contextlib import ExitStack

import numpy as np
import concourse.bass as bass
import concourse.mybir as mybir
from concourse.bass_utils import run_bass_kernel_spmd

F32 = mybir.dt.float32
BF16 = mybir.dt.bfloat16
I32 = mybir.dt.int32
AF = mybir.ActivationFunctionType
ALU = mybir.AluOpType

S = 2048
D = 2048
NT = 16
H = 16
DH = 128
A_IN = 6224
ALPHA = float((2 * 4) ** 0.25)
SCALE = float(DH ** -0.5)
NEG = -30000.0
PI = math.pi


class Buf:
    __slots__ = ("t", "w", "r", "name")

    def __init__(self, t, name=""):
        self.t = t
        self.w = None
        self.r = {}
        self.name = name

    def __getitem__(self, idx):
        return self.t[idx]


class Eng:
    def __init__(self, fw, name, eng, sem, self_sync=True):
        self.fw = fw
        self.name = name
        self.eng = eng
        self.sem = sem
        self.count = 0
        self.seen = {}
        self.self_sync = self_sync

    def wait(self, tok):
        if tok is None:
            return
        key, val = tok
        if val <= 0:
            return
        if key == self.name and not self.self_sync:
            return
        if self.seen.get(key, 0) >= val:
            return
        self.seen[key] = val
        self.eng.wait_ge(self.fw.sems[key], val)
        self.fw.nwaits += 1


class FW:
    def __init__(self, nc, ndma=32):
        self.nc = nc
        self.sems = {}
        self.nwaits = 0
        self.nops = 0
        self.engs = {}
        self.dma_keys = []
        self.dma_val = {}
        self.dma_rr = 0
        self.ndma = ndma

    def setup(self, stack):
        nc = self.nc
        for name, eng, ss in (("pe", nc.tensor, False), ("act", nc.scalar, True),
                              ("dve", nc.vector, True), ("pool", nc.gpsimd, True),
                              ("sp", nc.sync, False)):
            sem = stack.enter_context(nc.semaphore("s_" + name))
            self.sems[name] = sem
            self.engs[name] = Eng(self, name, eng, sem, ss)
        self.rings = {"sp": [], "pool": []}
        self.ring_rr = {"sp": 0, "pool": 0}
        for qn, pre in (("sp", "d"), ("pool", "g")):
            for i in range(self.ndma // 2):
                k = "%s%d" % (pre, i)
                self.sems[k] = stack.enter_context(nc.semaphore("s_" + k))
                self.dma_keys.append(k)
                self.rings[qn].append(k)
                self.dma_val[k] = 0
        self.pe = self.engs["pe"]
        self.act = self.engs["act"]
        self.dve = self.engs["dve"]
        self.pool = self.engs["pool"]
        self.sp = self.engs["sp"]

    def _deps(self, E, reads, writes):
        for t in reads:
            E.wait(t.w)
        for t in writes:
            E.wait(t.w)
            for k, v in t.r.items():
                E.wait((k, v))

    def op(self, E, fn, reads, writes):
        self._deps(E, reads, writes)
        ins = fn()
        self.nops += 1
        E.count += 1
        ins.then_inc(E.sem, 1)
        tok = (E.name, E.count)
        for t in reads:
            if t.r.get(E.name, 0) < E.count:
                t.r[E.name] = E.count
        for t in writes:
            t.w = tok
            t.r = {}
        return tok

    def dma(self, Q, out_b, in_b, out_ap, in_ap, **kw):
        outs = out_b if isinstance(out_b, (list, tuple)) else [out_b]
        ins_ = in_b if isinstance(in_b, (list, tuple)) else [in_b]
        self._deps(Q, ins_, outs)
        ring = self.rings[Q.name]
        k = ring[self.ring_rr[Q.name]]
        self.ring_rr[Q.name] = (self.ring_rr[Q.name] + 1) % len(ring)
        Q.wait((k, self.dma_val[k]))
        ins = Q.eng.dma_start(out=out_ap, in_=in_ap, **kw)
        self.dma_val[k] += 16
        ins.then_inc(self.sems[k], 16)
        tok = (k, self.dma_val[k])
        for b in ins_:
            b.r[k] = self.dma_val[k]
        for b in outs:
            b.w = tok
            b.r = {}
        self.nops += 1
        return tok

    def barrier(self):
        for E in self.engs.values():
            for E2 in self.engs.values():
                if E2 is not E:
                    E.wait((E2.name, E2.count))
            for k in self.dma_keys:
                E.wait((k, self.dma_val[k]))

    def finish(self, bufs):
        for b in bufs:
            self.sp.wait(b.w)


class StopBuild(Exception):
    pass


class Prog:
    def stage(self, name):
        if self.stop == name:
            self.stopped = True
        return self.stopped

    def __init__(self, nseq=2, nlayers=4, debug=False, stop=None):
        self.stop = stop
        self.stopped = False
        self.nseq = nseq
        self.nlayers = nlayers
        self.debug = debug
        self.nc = bass.Bass("TRN2", target_bir_lowering=False)
        self.rr = 0

    def sb(self, st, name, shape, dt):
        self.uid += 1
        return Buf(st.enter_context(self.nc.sbuf_tensor("%s_%d" % (name, self.uid), shape, dt)), name)

    def din(self, name, shape, dt):
        return self.nc.dram_tensor(name, shape, dt, kind="ExternalInput").ap()

    def dscr(self, name, shape, dt):
        kind = "ExternalOutput" if self.debug else "Internal"
        return self.nc.dram_tensor(name, shape, dt, kind=kind).ap()

    def evac_eng(self):
        self.rr += 1
        return self.fw.act if (self.rr & 1) else self.fw.dve

    def copy(self, E, out_ap, in_ap, reads, writes):
        nc = self.nc
        if E is self.fw.act:
            return self.fw.op(E, lambda: nc.scalar.copy(out=out_ap, in_=in_ap), reads, writes)
        elif E is self.fw.dve:
            return self.fw.op(E, lambda: nc.vector.tensor_copy(out=out_ap, in_=in_ap), reads, writes)
        else:
            return self.fw.op(E, lambda: nc.gpsimd.tensor_copy(out=out_ap, in_=in_ap), reads, writes)

    def load_w(self, wb, src2d, ncols):
        kc = src2d.shape[0] // 128
        self.fw.dma(self.fw.pool, wb, self.wdram, wb[:, 0:kc, 0:ncols],
                    src2d.rearrange("(kc p) n -> p kc n", p=128))

    def build(self):
        nc = self.nc
        ns = self.nseq
        self.uid = 0
        self.x_d = self.din("x", [ns, S, D], F32)
        self.p_d = self.din("p", [4, ns, S, 256], F32)
        self.pos_d = self.din("pos", [ns, S], I32)
        self.w_in_a = self.din("w_in_a", [2, D, A_IN], F32)
        self.w_out_a = self.din("w_out_a", [2, D, D], F32)
        self.w_q_b = self.din("w_q_b", [2, D, 2 * D], F32)
        self.w_kv_b = self.din("w_kv_b", [D, 2 * D], F32)
        self.w_out_b = self.din("w_out_b", [2, D, D], F32)
        self.ln_g = self.din("ln_g", [4, D], F32)
        self.ln_b = self.din("ln_b", [4, D], F32)
        self.w_ple = self.din("w_ple", [4, 256, D], F32)
        self.w_gate = self.din("w_ple_gate", [4, D, D], F32)
        self.invf_d = self.din("invf", [128, 2], F32)
        self.out_d = nc.dram_tensor("out", [ns, S, D], F32, kind="ExternalOutput").ap()
        self.xres = [self.dscr("xres%d" % i, [S, D], F32) for i in range(2)]
        self.qT_s = self.dscr("qT_s", [H, 128, S], BF16)
        self.gT_s = self.dscr("gT_s", [H, 128, S], BF16)
        self.biasT_s = self.dscr("biasT_s", [NT, 128, S], BF16)
        self.kTb_s = self.dscr("kTb_s", [H, 128, S], BF16)
        self.vb_s = self.dscr("vb_s", [H, 128, NT, 128], BF16)
        self.wdram = Buf(None, "weights")
        self.xin_b = Buf(None, "xin")
        self.xres_b = [[Buf(None) for _ in range(NT)] for _ in range(2)]
        self.out_b = [[Buf(None) for _ in range(NT)] for _ in range(ns)]
        self.qT_b = [[Buf(None) for _ in range(4)] for _ in range(H)]
        self.gT_b = [[Buf(None) for _ in range(4)] for _ in range(H)]
        self.biasT_b = [[Buf(None) for _ in range(NT)] for _ in range(NT)]
        self.kTb_b = [[Buf(None) for _ in range(4)] for _ in range(H)]
        self.vb_b = [Buf(None) for _ in range(H)]

        with ExitStack() as st:
            self.fw = fw = FW(nc)
            fw.setup(st)
            pzt = st.enter_context(nc.psum_tensor("pz", [128, 2048], F32))
            self.pzt = pzt
            self.pb = [Buf(pzt[:, i * 512:(i + 1) * 512], "pz%d" % i) for i in range(4)]
            for i in range(2):
                t = st.enter_context(nc.psum_tensor("pb%d" % (4 + i), [128, 512], F32))
                self.pb.append(Buf(t, "pb%d" % (4 + i)))
            self.ptb = [Buf(st.enter_context(nc.psum_tensor("ptb%d" % i, [128, 512], BF16)), "ptb%d" % i)
                        for i in range(2)]
            self.identf = self.sb(st, "identf", [128, 128], F32)
            self.ident = self.sb(st, "ident", [128, 128], BF16)
            self.ones_bf = self.sb(st, "ones_bf", [128, 128], BF16)
            self.invf = self.sb(st, "invf", [128, 2], F32)
            self.cst = self.sb(st, "cst", [128, 4], F32)
            self.big = self.sb(st, "big", [128, NT, S], BF16)
            idf = self.identf
            self.reg_neg = nc.gpsimd.to_reg(-1e30)
            self.reg_zero = nc.gpsimd.to_reg(0.0)
            fw.op(fw.pool, lambda: nc.gpsimd.memset(idf[:], 1.0), [], [idf])
            fw.op(fw.pool, lambda: nc.gpsimd.affine_select(out=idf[:], in_=idf[:], pattern=[[-1, 128]],
                                                           compare_op=ALU.is_equal, fill=0.0, base=0,
                                                           channel_multiplier=1), [idf], [idf])
            self.copy(fw.dve, self.ident[:], idf[:], [idf], [self.ident])
            fw.op(fw.dve, lambda: nc.vector.memset(self.ones_bf[:], 1.0), [], [self.ones_bf])
            fw.op(fw.dve, lambda: nc.vector.memset(self.cst[:, 0:1], PI), [], [self.cst])
            fw.op(fw.dve, lambda: nc.vector.memset(self.cst[:, 1:2], -PI), [], [self.cst])
            fw.op(fw.dve, lambda: nc.vector.memset(self.cst[:, 2:3], 0.0), [], [self.cst])
            fw.op(fw.dve, lambda: nc.vector.memset(self.cst[:, 3:4], 1.0), [], [self.cst])
            fw.dma(fw.sp, self.invf, self.wdram, self.invf[:], self.invf_d[:, :])

            for s in range(ns):
                if not self.stopped:
                    self.run_seq(s)
            outs = [b for row in self.out_b for b in row]
            if self.nlayers < 4:
                outs += [b for row in self.xres_b for b in row]
            fw.barrier()
            fw.finish(outs)
        return nc

    def run_seq(self, s):
        fw = self.fw
        for l in range(self.nlayers):
            self.tag = "s%dL%d_" % (s, l)
            if self.stopped:
                return
            if l == 0:
                src = self.x_d[s]
                src_b = [self.xin_b] * NT
            else:
                src = self.xres[(l - 1) % 2]
                src_b = self.xres_b[(l - 1) % 2]
            if l == 3:
                dst = self.out_d[s]
                dst_b = self.out_b[s]
            else:
                dst = self.xres[l % 2]
                dst_b = self.xres_b[l % 2]
            if True:
                aoT = self.big
                aoT_b = [[Buf(aoT.t) for _ in range(NT)] for _ in range(H)]
                if l < 2:
                    self.layer_a(s, l, src, src_b, aoT, aoT_b)
                else:
                    self.layer_b(s, l, src, src_b, aoT, aoT_b)
                fw.barrier()
                if self.stopped:
                    return
                w_out = self.w_out_a[l] if l < 2 else self.w_out_b[l - 2]
                self.epilogue(s, l, src, src_b, dst, dst_b, aoT, aoT_b, w_out)
                fw.barrier()

    def phase_T(self, st, src, src_b):
        nc, fw = self.nc, self.fw
        xT = self.big
        xT_b = [Buf(xT.t) for _ in range(NT)]
        with ExitStack() as st2:
            st2.enter_context(self.nc.named_scope(self.tag + "T"))
            xb = [self.sb(st2, "T_xb%d" % i, [128, D], BF16) for i in range(2)]
            for i in range(NT):
                b = xb[i % 2]
                fw.dma(fw.pool, b, src_b[i], b[:], src[i * 128:(i + 1) * 128, :])
                for c4 in range(4):
                    pt = self.ptb[(i * 4 + c4) % 2]

                    def f(pt=pt, b=b, c4=c4):
                        for c in range(4):
                            k = c4 * 4 + c
                            ins = nc.tensor.transpose(pt[:, c * 128:(c + 1) * 128], b[:, k * 128:(k + 1) * 128],
                                                      self.ident[:])
                        return ins
                    fw.op(fw.pe, f, [b, self.ident], [pt])
                    self.copy(self.evac_eng(), xT[:, c4 * 4:(c4 + 1) * 4, i * 128:(i + 1) * 128],
                              pt[:].rearrange("p (c t) -> p c t", c=4), [pt], [xT_b[i]])
            fw.barrier()
        self.stage("T")
        return xT, xT_b

    def rope_tables(self, st, s, col, period):
        nc, fw = self.nc, self.fw
        cosT = self.sb(st, "cosT", [128, S], F32)
        sinT = self.sb(st, "sinT", [128, S], F32)
        with ExitStack() as st2:
            posi = self.sb(st2, "posi", [128, S], I32)
            ang = self.sb(st2, "ang", [128, S], F32)
            fw.dma(fw.sp, posi, self.wdram, posi[:], self.pos_d[s:s + 1, :].partition_broadcast(128))
            self.copy(fw.dve, ang[:], posi[:], [posi], [ang])
            fw.op(fw.dve, lambda: nc.vector.tensor_scalar(out=ang[:], in0=ang[:], scalar1=self.invf[:, col:col + 1],
                                                          scalar2=None, op0=ALU.mult), [ang, self.invf], [ang])
            tmp = posi
            tmpf = tmp[:].bitcast(F32)
            C1 = 6.28125
            C2 = 2 * PI - 6.28125
            TS = nc.vector.tensor_scalar
            STT = nc.vector.scalar_tensor_tensor
            fw.op(fw.dve, lambda: TS(out=posi[:], in0=ang[:], scalar1=1.0 / (2 * PI), scalar2=None, op0=ALU.mult),
                  [ang], [posi])
            self.copy(fw.dve, cosT[:], posi[:], [posi], [cosT])
            fw.op(fw.dve, lambda: STT(out=ang[:], in0=cosT[:], scalar=-C1, in1=ang[:], op0=ALU.mult, op1=ALU.add),
                  [cosT, ang], [ang])
            fw.op(fw.dve, lambda: STT(out=ang[:], in0=cosT[:], scalar=-C2, in1=ang[:], op0=ALU.mult, op1=ALU.add),
                  [cosT, ang], [ang])
            fw.op(fw.dve, lambda: TS(out=sinT[:], in0=ang[:], scalar1=PI, scalar2=-2 * PI, op0=ALU.is_gt, op1=ALU.mult),
                  [ang], [sinT])
            fw.op(fw.dve, lambda: nc.vector.tensor_tensor(out=ang[:], in0=ang[:], in1=sinT[:], op=ALU.add),
                  [ang, sinT], [ang])
            fw.op(fw.dve, lambda: TS(out=sinT[:], in0=ang[:], scalar1=-PI, scalar2=2 * PI, op0=ALU.is_lt, op1=ALU.mult),
                  [ang], [sinT])
            fw.op(fw.dve, lambda: nc.vector.tensor_tensor(out=ang[:], in0=ang[:], in1=sinT[:], op=ALU.add),
                  [ang, sinT], [ang])
            fw.op(fw.dve, lambda: TS(out=ang[:], in0=ang[:], scalar1=-PI, scalar2=PI, op0=ALU.max, op1=ALU.min),
                  [ang], [ang])
            fw.op(fw.dve, lambda: TS(out=tmpf, in0=ang[:], scalar1=0.5 * PI, scalar2=None, op0=ALU.add), [ang], [tmp])
            fw.op(fw.dve, lambda: TS(out=cosT[:], in0=tmpf, scalar1=PI, scalar2=-2 * PI, op0=ALU.is_gt, op1=ALU.mult),
                  [tmp], [cosT])
            fw.op(fw.dve, lambda: nc.vector.tensor_tensor(out=tmpf, in0=tmpf, in1=cosT[:], op=ALU.add), [tmp, cosT], [tmp])
            fw.op(fw.dve, lambda: TS(out=tmpf, in0=tmpf, scalar1=-PI, scalar2=PI, op0=ALU.max, op1=ALU.min), [tmp], [tmp])
            half = period // 2
            for g0 in range(0, 128, period):
                fw.op(fw.act, lambda g0=g0: nc.scalar.activation(out=sinT[g0:g0 + half, :], in_=ang[g0:g0 + half, :],
                                                                 func=AF.Sin, scale=1.0), [ang], [sinT])
                fw.op(fw.act, lambda g0=g0: nc.scalar.activation(out=sinT[g0 + half:g0 + period, :],
                                                                 in_=ang[g0 + half:g0 + period, :],
                                                                 func=AF.Sin, scale=-1.0), [ang], [sinT])
            fw.op(fw.act, lambda: nc.scalar.activation(out=cosT[:], in_=tmpf, func=AF.Sin, scale=1.0), [tmp], [cosT])
        return cosT, sinT

    def rope_post(self, ps, raw, t1, t2, cosT, sinT, tb, period, out_ap, out_bufs):
        nc, fw = self.nc, self.fw
        cs = slice(tb * 512, (tb + 1) * 512)
        self.copy(fw.act, raw[:], ps[:], [ps], [raw])
        fw.op(fw.dve, lambda: nc.vector.tensor_tensor(out=t1[:], in0=raw[:], in1=cosT[:, cs], op=ALU.mult),
              [raw, cosT], [t1])
        half = period // 2
        for g0 in range(0, 128, period):
            a0, a1, a2 = g0, g0 + half, g0 + period
            fw.op(fw.dve, lambda a0=a0, a1=a1, a2=a2: nc.vector.tensor_tensor(
                out=t2[a0:a1, :], in0=raw[a1:a2, :], in1=sinT[a1:a2, cs], op=ALU.mult), [raw, sinT], [t2])
            fw.op(fw.dve, lambda a0=a0, a1=a1, a2=a2: nc.vector.tensor_tensor(
                out=t2[a1:a2, :], in0=raw[a0:a1, :], in1=sinT[a0:a1, cs], op=ALU.mult), [raw, sinT], [t2])
        fw.op(fw.dve, lambda: nc.vector.tensor_tensor(out=out_ap, in0=t1[:], in1=t2[:], op=ALU.add),
              [t1, t2], out_bufs)

    def fm_chunk(self, ps, wb, c, xT, xT_b, tb):
        nc, fw = self.nc, self.fw

        def f():
            for k in range(NT):
                ins = nc.tensor.matmul(ps[:], lhsT=wb[:, k, c * 128:(c + 1) * 128],
                                       rhs=xT[:, k, tb * 512:(tb + 1) * 512], start=(k == 0), stop=(k == NT - 1))
            return ins
        fw.op(fw.pe, f, [wb] + xT_b[tb * 4:(tb + 1) * 4], [ps])

    def layer_a(self, s, l, src, src_b, aoT, aoT_b):
        nc, fw = self.nc, self.fw
        W = self.w_in_a[l]
        with ExitStack() as st_kv:
            kT = self.sb(st_kv, "kT", [128, 4, S], BF16)
            kT_b = [[Buf(kT.t) for _ in range(4)] for _ in range(4)]
            v = self.sb(st_kv, "v", [128, NT, 512], BF16)
            v_b = [Buf(v.t) for _ in range(NT)]
            with ExitStack() as st_idx:
                qiT = self.sb(st_idx, "qiT", [128, 8, S], BF16)
                qiT_b = [[Buf(qiT.t) for _ in range(4)] for _ in range(8)]
                kiT = self.sb(st_idx, "kiT", [128, S], BF16)
                kiT_b = [Buf(kiT.t) for _ in range(4)]
                wi = self.sb(st_idx, "wi", [128, NT, 16], F32)
                wi_b = [Buf(wi.t) for _ in range(NT)]
                with ExitStack() as st:
                    xT, xT_b = self.phase_T(st, src, src_b)
                    if self.stopped:
                        return
                    wb = [self.sb(st, "wb%d" % i, [128, NT, 256], BF16) for i in range(2)]
                    raw = [self.sb(st, "raw%d" % i, [128, 512], F32) for i in range(2)]
                    t1 = [self.sb(st, "t1%d" % i, [128, 512], F32) for i in range(2)]
                    t2 = [self.sb(st, "t2%d" % i, [128, 512], F32) for i in range(2)]
                    stg = [self.sb(st, "stg%d" % i, [128, 512], BF16) for i in range(4)]
                    st.enter_context(nc.named_scope(self.tag + "projA"))
                    gi = 0
                    pi = 0
                    si = 0
                    ri = 0
                    with ExitStack() as st_r:
                        cosT, sinT = self.rope_tables(st_r, s, 0, 128)
                        for grp in range(10):
                            col0 = grp * 256
                            w = wb[gi % 2]; gi += 1
                            self.load_w(w, W[:, col0:col0 + 256], 256)
                            for c in range(2):
                                ch = grp * 2 + c
                                for tb in range(4):
                                    ps = self.pb[pi % 4]; pi += 1
                                    self.fm_chunk(ps, w, c, xT, xT_b, tb)
                                    r = ri % 2; ri += 1
                                    if ch < 16:
                                        sg = stg[si % 4]; si += 1
                                        self.rope_post(ps, raw[r], t1[r], t2[r], cosT, sinT, tb, 128, sg[:], [sg])
                                        fw.dma(fw.sp, self.qT_b[ch][tb], sg, self.qT_s[ch, :, tb * 512:(tb + 1) * 512], sg[:])
                                    else:
                                        g = ch - 16
                                        self.rope_post(ps, raw[r], t1[r], t2[r], cosT, sinT, tb, 128,
                                                       kT[:, g, tb * 512:(tb + 1) * 512], [kT_b[g][tb]])
                        fw.barrier()
                        if self.stage("qk"):
                            return
                    with ExitStack() as st_r:
                        cosT, sinT = self.rope_tables(st_r, s, 1, 64)
                        for grp in range(4):
                            col0 = 5120 + grp * 256
                            w = wb[gi % 2]; gi += 1
                            self.load_w(w, W[:, col0:col0 + 256], 256)
                            for c in range(2):
                                ch = grp * 2 + c
                                for tb in range(4):
                                    ps = self.pb[pi % 4]; pi += 1
                                    self.fm_chunk(ps, w, c, xT, xT_b, tb)
                                    r = ri % 2; ri += 1
                                    self.rope_post(ps, raw[r], t1[r], t2[r], cosT, sinT, tb, 64,
                                                   qiT[:, ch, tb * 512:(tb + 1) * 512], [qiT_b[ch][tb]])
                        w = wb[gi % 2]; gi += 1
                        Wk = W[:, 6160:6224].rearrange("(kc p) n -> p kc n", p=128)
                        fw.dma(fw.pool, w, self.wdram, w[:, :, 0:64], Wk)
                        fw.dma(fw.pool, w, self.wdram, w[:, :, 64:128], Wk)
                        fw.dma(fw.pool, w, self.wdram, w[:, :, 128:144],
                               W[:, 6144:6160].rearrange("(kc p) n -> p kc n", p=128))
                        for tb in range(4):
                            ps = self.pb[pi % 4]; pi += 1
                            self.fm_chunk(ps, w, 0, xT, xT_b, tb)
                            r = ri % 2; ri += 1
                            self.rope_post(ps, raw[r], t1[r], t2[r], cosT, sinT, tb, 64,
                                           kiT[:, tb * 512:(tb + 1) * 512], [kiT_b[tb]])
                        for i in range(NT):
                            ps = self.pb[pi % 4]; pi += 1

                            def f(ps=ps, i=i, w=w):
                                for k in range(NT):
                                    ins = nc.tensor.matmul(ps[:, 0:16], lhsT=xT[:, k, i * 128:(i + 1) * 128],
                                                           rhs=w[:, k, 128:144], start=(k == 0), stop=(k == NT - 1))
                                return ins
                            fw.op(fw.pe, f, [w, xT_b[i]], [ps])
                            fw.op(fw.act, lambda ps=ps, i=i: nc.scalar.mul(out=wi[:, i, :], in_=ps[:, 0:16], mul=1.0 / 32.0),
                                  [ps], [wi_b[i]])
                        fw.barrier()
                    self.xT_keep = (xT, xT_b)
                    fw.barrier()
                if self.stage("proj"):
                    return
                self.indexer(qiT, qiT_b, kiT, kiT_b, wi, wi_b, W, v, v_b)
                fw.barrier()
                if self.stage("idx"):
                    return
            self.attn_a(kT, kT_b, v, v_b, aoT, aoT_b)
            fw.barrier()
            if self.stage("attn"):
                return

    def indexer(self, qiT, qiT_b, kiT, kiT_b, wi, wi_b, W, v, v_b):
        nc, fw = self.nc, self.fw
        with ExitStack() as st:
            st.enter_context(nc.named_scope(self.tag + "idx"))
            score = [self.sb(st, "score%d" % i, [128, S], F32) for i in range(2)]
            work = self.sb(st, "work", [128, S], F32)
            m8 = self.sb(st, "m8", [128, 8], F32)
            brow = [self.sb(st, "brow%d" % i, [128, S], BF16) for i in range(2)]
            rl = [self.sb(st, "rl%d" % i, [128, 512], BF16) for i in range(4)]
            diag = [self.sb(st, "diag%d" % i, [128, 16, 128], BF16) for i in range(2)]
            bst = [self.sb(st, "bst%d" % i, [128, 4, 128], BF16) for i in range(2)]
            cnt = {"ri": 0, "pi": 0, "bi": 0, "ti": 0, "di": 0, "si": 0}
            xT, xT_b = self.xT_keep
            wb2 = [self.sb(st, "iwb%d" % i, [128, NT, 128], BF16) for i in range(2)]
            stg2 = [self.sb(st, "istg%d" % i, [128, 512], BF16) for i in range(2)]
            extra = []

            def g_unit(ch, tb):
                w = wb2[ch % 2]
                if tb == 0:
                    self.load_w(w, W[:, 3072 + ch * 128:3072 + (ch + 1) * 128], 128)
                ps = self.pb[cnt["di"] % 4]; cnt["di"] += 1
                self.fm_chunk(ps, w, 0, xT, xT_b, tb)
                sg = stg2[cnt["si"] % 2]; cnt["si"] += 1
                fw.op(fw.act, lambda: nc.scalar.activation(out=sg[:], in_=ps[:], func=AF.Silu), [ps], [sg])
                fw.dma(fw.sp, self.gT_b[ch][tb], sg, self.gT_s[ch, :, tb * 512:(tb + 1) * 512], sg[:])

            def v_unit(vg, tq):
                w = wb2[vg % 2]
                if tq == 0:
                    self.load_w(w, W[:, 2560 + vg * 128:2560 + (vg + 1) * 128], 128)
                ps = self.pb[cnt["di"] % 4]; cnt["di"] += 1

                def f():
                    for tt in range(4):
                        it_ = tq * 4 + tt
                        for k in range(NT):
                            ins = nc.tensor.matmul(ps[:, tt * 128:(tt + 1) * 128], lhsT=xT[:, k, it_ * 128:(it_ + 1) * 128],
                                                   rhs=w[:, k, 0:128], start=(k == 0), stop=(k == NT - 1))
                    return ins
                fw.op(fw.pe, f, [w] + xT_b[tq * 4:tq * 4 + 4], [ps])
                self.copy(fw.act, v[:, tq * 4:tq * 4 + 4, vg * 128:(vg + 1) * 128],
                          ps[:].rearrange("p (c t) -> p c t", c=4), [ps], v_b[tq * 4:tq * 4 + 4])

            for vg in range(4):
                for tq in range(4):
                    extra.append(lambda vg=vg, tq=tq: v_unit(vg, tq))
            for ch in range(16):
                for tb in range(4):
                    extra.append(lambda ch=ch, tb=tb: g_unit(ch, tb))

            def score_tile(i):
                L = (i + 1) * 128
                dg = diag[i % 2]
                sc = score[i % 2]
                for h in range(16):
                    fw.op(fw.act, lambda h=h: nc.scalar.mul(out=dg[:, h, :], in_=self.identf[:], mul=wi[:, i, h:h + 1]),
                          [self.identf, wi_b[i]], [dg])
                nkb = (L + 511) // 512
                seq = [(kb, h) for kb in range(nkb) for h in range(16)]
                pscs = {}
                inflight = {}

                def issue_dots(q):
                    kb, h = seq[q]
                    n = min(512, L - kb * 512)
                    ks = slice(kb * 512, kb * 512 + n)
                    c, half = h // 2, h % 2
                    pd = self.pb[cnt["di"] % 4]; cnt["di"] += 1
                    p0 = half * 64
                    fw.op(fw.pe, lambda: nc.tensor.matmul(
                        pd[:, 0:n], lhsT=qiT[p0:p0 + 64, c, i * 128:(i + 1) * 128], rhs=kiT[p0:p0 + 64, ks],
                        start=True, stop=True), [qiT_b[c][i // 4], kiT_b[kb]], [pd])
                    r = rl[cnt["ri"] % 4]; cnt["ri"] += 1
                    fw.op(fw.act, lambda: nc.scalar.activation(out=r[:, 0:n], in_=pd[:, 0:n], func=AF.Relu), [pd], [r])
                    inflight[q] = r

                def issue_acc(q):
                    kb, h = seq[q]
                    n = min(512, L - kb * 512)
                    ks = slice(kb * 512, kb * 512 + n)
                    if h == 0:
                        pscs[kb] = self.pb[4 + (cnt["pi"] % 2)]; cnt["pi"] += 1
                    psc = pscs[kb]
                    r = inflight.pop(q)
                    fw.op(fw.pe, lambda: nc.tensor.matmul(
                        psc[:, 0:n], lhsT=dg[:, h, :], rhs=r[:, 0:n], start=(h == 0), stop=(h == 15)),
                        [dg, r], [psc])
                    if h == 15:
                        self.copy(fw.act, sc[:, ks], psc[:, 0:n], [psc], [sc])

                LA = 3
                for q in range(min(LA, len(seq))):
                    issue_dots(q)
                for q in range(len(seq)):
                    if q + LA < len(seq):
                        issue_dots(q + LA)
                    issue_acc(q)
                fw.op(fw.pool, lambda: nc.gpsimd.affine_select(
                    out=sc[:, i * 128:(i + 1) * 128], in_=sc[:, i * 128:(i + 1) * 128], pattern=[[-1, 128]],
                    compare_op=ALU.is_ge, fill=self.reg_neg, base=0, channel_multiplier=1), [sc], [sc])

            def topk_tile(i):
                L = (i + 1) * 128
                sc = score[i % 2]
                br = brow[i % 2]
                if i >= 2:
                    cur = sc
                    for it in range(32):
                        fw.op(fw.dve, lambda: nc.vector.max(out=m8[:], in_=cur[:, 0:L]), [cur], [m8])
                        if it < 31:
                            fw.op(fw.dve, lambda: nc.vector.match_replace(
                                out=work[:, 0:L], in_to_replace=m8[:], in_values=cur[:, 0:L], imm_value=-1e30),
                                [cur, m8], [work])
                            cur = work
                    fw.op(fw.dve, lambda: nc.vector.tensor_scalar(
                        out=br[:, 0:L], in0=sc[:, 0:L], scalar1=m8[:, 7:8], scalar2=None, op0=ALU.is_ge),
                        [sc, m8], [br])
                else:
                    fw.op(fw.dve, lambda: nc.vector.tensor_scalar(
                        out=br[:, 0:L], in0=sc[:, 0:L], scalar1=-1e29, scalar2=None, op0=ALU.is_ge),
                        [sc], [br])
                for j4 in range(0, i + 1, 4):
                    nj = min(4, i + 1 - j4)
                    pt = self.ptb[cnt["ti"] % 2]; cnt["ti"] += 1

                    def f():
                        for jj in range(nj):
                            j = j4 + jj
                            ins = nc.tensor.transpose(pt[:, jj * 128:(jj + 1) * 128], br[:, j * 128:(j + 1) * 128],
                                                      self.ident[:])
                        return ins
                    fw.op(fw.pe, f, [br, self.ident], [pt])
                    b = bst[cnt["bi"] % 2]; cnt["bi"] += 1
                    self.copy(fw.act, b[:, 0:nj, :], pt[:, 0:nj * 128].rearrange("p (c t) -> p c t", c=nj), [pt], [b])
                    fw.dma(fw.sp, [self.biasT_b[j4 + jj][i] for jj in range(nj)], b,
                           self.biasT_s[j4:j4 + nj, :, i * 128:(i + 1) * 128].rearrange("j p t -> p j t"),
                           b[:, 0:nj, :])

            score_tile(0)
            for i in range(NT):
                if i + 1 < NT:
                    score_tile(i + 1)
                ne = (len(extra) + (NT - 1 - i)) // (NT - i)
                for _ in range(ne):
                    extra.pop(0)()
                topk_tile(i)
            assert not extra

    def attn_a(self, kT, kT_b, v, v_b, aoT, aoT_b):
        nc, fw = self.nc, self.fw
        with ExitStack() as st:
            st.enter_context(nc.named_scope(self.tag + "attnA"))
            qblk = self.sb(st, "qblk", [128, H, 512], BF16)
            gblk = self.sb(st, "gblk", [128, H, 512], BF16)
            bblk = self.sb(st, "bblk", [128, NT, 512], BF16)
            qblk_b = [Buf(qblk.t) for _ in range(H)]
            gblk_b = [Buf(gblk.t) for _ in range(H)]
            bblk_b = [Buf(bblk.t) for _ in range(NT)]
            pT = [self.sb(st, "pT%d" % i, [128, 512], BF16) for i in range(8)]
            rden = self.sb(st, "rden", [128, 512], F32)
            o = self.sb(st, "o", [128, 512], F32)
            pi = 0
            for b in range(4):
                qs = slice(b * 512, (b + 1) * 512)
                nj = 4 * b + 4
                for h in range(H):
                    fw.dma(fw.sp, qblk_b[h], self.qT_b[h][b], qblk[:, h, :], self.qT_s[h, :, qs])
                    fw.dma(fw.sp, gblk_b[h], self.gT_b[h][b], gblk[:, h, :], self.gT_s[h, :, qs])
                for j in range(nj):
                    t0 = max(0, j - 4 * b)
                    fw.dma(fw.sp, bblk_b[j], [self.biasT_b[j][i] for i in range(4 * b + t0, 4 * b + 4)],
                           bblk[:, j, t0 * 128:512], self.biasT_s[j, :, b * 512 + t0 * 128:(b + 1) * 512])
                for h in range(H):
                    g = h // 4
                    pout = self.pb[4]
                    pden = self.pb[5]

                    def issue_s(j):
                        q0 = max(0, j - 4 * b) * 128
                        ps = self.pb[j % 4]

                        def f():
                            return nc.tensor.matmul(ps[:, q0:512], lhsT=kT[:, g, j * 128:(j + 1) * 128],
                                                    rhs=qblk[:, h, q0:512], start=True, stop=True)
                        fw.op(fw.pe, f, [kT_b[g][j // 4], qblk_b[h]], [ps])
                        pt = pT[j % 8]
                        fw.op(fw.act, lambda: nc.scalar.activation(out=pt[:, q0:512], in_=ps[:, q0:512], func=AF.Exp,
                                                                   scale=SCALE), [ps], [pt])
                        fw.op(fw.dve, lambda: nc.vector.tensor_tensor(out=pt[:, q0:512], in0=pt[:, q0:512],
                                                                      in1=bblk[:, j, q0:512], op=ALU.mult),
                              [pt, bblk_b[j]], [pt])

                    def issue_av(j):
                        q0 = max(0, j - 4 * b) * 128
                        pt = pT[j % 8]
                        fw.op(fw.pe, lambda: nc.tensor.matmul(pout[:, q0:512], lhsT=v[:, j, g * 128:(g + 1) * 128],
                                                              rhs=pt[:, q0:512], start=(j == 0), stop=(j == nj - 1)),
                              [v_b[j], pt], [pout])
                        fw.op(fw.pe, lambda: nc.tensor.matmul(pden[:, q0:512], lhsT=self.ones_bf[:],
                                                              rhs=pt[:, q0:512], start=(j == 0), stop=(j == nj - 1)),
                              [self.ones_bf, pt], [pden])
                    LA = 3
                    for j in range(min(LA, nj)):
                        issue_s(j)
                    for j in range(nj):
                        if j + LA < nj:
                            issue_s(j + LA)
                        issue_av(j)
                    fw.op(fw.act, lambda: nc.scalar.activation(out=rden[:], in_=pden[:], func=AF.Ln), [pden], [rden])
                    fw.op(fw.act, lambda: nc.scalar.activation(out=rden[:], in_=rden[:], func=AF.Exp, scale=-1.0),
                          [rden], [rden])
                    fw.op(fw.dve, lambda: nc.vector.tensor_tensor(out=o[:], in0=pout[:], in1=rden[:], op=ALU.mult),
                          [pout, rden], [o])
                    fw.op(fw.pool, lambda h=h: nc.gpsimd.tensor_tensor(out=aoT[:, h, qs], in0=o[:], in1=gblk[:, h, :],
                                                                       op=ALU.mult),
                          [o, gblk_b[h]], aoT_b[h][4 * b:4 * b + 4])

    def proj_fm_store(self, W, col0, nchunks, xT, xT_b, wb, stg, dst_s, dst_b, act_func, gi0=0):
        nc, fw = self.nc, self.fw
        gi = gi0
        pi = 0
        si = 0
        for grp in range(nchunks // 2):
            w = wb[gi % 2]; gi += 1
            self.load_w(w, W[:, col0 + grp * 256:col0 + grp * 256 + 256], 256)
            for c in range(2):
                ch = grp * 2 + c
                for tb in range(4):
                    ps = self.pb[pi % 4]; pi += 1
                    self.fm_chunk(ps, w, c, xT, xT_b, tb)
                    sg = stg[si % 4]; si += 1
                    if act_func is None:
                        self.copy(self.evac_eng(), sg[:], ps[:], [ps], [sg])
                    else:
                        fw.op(fw.act, lambda ps=ps, sg=sg: nc.scalar.activation(out=sg[:], in_=ps[:], func=act_func),
                              [ps], [sg])
                    fw.dma(fw.sp, dst_b[ch][tb], sg, dst_s[ch, :, tb * 512:(tb + 1) * 512], sg[:])
        return gi

    def layer_b(self, s, l, src, src_b, aoT, aoT_b):
        nc, fw = self.nc, self.fw
        jl = l - 2
        with ExitStack() as st:
            xT, xT_b = self.phase_T(st, src, src_b)
            wb = [self.sb(st, "wb%d" % i, [128, NT, 256], BF16) for i in range(2)]
            stg = [self.sb(st, "stg%d" % i, [128, 512], BF16) for i in range(4)]
            st.enter_context(nc.named_scope(self.tag + "projB"))
            gi = 0
            if l == 2:
                gi = self.proj_fm_store(self.w_kv_b, 0, 16, xT, xT_b, wb, stg, self.kTb_s, self.kTb_b, None, gi)
                vst = self.sb(st, "vst", [128, NT, 256], BF16)
                pi = 0
                for grp in range(8):
                    w = wb[gi % 2]; gi += 1
                    self.load_w(w, self.w_kv_b[:, 2048 + grp * 256:2048 + grp * 256 + 256], 256)
                    for i in range(NT):
                        ps = self.pb[pi % 4]; pi += 1

                        def f(ps=ps, i=i, w=w):
                            for k in range(NT):
                                ins = nc.tensor.matmul(ps[:, 0:256], lhsT=xT[:, k, i * 128:(i + 1) * 128],
                                                       rhs=w[:, k, 0:256], start=(k == 0), stop=(k == NT - 1))
                            return ins
                        fw.op(fw.pe, f, [w, xT_b[i]], [ps])
                        self.copy(self.evac_eng(), vst[:, i, :], ps[:, 0:256], [ps], [vst])
                    for c in range(2):
                        hh = grp * 2 + c
                        fw.dma(fw.sp, self.vb_b[hh], vst, self.vb_s[hh], vst[:, :, c * 128:(c + 1) * 128])
            W = self.w_q_b[jl]
            gi = self.proj_fm_store(W, 0, 16, xT, xT_b, wb, stg, self.qT_s, self.qT_b, None, gi)
            gi = self.proj_fm_store(W, 2048, 16, xT, xT_b, wb, stg, self.gT_s, self.gT_b, AF.Silu, gi)
            fw.barrier()
        self.attn_b(aoT, aoT_b)
        fw.barrier()

    def attn_b(self, aoT, aoT_b):
        nc, fw = self.nc, self.fw
        with ExitStack() as st:
            st.enter_context(nc.named_scope(self.tag + "attnB"))
            kh = [self.sb(st, "kh%d" % i, [128, S], BF16) for i in range(2)]
            vh = [self.sb(st, "vh%d" % i, [128, NT, 128], BF16) for i in range(2)]
            qh = [self.sb(st, "qh%d" % i, [128, S], BF16) for i in range(2)]
            gh = [self.sb(st, "gh%d" % i, [128, S], BF16) for i in range(2)]
            Eb = [self.sb(st, "Eb%d" % i, [128, S], F32) for i in range(3)]
            Sb = [self.sb(st, "Sb%d" % i, [128, S], F32) for i in range(3)]
            Cb = [self.sb(st, "Cb%d" % i, [128, S + 1], F32) for i in range(2)]
            ntot = [self.sb(st, "ntot%d" % i, [128, 1], F32) for i in range(2)]
            ones = self.sb(st, "ones", [128, S], F32)
            Ab = [self.sb(st, "Ab%d" % i, [128, S], BF16) for i in range(2)]
            AT = [self.sb(st, "AT%d" % i, [128, NT, 128], BF16) for i in range(2)]
            fw.op(fw.dve, lambda: nc.vector.memset(ones[:], 1.0), [], [ones])
            for c in Cb:
                fw.op(fw.dve, lambda c=c: nc.vector.memset(c[:, 0:1], 0.0), [], [c])
            iters = [(h, i) for h in range(H) for i in range(NT)]
            N = len(iters)
            pz = self.pzt
            st8 = {"cc": 0, "ti": 0}

            def loadhead(h):
                k_, v_, q_, g_ = kh[h % 2], vh[h % 2], qh[h % 2], gh[h % 2]
                fw.dma(fw.sp, k_, self.kTb_b[h], k_[:], self.kTb_s[h])
                fw.dma(fw.sp, q_, self.qT_b[h], q_[:], self.qT_s[h])
                fw.dma(fw.sp, v_, self.vb_b[h], v_[:], self.vb_s[h])
                fw.dma(fw.sp, g_, self.gT_b[h], g_[:], self.gT_s[h])

            def stageA(n):
                h, i = iters[n]
                if n == 0:
                    loadhead(0)
                if i == 8 and h + 1 < H:
                    loadhead(h + 1)
                k_, q_ = kh[h % 2], qh[h % 2]
                L = (i + 1) * 128
                e, sp = Eb[n % 3], Sb[n % 3]
                for c0 in range(0, L, 1024):
                    nn = min(1024, L - c0)
                    cc = st8["cc"]; st8["cc"] += 1
                    base = (cc % 2) * 1024
                    zb = self.pb[2 * (cc % 2):2 * (cc % 2) + (nn + 511) // 512]

                    def fz():
                        for s0 in range(0, nn, 512):
                            m = min(512, nn - s0)
                            ins = nc.tensor.matmul(pz[:, base + s0:base + s0 + m], lhsT=q_[:, i * 128:(i + 1) * 128],
                                                   rhs=k_[:, c0 + s0:c0 + s0 + m], start=True, stop=True)
                        return ins
                    fw.op(fw.pe, fz, [q_, k_], zb)
                    fw.op(fw.act, lambda: nc.scalar.activation(out=e[:, c0:c0 + nn], in_=pz[:, base:base + nn],
                                                               func=AF.Exp, scale=SCALE), zb, [e])
                fw.op(fw.act, lambda: nc.scalar.activation(out=sp[:, 0:L], in_=e[:, 0:L], func=AF.Ln,
                                                           bias=self.cst[:, 3:4]), [e, self.cst], [sp])
                d0 = i * 128
                fw.op(fw.pool, lambda: nc.gpsimd.affine_select(
                    out=sp[:, d0:d0 + 128], in_=sp[:, d0:d0 + 128], pattern=[[-1, 128]], compare_op=ALU.is_gt,
                    fill=self.reg_zero, base=0, channel_multiplier=1), [sp], [sp])

            def stageB(n):
                h, i = iters[n]
                L = (i + 1) * 128
                sp, cb, nt = Sb[n % 3], Cb[n % 2], ntot[n % 2]
                fw.op(fw.dve, lambda: nc.vector.tensor_tensor_scan(
                    out=cb[:, 1:L + 1], data0=ones[:, 0:L], data1=sp[:, 0:L], initial=0.0, op0=ALU.mult, op1=ALU.add),
                    [ones, sp], [cb])
                fw.op(fw.dve, lambda: nc.vector.tensor_scalar(out=nt[:], in0=cb[:, L:L + 1], scalar1=-1.0,
                                                              scalar2=None, op0=ALU.mult), [cb], [nt])
                fw.op(fw.act, lambda: nc.scalar.activation(out=sp[:, 0:L], in_=cb[:, 0:L], func=AF.Exp,
                                                           bias=nt[:, 0:1]), [cb, nt], [sp])

            def stageB2(n):
                h, i = iters[n]
                L = (i + 1) * 128
                e, sp, A = Eb[n % 3], Sb[n % 3], Ab[n % 2]
                fw.op(fw.dve, lambda: nc.vector.tensor_tensor(out=A[:, 0:L], in0=e[:, 0:L], in1=sp[:, 0:L], op=ALU.mult),
                      [e, sp], [A])
                d0 = i * 128
                fw.op(fw.pool, lambda: nc.gpsimd.affine_select(
                    out=A[:, d0:d0 + 128], in_=A[:, d0:d0 + 128], pattern=[[-1, 128]], compare_op=ALU.is_gt,
                    fill=self.reg_zero, base=0, channel_multiplier=1), [A], [A])

            def stageC(n):
                h, i = iters[n]
                A, at = Ab[n % 2], AT[n % 2]
                for j4 in range(0, i + 1, 4):
                    nj = min(4, i + 1 - j4)
                    pt = self.ptb[st8["ti"] % 2]; st8["ti"] += 1

                    def f():
                        for jj in range(nj):
                            j = j4 + jj
                            ins = nc.tensor.transpose(pt[:, jj * 128:(jj + 1) * 128], A[:, j * 128:(j + 1) * 128],
                                                      self.ident[:])
                        return ins
                    fw.op(fw.pe, f, [A, self.ident], [pt])
                    self.copy(fw.act if (st8["ti"] % 3 == 0) else fw.dve, at[:, j4:j4 + nj, :],
                              pt[:, 0:nj * 128].rearrange("p (c t) -> p c t", c=nj), [pt], [at])

            def stageC2(n):
                h, i = iters[n]
                v_, g_ = vh[h % 2], gh[h % 2]
                at = AT[n % 2]
                po = self.pb[4 + (n % 2)]

                def fo():
                    for j in range(i + 1):
                        ins = nc.tensor.matmul(po[:, 0:128], lhsT=v_[:, j, :], rhs=at[:, j, :],
                                               start=(j == 0), stop=(j == i))
                    return ins
                fw.op(fw.pe, fo, [v_, at], [po])
                fw.op(fw.dve, lambda: nc.vector.tensor_tensor(out=aoT[:, h, i * 128:(i + 1) * 128], in0=po[:, 0:128],
                                                              in1=g_[:, i * 128:(i + 1) * 128], op=ALU.mult),
                      [po, g_], [aoT_b[h][i]])

            for t in range(N + 4):
                if t < N:
                    stageA(t)
                if 0 <= t - 1 < N:
                    stageB(t - 1)
                if 0 <= t - 2 < N:
                    stageB2(t - 2)
                if 0 <= t - 3 < N:
                    stageC(t - 3)
                if 0 <= t - 4 < N:
                    stageC2(t - 4)

    def epilogue(self, s, l, src, src_b, dst, dst_b, aoT, aoT_b, w_out):
        nc, fw = self.nc, self.fw
        with ExitStack() as st:
            st.enter_context(nc.named_scope(self.tag + "epi"))
            wb = [self.sb(st, "ewb%d" % i, [128, NT, 512], BF16) for i in range(2)]
            wp = [self.sb(st, "ewp%d" % i, [128, 2, 512], BF16) for i in range(2)]
            z = [self.sb(st, "z%d" % i, [128, D], F32) for i in range(4)]
            gB = self.sb(st, "gB", [128, D], F32)
            bB = self.sb(st, "bB", [128, D], F32)
            xlnT = self.sb(st, "xlnT", [128, NT, 512], BF16)
            xlnT_b = [Buf(xlnT.t) for _ in range(4)]
            pT = self.sb(st, "ppT", [128, 2, 512], BF16)
            pT_b = [Buf(pT.t) for _ in range(4)]
            pbf = [self.sb(st, "pbf%d" % i, [128, 256], BF16) for i in range(4)]
            xb = [self.sb(st, "exb%d" % i, [128, D], BF16) for i in range(2)]
            sig = [self.sb(st, "sig%d" % i, [128, 512], F32) for i in range(2)]
            tmp = [self.sb(st, "tmp%d" % i, [128, 512], F32) for i in range(2)]
            stats = self.sb(st, "stats", [128, 4, 4, 6], F32)
            mv = self.sb(st, "mv", [128, 4, 2], F32)
            rstd = self.sb(st, "rstd", [128, 4, 1], F32)
            nbias = self.sb(st, "nbias", [128, 1], F32)
            fw.dma(fw.sp, gB, self.wdram, gB[:], self.ln_g[l:l + 1, :].partition_broadcast(128))
            fw.dma(fw.sp, bB, self.wdram, bB[:], self.ln_b[l:l + 1, :].partition_broadcast(128))
            gi = 0
            pi = 0
            ti = 0
            for b in range(4):
                if b == 0:
                    for t in range(4):
                        fw.dma(fw.sp, z[t], src_b[t], z[t][:], src[t * 128:(t + 1) * 128, :])
                for t in range(4):
                    i = 4 * b + t
                    pb_ = pbf[t]
                    fw.dma(fw.pool, pb_, self.wdram, pb_[:], self.p_d[l, s, i * 128:(i + 1) * 128, :])
                    pt = self.ptb[ti % 2]; ti += 1

                    def f2(pt=pt, pb_=pb_):
                        for c in range(2):
                            ins = nc.tensor.transpose(pt[:, c * 128:(c + 1) * 128], pb_[:, c * 128:(c + 1) * 128],
                                                      self.ident[:])
                        return ins
                    fw.op(fw.pe, f2, [pb_, self.ident], [pt])
                    self.copy(fw.act, pT[:, :, t * 128:(t + 1) * 128],
                              pt[:, 0:256].rearrange("p (c t) -> p c t", c=2), [pt], [pT_b[t]])
                for n in range(4):
                    w = wb[gi % 2]; gi += 1
                    ns_ = slice(n * 512, (n + 1) * 512)
                    self.load_w(w, w_out[:, ns_], 512)
                    for t in range(4):
                        i = 4 * b + t
                        ps = self.pb[pi % 4]; pi += 1

                        def f(ps=ps, w=w, i=i):
                            for k in range(NT):
                                ins = nc.tensor.matmul(ps[:], lhsT=aoT[:, k, i * 128:(i + 1) * 128], rhs=w[:, k, :],
                                                       start=(k == 0), stop=(k == NT - 1))
                            return ins
                        fw.op(fw.pe, f, [w] + [aoT_b[k][i] for k in range(H)], [ps])
                        fw.op(fw.dve, lambda ps=ps, t=t, ns_=ns_: nc.vector.scalar_tensor_tensor(
                            out=z[t][:, ns_], in0=z[t][:, ns_], scalar=ALPHA, in1=ps[:], op0=ALU.mult, op1=ALU.add),
                            [z[t], ps], [z[t]])
                        fw.op(fw.dve, lambda t=t, n=n, ns_=ns_: nc.vector.bn_stats(out=stats[:, t, n, :], in_=z[t][:, ns_]),
                              [z[t]], [stats])
                w3 = wb[gi % 2]; gi += 1
                self.load_w(w3, self.w_gate[l][:, 0:512], 512)
                self.load_w(wp[0], self.w_ple[l][:, 0:512], 512)
                for t in range(4):
                    fw.op(fw.dve, lambda t=t: nc.vector.bn_aggr(out=mv[:, t, :], in_=stats[:, t].rearrange("p a b -> p (a b)")),
                          [stats], [mv])
                fw.op(fw.dve, lambda: nc.vector.tensor_scalar(out=rstd[:], in0=mv[:, :, 1:2], scalar1=1e-5, scalar2=None,
                                                              op0=ALU.add), [mv], [rstd])
                fw.op(fw.act, lambda: nc.scalar.sqrt(out=rstd[:], in_=rstd[:]), [rstd], [rstd])
                fw.op(fw.dve, lambda: nc.vector.reciprocal(out=rstd[:], in_=rstd[:]), [rstd], [rstd])
                for t in range(4):
                    i = 4 * b + t
                    zt = z[t]
                    fw.op(fw.dve, lambda zt=zt, t=t: nc.vector.scalar_tensor_tensor(
                        out=zt[:], in0=zt[:], scalar=mv[:, t, 0:1], in1=gB[:], op0=ALU.subtract, op1=ALU.mult),
                        [zt, mv, gB], [zt])
                    fw.op(fw.dve, lambda zt=zt, t=t: nc.vector.scalar_tensor_tensor(
                        out=zt[:], in0=zt[:], scalar=rstd[:, t, 0:1], in1=bB[:], op0=ALU.mult, op1=ALU.add),
                        [zt, rstd, bB], [zt])
                    x_ = xb[t % 2]
                    self.copy(fw.act, x_[:], zt[:], [zt], [x_])
                    for c4 in range(4):
                        pt = self.ptb[ti % 2]; ti += 1

                        def f(pt=pt, x_=x_, c4=c4):
                            for c in range(4):
                                k = c4 * 4 + c
                                ins = nc.tensor.transpose(pt[:, c * 128:(c + 1) * 128], x_[:, k * 128:(k + 1) * 128],
                                                          self.ident[:])
                            return ins
                        fw.op(fw.pe, f, [x_, self.ident], [pt])
                        self.copy(self.evac_eng(), xlnT[:, c4 * 4:(c4 + 1) * 4, t * 128:(t + 1) * 128],
                                  pt[:].rearrange("p (c t) -> p c t", c=4), [pt], [xlnT_b[t]])
                for n in range(4):
                    ns_ = slice(n * 512, (n + 1) * 512)
                    w2 = wp[n % 2]
                    if n == 0:
                        w = w3
                    else:
                        w = wb[gi % 2]; gi += 1
                        self.load_w(w, self.w_gate[l][:, ns_], 512)
                        self.load_w(w2, self.w_ple[l][:, ns_], 512)
                    for t in range(4):
                        zt = z[t]
                        ps = self.pb[pi % 4]; pi += 1
                        ps2 = self.pb[4 + (pi % 2)]

                        def f(ps=ps, w=w, t=t):
                            for k in range(NT):
                                ins = nc.tensor.matmul(ps[:], lhsT=xlnT[:, k, t * 128:(t + 1) * 128], rhs=w[:, k, :],
                                                       start=(k == 0), stop=(k == NT - 1))
                            return ins
                        fw.op(fw.pe, f, [w, xlnT_b[t]], [ps])

                        def f3(ps2=ps2, w2=w2, t=t):
                            for k in range(2):
                                ins = nc.tensor.matmul(ps2[:], lhsT=pT[:, k, t * 128:(t + 1) * 128], rhs=w2[:, k, :],
                                                       start=(k == 0), stop=(k == 1))
                            return ins
                        fw.op(fw.pe, f3, [w2, pT_b[t]], [ps2])
                        sg = sig[(n * 4 + t) % 2]
                        tm = tmp[(n * 4 + t) % 2]
                        fw.op(fw.act, lambda ps=ps, sg=sg: nc.scalar.activation(out=sg[:], in_=ps[:], func=AF.Sigmoid),
                              [ps], [sg])
                        fw.op(fw.dve, lambda ps2=ps2, sg=sg, tm=tm: nc.vector.tensor_tensor(out=tm[:], in0=ps2[:], in1=sg[:],
                                                                                            op=ALU.mult), [ps2, sg], [tm])
                        fw.op(fw.dve, lambda zt=zt, tm=tm, ns_=ns_: nc.vector.tensor_tensor(out=zt[:, ns_], in0=zt[:, ns_],
                                                                                            in1=tm[:], op=ALU.add),
                              [zt, tm], [zt])
                for t in range(4):
                    i = 4 * b + t
                    fw.dma(fw.sp, dst_b[i], z[t], dst[i * 128:(i + 1) * 128, :], z[t][:])
                    if b < 3:
                        i2 = i + 4
                        fw.dma(fw.sp, z[t], src_b[i2], z[t][:], src[i2 * 128:(i2 + 1) * 128, :])


_CACHE = {}


def _inv_freq_table():
    p = np.arange(128)
    c0 = (np.float32(10000.0) ** (-(p % 64).astype(np.float32) / np.float32(64))).astype(np.float32)
    c1 = (np.float32(10000.0) ** (-(p % 32).astype(np.float32) / np.float32(32))).astype(np.float32)
    return np.ascontiguousarray(np.stack([c0, c1], axis=1).astype(np.float32))


def kernel(x, p, positions, w_in_a, w_out_a, w_q_b, w_kv_b, w_out_b, ln_g, ln_b, w_ple, w_ple_gate):
    n = 8
    ns = 2
    if "nc" not in _CACHE:
        _CACHE["nc"] = Prog(nseq=ns, nlayers=4).build()
    nc = _CACHE["nc"]
    f32 = lambda a: np.ascontiguousarray(np.asarray(a), dtype=np.float32)
    x = f32(x); p = f32(p)
    pos = np.ascontiguousarray(np.asarray(positions), dtype=np.int32)
    shared = {"w_in_a": f32(w_in_a), "w_out_a": f32(w_out_a), "w_q_b": f32(w_q_b), "w_kv_b": f32(w_kv_b),
              "w_out_b": f32(w_out_b), "ln_g": f32(ln_g), "ln_b": f32(ln_b), "w_ple": f32(w_ple),
              "w_ple_gate": f32(w_ple_gate), "invf": _inv_freq_table()}
    in_maps = []
    for c in range(n):
        m = dict(shared)
        m["x"] = np.ascontiguousarray(x[c * ns:(c + 1) * ns])
        m["p"] = np.ascontiguousarray(p[:, c * ns:(c + 1) * ns])
        m["pos"] = np.ascontiguousarray(pos[c * ns:(c + 1) * ns])
        in_maps.append(m)
    res = run_bass_kernel_spmd(nc, in_maps, core_ids=list(range(n)))
    return np.concatenate([np.asarray(r["out"], dtype=np.float32) for r in res.results], axis=0)
```

```python
import math
from contextlib import ExitStack

import numpy as np
import concourse.bass as bass
import concourse.mybir as mybir
from concourse.bass_utils import run_bass_kernel_spmd

F32 = mybir.dt.float32
BF16 = mybir.dt.bfloat16
I32 = mybir.dt.int32
AF = mybir.ActivationFunctionType
ALU = mybir.AluOpType

S = 2048
D = 2048
NT = 16
H = 16
DH = 128
A_IN = 6224
ALPHA = float((2 * 4) ** 0.25)
SCALE = float(DH ** -0.5)
NEG = -30000.0
PI = math.pi


class Buf:
    __slots__ = ("t", "w", "r", "name")

    def __init__(self, t, name=""):
        self.t = t
        self.w = None
        self.r = {}
        self.name = name

    def __getitem__(self, idx):
        return self.t[idx]


class Eng:
    def __init__(self, fw, name, eng, sem, self_sync=True):
        self.fw = fw
        self.name = name
        self.eng = eng
        self.sem = sem
        self.count = 0
        self.seen = {}
        self.self_sync = self_sync

    def wait(self, tok):
        if tok is None:
            return
        key, val = tok
        if val <= 0:
            return
        if key == self.name and not self.self_sync:
            return
        if self.seen.get(key, 0) >= val:
            return
        self.seen[key] = val
        self.eng.wait_ge(self.fw.sems[key], val)
        self.fw.nwaits += 1


class FW:
    def __init__(self, nc, ndma=32):
        self.nc = nc
        self.sems = {}
        self.nwaits = 0
        self.nops = 0
        self.engs = {}
        self.dma_keys = []
        self.dma_val = {}
        self.dma_rr = 0
        self.ndma = ndma

    def setup(self, stack):
        nc = self.nc
        for name, eng, ss in (("pe", nc.tensor, False), ("act", nc.scalar, True),
                              ("dve", nc.vector, True), ("pool", nc.gpsimd, True),
                              ("sp", nc.sync, False)):
            sem = stack.enter_context(nc.semaphore("s_" + name))
            self.sems[name] = sem
            self.engs[name] = Eng(self, name, eng, sem, ss)
        self.rings = {"sp": [], "pool": []}
        self.ring_rr = {"sp": 0, "pool": 0}
        for qn, pre in (("sp", "d"), ("pool", "g")):
            for i in range(self.ndma // 2):
                k = "%s%d" % (pre, i)
                self.sems[k] = stack.enter_context(nc.semaphore("s_" + k))
                self.dma_keys.append(k)
                self.rings[qn].append(k)
                self.dma_val[k] = 0
        self.pe = self.engs["pe"]
        self.act = self.engs["act"]
        self.dve = self.engs["dve"]
        self.pool = self.engs["pool"]
        self.sp = self.engs["sp"]

    def _deps(self, E, reads, writes):
        for t in reads:
            E.wait(t.w)
        for t in writes:
            E.wait(t.w)
            for k, v in t.r.items():
                E.wait((k, v))

    def op(self, E, fn, reads, writes):
        self._deps(E, reads, writes)
        ins = fn()
        self.nops += 1
        E.count += 1
        ins.then_inc(E.sem, 1)
        tok = (E.name, E.count)
        for t in reads:
            if t.r.get(E.name, 0) < E.count:
                t.r[E.name] = E.count
        for t in writes:
            t.w = tok
            t.r = {}
        return tok

    def dma(self, Q, out_b, in_b, out_ap, in_ap, **kw):
        outs = out_b if isinstance(out_b, (list, tuple)) else [out_b]
        ins_ = in_b if isinstance(in_b, (list, tuple)) else [in_b]
        self._deps(Q, ins_, outs)
        ring = self.rings[Q.name]
        k = ring[self.ring_rr[Q.name]]
        self.ring_rr[Q.name] = (self.ring_rr[Q.name] + 1) % len(ring)
        Q.wait((k, self.dma_val[k]))
        ins = Q.eng.dma_start(out=out_ap, in_=in_ap, **kw)
        self.dma_val[k] += 16
        ins.then_inc(self.sems[k], 16)
        tok = (k, self.dma_val[k])
        for b in ins_:
            b.r[k] = self.dma_val[k]
        for b in outs:
            b.w = tok
            b.r = {}
        self.nops += 1
        return tok

    def barrier(self):
        for E in self.engs.values():
            for E2 in self.engs.values():
                if E2 is not E:
                    E.wait((E2.name, E2.count))
            for k in self.dma_keys:
                E.wait((k, self.dma_val[k]))

    def finish(self, bufs):
        for b in bufs:
            self.sp.wait(b.w)


class StopBuild(Exception):
    pass


class Prog:
    def stage(self, name):
        if self.stop == name:
            self.stopped = True
        return self.stopped

    def __init__(self, nseq=2, nlayers=4, debug=False, stop=None):
        self.stop = stop
        self.stopped = False
        self.nseq = nseq
        self.nlayers = nlayers
        self.debug = debug
        self.nc = bass.Bass("TRN2", target_bir_lowering=False)
        self.rr = 0

    def sb(self, st, name, shape, dt):
        self.uid += 1
        return Buf(st.enter_context(self.nc.sbuf_tensor("%s_%d" % (name, self.uid), shape, dt)), name)

    def din(self, name, shape, dt):
        return self.nc.dram_tensor(name, shape, dt, kind="ExternalInput").ap()

    def dscr(self, name, shape, dt):
        kind = "ExternalOutput" if self.debug else "Internal"
        return self.nc.dram_tensor(name, shape, dt, kind=kind).ap()

    def evac_eng(self):
        self.rr += 1
        return self.fw.act if (self.rr & 1) else self.fw.dve

    def copy(self, E, out_ap, in_ap, reads, writes):
        nc = self.nc
        if E is self.fw.act:
            return self.fw.op(E, lambda: nc.scalar.copy(out=out_ap, in_=in_ap), reads, writes)
        elif E is self.fw.dve:
            return self.fw.op(E, lambda: nc.vector.tensor_copy(out=out_ap, in_=in_ap), reads, writes)
        else:
            return self.fw.op(E, lambda: nc.gpsimd.tensor_copy(out=out_ap, in_=in_ap), reads, writes)

    def load_w(self, wb, src2d, ncols):
        kc = src2d.shape[0] // 128
        self.fw.dma(self.fw.pool, wb, self.wdram, wb[:, 0:kc, 0:ncols],
                    src2d.rearrange("(kc p) n -> p kc n", p=128))

    def build(self):
        nc = self.nc
        ns = self.nseq
        self.uid = 0
        self.x_d = self.din("x", [ns, S, D], F32)
        self.p_d = self.din("p", [4, ns, S, 256], F32)
        self.pos_d = self.din("pos", [ns, S], I32)
        self.w_in_a = self.din("w_in_a", [2, D, A_IN], F32)
        self.w_out_a = self.din("w_out_a", [2, D, D], F32)
        self.w_q_b = self.din("w_q_b", [2, D, 2 * D], F32)
        self.w_kv_b = self.din("w_kv_b", [D, 2 * D], F32)
        self.w_out_b = self.din("w_out_b", [2, D, D], F32)
        self.ln_g = self.din("ln_g", [4, D], F32)
        self.ln_b = self.din("ln_b", [4, D], F32)
        self.w_ple = self.din("w_ple", [4, 256, D], F32)
        self.w_gate = self.din("w_ple_gate", [4, D, D], F32)
        self.invf_d = self.din("invf", [128, 2], F32)
        self.out_d = nc.dram_tensor("out", [ns, S, D], F32, kind="ExternalOutput").ap()
        self.xres = [self.dscr("xres%d" % i, [S, D], F32) for i in range(2)]
        self.qT_s = self.dscr("qT_s", [H, 128, S], BF16)
        self.gT_s = self.dscr("gT_s", [H, 128, S], BF16)
        self.biasT_s = self.dscr("biasT_s", [NT, 128, S], BF16)
        self.kTb_s = self.dscr("kTb_s", [H, 128, S], BF16)
        self.vb_s = self.dscr("vb_s", [H, 128, NT, 128], BF16)
        self.wdram = Buf(None, "weights")
        self.xin_b = Buf(None, "xin")
        self.xres_b = [[Buf(None) for _ in range(NT)] for _ in range(2)]
        self.out_b = [[Buf(None) for _ in range(NT)] for _ in range(ns)]
        self.qT_b = [[Buf(None) for _ in range(4)] for _ in range(H)]
        self.gT_b = [[Buf(None) for _ in range(4)] for _ in range(H)]
        self.biasT_b = [[Buf(None) for _ in range(NT)] for _ in range(NT)]
        self.kTb_b = [[Buf(None) for _ in range(4)] for _ in range(H)]
        self.vb_b = [Buf(None) for _ in range(H)]

        with ExitStack() as st:
            self.fw = fw = FW(nc)
            fw.setup(st)
            pzt = st.enter_context(nc.psum_tensor("pz", [128, 2048], F32))
            self.pzt = pzt
            self.pb = [Buf(pzt[:, i * 512:(i + 1) * 512], "pz%d" % i) for i in range(4)]
            for i in range(2):
                t = st.enter_context(nc.psum_tensor("pb%d" % (4 + i), [128, 512], F32))
                self.pb.append(Buf(t, "pb%d" % (4 + i)))
            self.ptb = [Buf(st.enter_context(nc.psum_tensor("ptb%d" % i, [128, 512], BF16)), "ptb%d" % i)
                        for i in range(2)]
            self.identf = self.sb(st, "identf", [128, 128], F32)
            self.ident = self.sb(st, "ident", [128, 128], BF16)
            self.ones_bf = self.sb(st, "ones_bf", [128, 128], BF16)
            self.invf = self.sb(st, "invf", [128, 2], F32)
            self.cst = self.sb(st, "cst", [128, 4], F32)
            self.big = self.sb(st, "big", [128, NT, S], BF16)
            idf = self.identf
            self.reg_neg = nc.gpsimd.to_reg(-1e30)
            self.reg_zero = nc.gpsimd.to_reg(0.0)
            fw.op(fw.pool, lambda: nc.gpsimd.memset(idf[:], 1.0), [], [idf])
            fw.op(fw.pool, lambda: nc.gpsimd.affine_select(out=idf[:], in_=idf[:], pattern=[[-1, 128]],
                                                           compare_op=ALU.is_equal, fill=0.0, base=0,
                                                           channel_multiplier=1), [idf], [idf])
            self.copy(fw.dve, self.ident[:], idf[:], [idf], [self.ident])
            fw.op(fw.dve, lambda: nc.vector.memset(self.ones_bf[:], 1.0), [], [self.ones_bf])
            fw.op(fw.dve, lambda: nc.vector.memset(self.cst[:, 0:1], PI), [], [self.cst])
            fw.op(fw.dve, lambda: nc.vector.memset(self.cst[:, 1:2], -PI), [], [self.cst])
            fw.op(fw.dve, lambda: nc.vector.memset(self.cst[:, 2:3], 0.0), [], [self.cst])
            fw.op(fw.dve, lambda: nc.vector.memset(self.cst[:, 3:4], 1.0), [], [self.cst])
            fw.dma(fw.sp, self.invf, self.wdram, self.invf[:], self.invf_d[:, :])

            for s in range(ns):
                if not self.stopped:
                    self.run_seq(s)
            outs = [b for row in self.out_b for b in row]
            if self.nlayers < 4:
                outs += [b for row in self.xres_b for b in row]
            fw.barrier()
            fw.finish(outs)
        return nc

    def run_seq(self, s):
        fw = self.fw
        for l in range(self.nlayers):
            self.tag = "s%dL%d_" % (s, l)
            if self.stopped:
                return
            if l == 0:
                src = self.x_d[s]
                src_b = [self.xin_b] * NT
            else:
                src = self.xres[(l - 1) % 2]
                src_b = self.xres_b[(l - 1) % 2]
            if l == 3:
                dst = self.out_d[s]
                dst_b = self.out_b[s]
            else:
                dst = self.xres[l % 2]
                dst_b = self.xres_b[l % 2]
            if True:
                aoT = self.big
                aoT_b = [[Buf(aoT.t) for _ in range(NT)] for _ in range(H)]
                if l < 2:
                    self.layer_a(s, l, src, src_b, aoT, aoT_b)
                else:
                    self.layer_b(s, l, src, src_b, aoT, aoT_b)
                fw.barrier()
                if self.stopped:
                    return
                w_out = self.w_out_a[l] if l < 2 else self.w_out_b[l - 2]
                self.epilogue(s, l, src, src_b, dst, dst_b, aoT, aoT_b, w_out)
                fw.barrier()

    def phase_T(self, st, src, src_b):
        nc, fw = self.nc, self.fw
        xT = self.big
        xT_b = [Buf(xT.t) for _ in range(NT)]
        with ExitStack() as st2:
            st2.enter_context(self.nc.named_scope(self.tag + "T"))
            xb = [self.sb(st2, "T_xb%d" % i, [128, D], BF16) for i in range(2)]
            for i in range(NT):
                b = xb[i % 2]
                fw.dma(fw.pool, b, src_b[i], b[:], src[i * 128:(i + 1) * 128, :])
                for c4 in range(4):
                    pt = self.ptb[(i * 4 + c4) % 2]

                    def f(pt=pt, b=b, c4=c4):
                        for c in range(4):
                            k = c4 * 4 + c
                            ins = nc.tensor.transpose(pt[:, c * 128:(c + 1) * 128], b[:, k * 128:(k + 1) * 128],
                                                      self.ident[:])
                        return ins
                    fw.op(fw.pe, f, [b, self.ident], [pt])
                    self.copy(self.evac_eng(), xT[:, c4 * 4:(c4 + 1) * 4, i * 128:(i + 1) * 128],
                              pt[:].rearrange("p (c t) -> p c t", c=4), [pt], [xT_b[i]])
            fw.barrier()
        self.stage("T")
        return xT, xT_b

    def rope_tables(self, st, s, col, period):
        nc, fw = self.nc, self.fw
        cosT = self.sb(st, "cosT", [128, S], F32)
        sinT = self.sb(st, "sinT", [128, S], F32)
        with ExitStack() as st2:
            posi = self.sb(st2, "posi", [128, S], I32)
            ang = self.sb(st2, "ang", [128, S], F32)
            fw.dma(fw.sp, posi, self.wdram, posi[:], self.pos_d[s:s + 1, :].partition_broadcast(128))
            self.copy(fw.dve, ang[:], posi[:], [posi], [ang])
            fw.op(fw.dve, lambda: nc.vector.tensor_scalar(out=ang[:], in0=ang[:], scalar1=self.invf[:, col:col + 1],
                                                          scalar2=None, op0=ALU.mult), [ang, self.invf], [ang])
            tmp = posi
            tmpf = tmp[:].bitcast(F32)
            C1 = 6.28125
            C2 = 2 * PI - 6.28125
            TS = nc.vector.tensor_scalar
            STT = nc.vector.scalar_tensor_tensor
            fw.op(fw.dve, lambda: TS(out=posi[:], in0=ang[:], scalar1=1.0 / (2 * PI), scalar2=None, op0=ALU.mult),
                  [ang], [posi])
            self.copy(fw.dve, cosT[:], posi[:], [posi], [cosT])
            fw.op(fw.dve, lambda: STT(out=ang[:], in0=cosT[:], scalar=-C1, in1=ang[:], op0=ALU.mult, op1=ALU.add),
                  [cosT, ang], [ang])
            fw.op(fw.dve, lambda: STT(out=ang[:], in0=cosT[:], scalar=-C2, in1=ang[:], op0=ALU.mult, op1=ALU.add),
                  [cosT, ang], [ang])
            fw.op(fw.dve, lambda: TS(out=sinT[:], in0=ang[:], scalar1=PI, scalar2=-2 * PI, op0=ALU.is_gt, op1=ALU.mult),
                  [ang], [sinT])
            fw.op(fw.dve, lambda: nc.vector.tensor_tensor(out=ang[:], in0=ang[:], in1=sinT[:], op=ALU.add),
                  [ang, sinT], [ang])
            fw.op(fw.dve, lambda: TS(out=sinT[:], in0=ang[:], scalar1=-PI, scalar2=2 * PI, op0=ALU.is_lt, op1=ALU.mult),
                  [ang], [sinT])
            fw.op(fw.dve, lambda: nc.vector.tensor_tensor(out=ang[:], in0=ang[:], in1=sinT[:], op=ALU.add),
                  [ang, sinT], [ang])
            fw.op(fw.dve, lambda: TS(out=ang[:], in0=ang[:], scalar1=-PI, scalar2=PI, op0=ALU.max, op1=ALU.min),
                  [ang], [ang])
            fw.op(fw.dve, lambda: TS(out=tmpf, in0=ang[:], scalar1=0.5 * PI, scalar2=None, op0=ALU.add), [ang], [tmp])
            fw.op(fw.dve, lambda: TS(out=cosT[:], in0=tmpf, scalar1=PI, scalar2=-2 * PI, op0=ALU.is_gt, op1=ALU.mult),
                  [tmp], [cosT])
            fw.op(fw.dve, lambda: nc.vector.tensor_tensor(out=tmpf, in0=tmpf, in1=cosT[:], op=ALU.add), [tmp, cosT], [tmp])
            fw.op(fw.dve, lambda: TS(out=tmpf, in0=tmpf, scalar1=-PI, scalar2=PI, op0=ALU.max, op1=ALU.min), [tmp], [tmp])
            half = period // 2
            for g0 in range(0, 128, period):
                fw.op(fw.act, lambda g0=g0: nc.scalar.activation(out=sinT[g0:g0 + half, :], in_=ang[g0:g0 + half, :],
                                                                 func=AF.Sin, scale=1.0), [ang], [sinT])
                fw.op(fw.act, lambda g0=g0: nc.scalar.activation(out=sinT[g0 + half:g0 + period, :],
                                                                 in_=ang[g0 + half:g0 + period, :],
                                                                 func=AF.Sin, scale=-1.0), [ang], [sinT])
            fw.op(fw.act, lambda: nc.scalar.activation(out=cosT[:], in_=tmpf, func=AF.Sin, scale=1.0), [tmp], [cosT])
        return cosT, sinT

    def rope_post(self, ps, raw, t1, t2, cosT, sinT, tb, period, out_ap, out_bufs):
        nc, fw = self.nc, self.fw
        cs = slice(tb * 512, (tb + 1) * 512)
        self.copy(fw.act, raw[:], ps[:], [ps], [raw])
        fw.op(fw.dve, lambda: nc.vector.tensor_tensor(out=t1[:], in0=raw[:], in1=cosT[:, cs], op=ALU.mult),
              [raw, cosT], [t1])
        half = period // 2
        for g0 in range(0, 128, period):
            a0, a1, a2 = g0, g0 + half, g0 + period
            fw.op(fw.dve, lambda a0=a0, a1=a1, a2=a2: nc.vector.tensor_tensor(
                out=t2[a0:a1, :], in0=raw[a1:a2, :], in1=sinT[a1:a2, cs], op=ALU.mult), [raw, sinT], [t2])
            fw.op(fw.dve, lambda a0=a0, a1=a1, a2=a2: nc.vector.tensor_tensor(
                out=t2[a1:a2, :], in0=raw[a0:a1, :], in1=sinT[a0:a1, cs], op=ALU.mult), [raw, sinT], [t2])
        fw.op(fw.dve, lambda: nc.vector.tensor_tensor(out=out_ap, in0=t1[:], in1=t2[:], op=ALU.add),
              [t1, t2], out_bufs)

    def fm_chunk(self, ps, wb, c, xT, xT_b, tb):
        nc, fw = self.nc, self.fw

        def f():
            for k in range(NT):
                ins = nc.tensor.matmul(ps[:], lhsT=wb[:, k, c * 128:(c + 1) * 128],
                                       rhs=xT[:, k, tb * 512:(tb + 1) * 512], start=(k == 0), stop=(k == NT - 1))
            return ins
        fw.op(fw.pe, f, [wb] + xT_b[tb * 4:(tb + 1) * 4], [ps])

    def layer_a(self, s, l, src, src_b, aoT, aoT_b):
        nc, fw = self.nc, self.fw
        W = self.w_in_a[l]
        with ExitStack() as st_kv:
            kT = self.sb(st_kv, "kT", [128, 4, S], BF16)
            kT_b = [[Buf(kT.t) for _ in range(4)] for _ in range(4)]
            v = self.sb(st_kv, "v", [128, NT, 512], BF16)
            v_b = [Buf(v.t) for _ in range(NT)]
            with ExitStack() as st_idx:
                qiT = self.sb(st_idx, "qiT", [128, 8, S], BF16)
                qiT_b = [[Buf(qiT.t) for _ in range(4)] for _ in range(8)]
                kiT = self.sb(st_idx, "kiT", [128, S], BF16)
                kiT_b = [Buf(kiT.t) for _ in range(4)]
                wi = self.sb(st_idx, "wi", [128, NT, 16], F32)
                wi_b = [Buf(wi.t) for _ in range(NT)]
                with ExitStack() as st:
                    xT, xT_b = self.phase_T(st, src, src_b)
                    if self.stopped:
                        return
                    wb = [self.sb(st, "wb%d" % i, [128, NT, 256], BF16) for i in range(2)]
                    raw = [self.sb(st, "raw%d" % i, [128, 512], F32) for i in range(2)]
                    t1 = [self.sb(st, "t1%d" % i, [128, 512], F32) for i in range(2)]
                    t2 = [self.sb(st, "t2%d" % i, [128, 512], F32) for i in range(2)]
                    stg = [self.sb(st, "stg%d" % i, [128, 512], BF16) for i in range(4)]
                    st.enter_context(nc.named_scope(self.tag + "projA"))
                    gi = 0
                    pi = 0
                    si = 0
                    ri = 0
                    with ExitStack() as st_r:
                        cosT, sinT = self.rope_tables(st_r, s, 0, 128)
                        for grp in range(10):
                            col0 = grp * 256
                            w = wb[gi % 2]; gi += 1
                            self.load_w(w, W[:, col0:col0 + 256], 256)
                            for c in range(2):
                                ch = grp * 2 + c
                                for tb in range(4):
                                    ps = self.pb[pi % 4]; pi += 1
                                    self.fm_chunk(ps, w, c, xT, xT_b, tb)
                                    r = ri % 2; ri += 1
                                    if ch < 16:
                                        sg = stg[si % 4]; si += 1
                                        self.rope_post(ps, raw[r], t1[r], t2[r], cosT, sinT, tb, 128, sg[:], [sg])
                                        fw.dma(fw.sp, self.qT_b[ch][tb], sg, self.qT_s[ch, :, tb * 512:(tb + 1) * 512], sg[:])
                                    else:
                                        g = ch - 16
                                        self.rope_post(ps, raw[r], t1[r], t2[r], cosT, sinT, tb, 128,
                                                       kT[:, g, tb * 512:(tb + 1) * 512], [kT_b[g][tb]])
                        fw.barrier()
                        if self.stage("qk"):
                            return
                    with ExitStack() as st_r:
                        cosT, sinT = self.rope_tables(st_r, s, 1, 64)
                        for grp in range(4):
                            col0 = 5120 + grp * 256
                            w = wb[gi % 2]; gi += 1
                            self.load_w(w, W[:, col0:col0 + 256], 256)
                            for c in range(2):
                                ch = grp * 2 + c
                                for tb in range(4):
                                    ps = self.pb[pi % 4]; pi += 1
                                    self.fm_chunk(ps, w, c, xT, xT_b, tb)
                                    r = ri % 2; ri += 1
                                    self.rope_post(ps, raw[r], t1[r], t2[r], cosT, sinT, tb, 64,
                                                   qiT[:, ch, tb * 512:(tb + 1) * 512], [qiT_b[ch][tb]])
                        w = wb[gi % 2]; gi += 1
                        Wk = W[:, 6160:6224].rearrange("(kc p) n -> p kc n", p=128)
                        fw.dma(fw.pool, w, self.wdram, w[:, :, 0:64], Wk)
                        fw.dma(fw.pool, w, self.wdram, w[:, :, 64:128], Wk)
                        fw.dma(fw.pool, w, self.wdram, w[:, :, 128:144],
                               W[:, 6144:6160].rearrange("(kc p) n -> p kc n", p=128))
                        for tb in range(4):
                            ps = self.pb[pi % 4]; pi += 1
                            self.fm_chunk(ps, w, 0, xT, xT_b, tb)
                            r = ri % 2; ri += 1
                            self.rope_post(ps, raw[r], t1[r], t2[r], cosT, sinT, tb, 64,
                                           kiT[:, tb * 512:(tb + 1) * 512], [kiT_b[tb]])
                        for i in range(NT):
                            ps = self.pb[pi % 4]; pi += 1

                            def f(ps=ps, i=i, w=w):
                                for k in range(NT):
                                    ins = nc.tensor.matmul(ps[:, 0:16], lhsT=xT[:, k, i * 128:(i + 1) * 128],
                                                           rhs=w[:, k, 128:144], start=(k == 0), stop=(k == NT - 1))
                                return ins
                            fw.op(fw.pe, f, [w, xT_b[i]], [ps])
                            fw.op(fw.act, lambda ps=ps, i=i: nc.scalar.mul(out=wi[:, i, :], in_=ps[:, 0:16], mul=1.0 / 32.0),
                                  [ps], [wi_b[i]])
                        fw.barrier()
                    self.xT_keep = (xT, xT_b)
                    fw.barrier()
                if self.stage("proj"):
                    return
                self.indexer(qiT, qiT_b, kiT, kiT_b, wi, wi_b, W, v, v_b)
                fw.barrier()
                if self.stage("idx"):
                    return
            self.attn_a(kT, kT_b, v, v_b, aoT, aoT_b)
            fw.barrier()
            if self.stage("attn"):
                return

    def indexer(self, qiT, qiT_b, kiT, kiT_b, wi, wi_b, W, v, v_b):
        nc, fw = self.nc, self.fw
        with ExitStack() as st:
            st.enter_context(nc.named_scope(self.tag + "idx"))
            score = [self.sb(st, "score%d" % i, [128, S], F32) for i in range(2)]
            work = self.sb(st, "work", [128, S], F32)
            m8 = self.sb(st, "m8", [128, 8], F32)
            brow = [self.sb(st, "brow%d" % i, [128, S], BF16) for i in range(2)]
            rl = [self.sb(st, "rl%d" % i, [128, 512], BF16) for i in range(4)]
            diag = [self.sb(st, "diag%d" % i, [128, 16, 128], BF16) for i in range(2)]
            bst = [self.sb(st, "bst%d" % i, [128, 4, 128], BF16) for i in range(2)]
            cnt = {"ri": 0, "pi": 0, "bi": 0, "ti": 0, "di": 0, "si": 0}
            xT, xT_b = self.xT_keep
            wb2 = [self.sb(st, "iwb%d" % i, [128, NT, 128], BF16) for i in range(2)]
            stg2 = [self.sb(st, "istg%d" % i, [128, 512], BF16) for i in range(2)]
            extra = []

            def g_unit(ch, tb):
                w = wb2[ch % 2]
                if tb == 0:
                    self.load_w(w, W[:, 3072 + ch * 128:3072 + (ch + 1) * 128], 128)
                ps = self.pb[cnt["di"] % 4]; cnt["di"] += 1
                self.fm_chunk(ps, w, 0, xT, xT_b, tb)
                sg = stg2[cnt["si"] % 2]; cnt["si"] += 1
                fw.op(fw.act, lambda: nc.scalar.activation(out=sg[:], in_=ps[:], func=AF.Silu), [ps], [sg])
                fw.dma(fw.sp, self.gT_b[ch][tb], sg, self.gT_s[ch, :, tb * 512:(tb + 1) * 512], sg[:])

            def v_unit(vg, tq):
                w = wb2[vg % 2]
                if tq == 0:
                    self.load_w(w, W[:, 2560 + vg * 128:2560 + (vg + 1) * 128], 128)
                ps = self.pb[cnt["di"] % 4]; cnt["di"] += 1

                def f():
                    for tt in range(4):
                        it_ = tq * 4 + tt
                        for k in range(NT):
                            ins = nc.tensor.matmul(ps[:, tt * 128:(tt + 1) * 128], lhsT=xT[:, k, it_ * 128:(it_ + 1) * 128],
                                                   rhs=w[:, k, 0:128], start=(k == 0), stop=(k == NT - 1))
                    return ins
                fw.op(fw.pe, f, [w] + xT_b[tq * 4:tq * 4 + 4], [ps])
                self.copy(fw.act, v[:, tq * 4:tq * 4 + 4, vg * 128:(vg + 1) * 128],
                          ps[:].rearrange("p (c t) -> p c t", c=4), [ps], v_b[tq * 4:tq * 4 + 4])

            for vg in range(4):
                for tq in range(4):
                    extra.append(lambda vg=vg, tq=tq: v_unit(vg, tq))
            for ch in range(16):
                for tb in range(4):
                    extra.append(lambda ch=ch, tb=tb: g_unit(ch, tb))

            def score_tile(i):
                L = (i + 1) * 128
                dg = diag[i % 2]
                sc = score[i % 2]
                for h in range(16):
                    fw.op(fw.act, lambda h=h: nc.scalar.mul(out=dg[:, h, :], in_=self.identf[:], mul=wi[:, i, h:h + 1]),
                          [self.identf, wi_b[i]], [dg])
                nkb = (L + 511) // 512
                seq = [(kb, h) for kb in range(nkb) for h in range(16)]
                pscs = {}
                inflight = {}

                def issue_dots(q):
                    kb, h = seq[q]
                    n = min(512, L - kb * 512)
                    ks = slice(kb * 512, kb * 512 + n)
                    c, half = h // 2, h % 2
                    pd = self.pb[cnt["di"] % 4]; cnt["di"] += 1
                    p0 = half * 64
                    fw.op(fw.pe, lambda: nc.tensor.matmul(
                        pd[:, 0:n], lhsT=qiT[p0:p0 + 64, c, i * 128:(i + 1) * 128], rhs=kiT[p0:p0 + 64, ks],
                        start=True, stop=True), [qiT_b[c][i // 4], kiT_b[kb]], [pd])
                    r = rl[cnt["ri"] % 4]; cnt["ri"] += 1
                    fw.op(fw.act, lambda: nc.scalar.activation(out=r[:, 0:n], in_=pd[:, 0:n], func=AF.Relu), [pd], [r])
                    inflight[q] = r

                def issue_acc(q):
                    kb, h = seq[q]
                    n = min(512, L - kb * 512)
                    ks = slice(kb * 512, kb * 512 + n)
                    if h == 0:
                        pscs[kb] = self.pb[4 + (cnt["pi"] % 2)]; cnt["pi"] += 1
                    psc = pscs[kb]
                    r = inflight.pop(q)
                    fw.op(fw.pe, lambda: nc.tensor.matmul(
                        psc[:, 0:n], lhsT=dg[:, h, :], rhs=r[:, 0:n], start=(h == 0), stop=(h == 15)),
                        [dg, r], [psc])
                    if h == 15:
                        self.copy(fw.act, sc[:, ks], psc[:, 0:n], [psc], [sc])

                LA = 3
                for q in range(min(LA, len(seq))):
                    issue_dots(q)
                for q in range(len(seq)):
                    if q + LA < len(seq):
                        issue_dots(q + LA)
                    issue_acc(q)
                fw.op(fw.pool, lambda: nc.gpsimd.affine_select(
                    out=sc[:, i * 128:(i + 1) * 128], in_=sc[:, i * 128:(i + 1) * 128], pattern=[[-1, 128]],
                    compare_op=ALU.is_ge, fill=self.reg_neg, base=0, channel_multiplier=1), [sc], [sc])

            def topk_tile(i):
                L = (i + 1) * 128
                sc = score[i % 2]
                br = brow[i % 2]
                if i >= 2:
                    cur = sc
                    for it in range(32):
                        fw.op(fw.dve, lambda: nc.vector.max(out=m8[:], in_=cur[:, 0:L]), [cur], [m8])
                        if it < 31:
                            fw.op(fw.dve, lambda: nc.vector.match_replace(
                                out=work[:, 0:L], in_to_replace=m8[:], in_values=cur[:, 0:L], imm_value=-1e30),
                                [cur, m8], [work])
                            cur = work
                    fw.op(fw.dve, lambda: nc.vector.tensor_scalar(
                        out=br[:, 0:L], in0=sc[:, 0:L], scalar1=m8[:, 7:8], scalar2=None, op0=ALU.is_ge),
                        [sc, m8], [br])
                else:
                    fw.op(fw.dve, lambda: nc.vector.tensor_scalar(
                        out=br[:, 0:L], in0=sc[:, 0:L], scalar1=-1e29, scalar2=None, op0=ALU.is_ge),
                        [sc], [br])
                for j4 in range(0, i + 1, 4):
                    nj = min(4, i + 1 - j4)
                    pt = self.ptb[cnt["ti"] % 2]; cnt["ti"] += 1

                    def f():
                        for jj in range(nj):
                            j = j4 + jj
                            ins = nc.tensor.transpose(pt[:, jj * 128:(jj + 1) * 128], br[:, j * 128:(j + 1) * 128],
                                                      self.ident[:])
                        return ins
                    fw.op(fw.pe, f, [br, self.ident], [pt])
                    b = bst[cnt["bi"] % 2]; cnt["bi"] += 1
                    self.copy(fw.act, b[:, 0:nj, :], pt[:, 0:nj * 128].rearrange("p (c t) -> p c t", c=nj), [pt], [b])
                    fw.dma(fw.sp, [self.biasT_b[j4 + jj][i] for jj in range(nj)], b,
                           self.biasT_s[j4:j4 + nj, :, i * 128:(i + 1) * 128].rearrange("j p t -> p j t"),
                           b[:, 0:nj, :])

            score_tile(0)
            for i in range(NT):
                if i + 1 < NT:
                    score_tile(i + 1)
                rem_w = sum(range(i + 1, NT + 1))
                ne = len(extra) if i == NT - 1 else min(len(extra), (len(extra) * (i + 1) + rem_w - 1) // rem_w)
                for _ in range(ne):
                    extra.pop(0)()
                topk_tile(i)
            assert not extra

    def attn_a(self, kT, kT_b, v, v_b, aoT, aoT_b):
        nc, fw = self.nc, self.fw
        with ExitStack() as st:
            st.enter_context(nc.named_scope(self.tag + "attnA"))
            qblk = self.sb(st, "qblk", [128, H, 512], BF16)
            gblk = self.sb(st, "gblk", [128, H, 512], BF16)
            bblk = self.sb(st, "bblk", [128, NT, 512], BF16)
            qblk_b = [Buf(qblk.t) for _ in range(H)]
            gblk_b = [Buf(gblk.t) for _ in range(H)]
            bblk_b = [Buf(bblk.t) for _ in range(NT)]
            pT = [self.sb(st, "pT%d" % i, [128, 512], BF16) for i in range(8)]
            rden = self.sb(st, "rden", [128, 512], F32)
            o = self.sb(st, "o", [128, 512], F32)
            pi = 0
            for b in range(4):
                qs = slice(b * 512, (b + 1) * 512)
                nj = 4 * b + 4
                for h in range(H):
                    fw.dma(fw.sp, qblk_b[h], self.qT_b[h][b], qblk[:, h, :], self.qT_s[h, :, qs])
                    fw.dma(fw.sp, gblk_b[h], self.gT_b[h][b], gblk[:, h, :], self.gT_s[h, :, qs])
                for j in range(nj):
                    t0 = max(0, j - 4 * b)
                    fw.dma(fw.sp, bblk_b[j], [self.biasT_b[j][i] for i in range(4 * b + t0, 4 * b + 4)],
                           bblk[:, j, t0 * 128:512], self.biasT_s[j, :, b * 512 + t0 * 128:(b + 1) * 512])
                for h in range(H):
                    g = h // 4
                    pout = self.pb[4]
                    pden = self.pb[5]

                    def issue_s(j):
                        q0 = max(0, j - 4 * b) * 128
                        ps = self.pb[j % 4]

                        def f():
                            return nc.tensor.matmul(ps[:, q0:512], lhsT=kT[:, g, j * 128:(j + 1) * 128],
                                                    rhs=qblk[:, h, q0:512], start=True, stop=True)
                        fw.op(fw.pe, f, [kT_b[g][j // 4], qblk_b[h]], [ps])
                        pt = pT[j % 8]
                        fw.op(fw.act, lambda: nc.scalar.activation(out=pt[:, q0:512], in_=ps[:, q0:512], func=AF.Exp,
                                                                   scale=SCALE), [ps], [pt])
                        fw.op(fw.dve, lambda: nc.vector.tensor_tensor(out=pt[:, q0:512], in0=pt[:, q0:512],
                                                                      in1=bblk[:, j, q0:512], op=ALU.mult),
                              [pt, bblk_b[j]], [pt])

                    def issue_av(j):
                        q0 = max(0, j - 4 * b) * 128
                        pt = pT[j % 8]
                        fw.op(fw.pe, lambda: nc.tensor.matmul(pout[:, q0:512], lhsT=v[:, j, g * 128:(g + 1) * 128],
                                                              rhs=pt[:, q0:512], start=(j == 0), stop=(j == nj - 1)),
                              [v_b[j], pt], [pout])
                        fw.op(fw.pe, lambda: nc.tensor.matmul(pden[:, q0:512], lhsT=self.ones_bf[:],
                                                              rhs=pt[:, q0:512], start=(j == 0), stop=(j == nj - 1)),
                              [self.ones_bf, pt], [pden])
                    LA = 3
                    for j in range(min(LA, nj)):
                        issue_s(j)
                    for j in range(nj):
                        if j + LA < nj:
                            issue_s(j + LA)
                        issue_av(j)
                    fw.op(fw.act, lambda: nc.scalar.activation(out=rden[:], in_=pden[:], func=AF.Ln), [pden], [rden])
                    fw.op(fw.act, lambda: nc.scalar.activation(out=rden[:], in_=rden[:], func=AF.Exp, scale=-1.0),
                          [rden], [rden])
                    fw.op(fw.dve, lambda: nc.vector.tensor_tensor(out=o[:], in0=pout[:], in1=rden[:], op=ALU.mult),
                          [pout, rden], [o])
                    fw.op(fw.pool, lambda h=h: nc.gpsimd.tensor_tensor(out=aoT[:, h, qs], in0=o[:], in1=gblk[:, h, :],
                                                                       op=ALU.mult),
                          [o, gblk_b[h]], aoT_b[h][4 * b:4 * b + 4])

    def proj_fm_store(self, W, col0, nchunks, xT, xT_b, wb, stg, dst_s, dst_b, act_func, gi0=0):
        nc, fw = self.nc, self.fw
        gi = gi0
        pi = 0
        si = 0
        for grp in range(nchunks // 2):
            w = wb[gi % 2]; gi += 1
            self.load_w(w, W[:, col0 + grp * 256:col0 + grp * 256 + 256], 256)
            for c in range(2):
                ch = grp * 2 + c
                for tb in range(4):
                    ps = self.pb[pi % 4]; pi += 1
                    self.fm_chunk(ps, w, c, xT, xT_b, tb)
                    sg = stg[si % 4]; si += 1
                    if act_func is None:
                        self.copy(self.evac_eng(), sg[:], ps[:], [ps], [sg])
                    else:
                        fw.op(fw.act, lambda ps=ps, sg=sg: nc.scalar.activation(out=sg[:], in_=ps[:], func=act_func),
                              [ps], [sg])
                    fw.dma(fw.sp, dst_b[ch][tb], sg, dst_s[ch, :, tb * 512:(tb + 1) * 512], sg[:])
        return gi

    def layer_b(self, s, l, src, src_b, aoT, aoT_b):
        nc, fw = self.nc, self.fw
        jl = l - 2
        with ExitStack() as st:
            xT, xT_b = self.phase_T(st, src, src_b)
            wb = [self.sb(st, "wb%d" % i, [128, NT, 256], BF16) for i in range(2)]
            stg = [self.sb(st, "stg%d" % i, [128, 512], BF16) for i in range(4)]
            st.enter_context(nc.named_scope(self.tag + "projB"))
            gi = 0
            if l == 2:
                gi = self.proj_fm_store(self.w_kv_b, 0, 16, xT, xT_b, wb, stg, self.kTb_s, self.kTb_b, None, gi)
                vst = self.sb(st, "vst", [128, NT, 256], BF16)
                pi = 0
                for grp in range(8):
                    w = wb[gi % 2]; gi += 1
                    self.load_w(w, self.w_kv_b[:, 2048 + grp * 256:2048 + grp * 256 + 256], 256)
                    for i in range(NT):
                        ps = self.pb[pi % 4]; pi += 1

                        def f(ps=ps, i=i, w=w):
                            for k in range(NT):
                                ins = nc.tensor.matmul(ps[:, 0:256], lhsT=xT[:, k, i * 128:(i + 1) * 128],
                                                       rhs=w[:, k, 0:256], start=(k == 0), stop=(k == NT - 1))
                            return ins
                        fw.op(fw.pe, f, [w, xT_b[i]], [ps])
                        self.copy(self.evac_eng(), vst[:, i, :], ps[:, 0:256], [ps], [vst])
                    for c in range(2):
                        hh = grp * 2 + c
                        fw.dma(fw.sp, self.vb_b[hh], vst, self.vb_s[hh], vst[:, :, c * 128:(c + 1) * 128])
            W = self.w_q_b[jl]
            gi = self.proj_fm_store(W, 0, 16, xT, xT_b, wb, stg, self.qT_s, self.qT_b, None, gi)
            gi = self.proj_fm_store(W, 2048, 16, xT, xT_b, wb, stg, self.gT_s, self.gT_b, AF.Silu, gi)
            fw.barrier()
        self.attn_b(aoT, aoT_b)
        fw.barrier()

    def attn_b(self, aoT, aoT_b):
        nc, fw = self.nc, self.fw
        with ExitStack() as st:
            st.enter_context(nc.named_scope(self.tag + "attnB"))
            kh = [self.sb(st, "kh%d" % i, [128, S], BF16) for i in range(2)]
            vh = [self.sb(st, "vh%d" % i, [128, NT, 128], BF16) for i in range(2)]
            qh = [self.sb(st, "qh%d" % i, [128, S], BF16) for i in range(2)]
            gh = [self.sb(st, "gh%d" % i, [128, S], BF16) for i in range(2)]
            Eb = [self.sb(st, "Eb%d" % i, [128, S], F32) for i in range(3)]
            Sb = [self.sb(st, "Sb%d" % i, [128, S], F32) for i in range(3)]
            Cb = [self.sb(st, "Cb%d" % i, [128, S + 1], F32) for i in range(2)]
            ntot = [self.sb(st, "ntot%d" % i, [128, 1], F32) for i in range(2)]
            ones = self.sb(st, "ones", [128, S], F32)
            Ab = [self.sb(st, "Ab%d" % i, [128, S], BF16) for i in range(2)]
            AT = [self.sb(st, "AT%d" % i, [128, NT, 128], BF16) for i in range(2)]
            fw.op(fw.dve, lambda: nc.vector.memset(ones[:], 1.0), [], [ones])
            for c in Cb:
                fw.op(fw.dve, lambda c=c: nc.vector.memset(c[:, 0:1], 0.0), [], [c])
            iters = [(h, i) for h in range(H) for i in range(NT)]
            N = len(iters)
            pz = self.pzt
            st8 = {"cc": 0, "ti": 0}

            def loadhead(h):
                k_, v_, q_, g_ = kh[h % 2], vh[h % 2], qh[h % 2], gh[h % 2]
                fw.dma(fw.sp, k_, self.kTb_b[h], k_[:], self.kTb_s[h])
                fw.dma(fw.sp, q_, self.qT_b[h], q_[:], self.qT_s[h])
                fw.dma(fw.sp, v_, self.vb_b[h], v_[:], self.vb_s[h])
                fw.dma(fw.sp, g_, self.gT_b[h], g_[:], self.gT_s[h])

            def stageA(n):
                h, i = iters[n]
                if n == 0:
                    loadhead(0)
                if i == 8 and h + 1 < H:
                    loadhead(h + 1)
                k_, q_ = kh[h % 2], qh[h % 2]
                L = (i + 1) * 128
                e, sp = Eb[n % 3], Sb[n % 3]
                for c0 in range(0, L, 1024):
                    nn = min(1024, L - c0)
                    cc = st8["cc"]; st8["cc"] += 1
                    base = (cc % 2) * 1024
                    zb = self.pb[2 * (cc % 2):2 * (cc % 2) + (nn + 511) // 512]

                    def fz():
                        for s0 in range(0, nn, 512):
                            m = min(512, nn - s0)
                            ins = nc.tensor.matmul(pz[:, base + s0:base + s0 + m], lhsT=q_[:, i * 128:(i + 1) * 128],
                                                   rhs=k_[:, c0 + s0:c0 + s0 + m], start=True, stop=True)
                        return ins
                    fw.op(fw.pe, fz, [q_, k_], zb)
                    fw.op(fw.act, lambda: nc.scalar.activation(out=e[:, c0:c0 + nn], in_=pz[:, base:base + nn],
                                                               func=AF.Exp, scale=SCALE), zb, [e])
                fw.op(fw.act, lambda: nc.scalar.activation(out=sp[:, 0:L], in_=e[:, 0:L], func=AF.Ln,
                                                           bias=self.cst[:, 3:4]), [e, self.cst], [sp])
                d0 = i * 128
                fw.op(fw.pool, lambda: nc.gpsimd.affine_select(
                    out=sp[:, d0:d0 + 128], in_=sp[:, d0:d0 + 128], pattern=[[-1, 128]], compare_op=ALU.is_gt,
                    fill=self.reg_zero, base=0, channel_multiplier=1), [sp], [sp])

            def stageB(n):
                h, i = iters[n]
                L = (i + 1) * 128
                sp, cb, nt = Sb[n % 3], Cb[n % 2], ntot[n % 2]
                fw.op(fw.dve, lambda: nc.vector.tensor_tensor_scan(
                    out=cb[:, 1:L + 1], data0=ones[:, 0:L], data1=sp[:, 0:L], initial=0.0, op0=ALU.mult, op1=ALU.add),
                    [ones, sp], [cb])
                fw.op(fw.dve, lambda: nc.vector.tensor_scalar(out=nt[:], in0=cb[:, L:L + 1], scalar1=-1.0,
                                                              scalar2=None, op0=ALU.mult), [cb], [nt])
                fw.op(fw.act, lambda: nc.scalar.activation(out=sp[:, 0:L], in_=cb[:, 0:L], func=AF.Exp,
                                                           bias=nt[:, 0:1]), [cb, nt], [sp])

            def stageB2(n):
                h, i = iters[n]
                L = (i + 1) * 128
                e, sp, A = Eb[n % 3], Sb[n % 3], Ab[n % 2]
                fw.op(fw.dve, lambda: nc.vector.tensor_tensor(out=A[:, 0:L], in0=e[:, 0:L], in1=sp[:, 0:L], op=ALU.mult),
                      [e, sp], [A])
                d0 = i * 128
                fw.op(fw.pool, lambda: nc.gpsimd.affine_select(
                    out=A[:, d0:d0 + 128], in_=A[:, d0:d0 + 128], pattern=[[-1, 128]], compare_op=ALU.is_gt,
                    fill=self.reg_zero, base=0, channel_multiplier=1), [A], [A])

            def stageC(n):
                h, i = iters[n]
                A, at = Ab[n % 2], AT[n % 2]
                for j4 in range(0, i + 1, 4):
                    nj = min(4, i + 1 - j4)
                    pt = self.ptb[st8["ti"] % 2]; st8["ti"] += 1

                    def f():
                        for jj in range(nj):
                            j = j4 + jj
                            ins = nc.tensor.transpose(pt[:, jj * 128:(jj + 1) * 128], A[:, j * 128:(j + 1) * 128],
                                                      self.ident[:])
                        return ins
                    fw.op(fw.pe, f, [A, self.ident], [pt])
                    self.copy(fw.act if (st8["ti"] % 3 == 0) else fw.dve, at[:, j4:j4 + nj, :],
                              pt[:, 0:nj * 128].rearrange("p (c t) -> p c t", c=nj), [pt], [at])

            def stageC2(n):
                h, i = iters[n]
                v_, g_ = vh[h % 2], gh[h % 2]
                at = AT[n % 2]
                po = self.pb[4 + (n % 2)]

                def fo():
                    for j in range(i + 1):
                        ins = nc.tensor.matmul(po[:, 0:128], lhsT=v_[:, j, :], rhs=at[:, j, :],
                                               start=(j == 0), stop=(j == i))
                    return ins
                fw.op(fw.pe, fo, [v_, at], [po])
                fw.op(fw.dve, lambda: nc.vector.tensor_tensor(out=aoT[:, h, i * 128:(i + 1) * 128], in0=po[:, 0:128],
                                                              in1=g_[:, i * 128:(i + 1) * 128], op=ALU.mult),
                      [po, g_], [aoT_b[h][i]])

            for t in range(N + 4):
                if t < N:
                    stageA(t)
                if 0 <= t - 1 < N:
                    stageB(t - 1)
                if 0 <= t - 2 < N:
                    stageB2(t - 2)
                if 0 <= t - 3 < N:
                    stageC(t - 3)
                if 0 <= t - 4 < N:
                    stageC2(t - 4)

    def epilogue(self, s, l, src, src_b, dst, dst_b, aoT, aoT_b, w_out):
        nc, fw = self.nc, self.fw
        with ExitStack() as st:
            st.enter_context(nc.named_scope(self.tag + "epi"))
            wb = [self.sb(st, "ewb%d" % i, [128, NT, 512], BF16) for i in range(2)]
            wp = [self.sb(st, "ewp%d" % i, [128, 2, 512], BF16) for i in range(2)]
            z = [self.sb(st, "z%d" % i, [128, D], F32) for i in range(4)]
            gB = self.sb(st, "gB", [128, D], F32)
            bB = self.sb(st, "bB", [128, D], F32)
            xlnT = self.sb(st, "xlnT", [128, NT, 512], BF16)
            xlnT_b = [Buf(xlnT.t) for _ in range(4)]
            pT = self.sb(st, "ppT", [128, 2, 512], BF16)
            pT_b = [Buf(pT.t) for _ in range(4)]
            pbf = [self.sb(st, "pbf%d" % i, [128, 256], BF16) for i in range(4)]
            xb = [self.sb(st, "exb%d" % i, [128, D], BF16) for i in range(2)]
            sig = [self.sb(st, "sig%d" % i, [128, 512], F32) for i in range(2)]
            tmp = [self.sb(st, "tmp%d" % i, [128, 512], F32) for i in range(2)]
            stats = self.sb(st, "stats", [128, 4, 4, 6], F32)
            mv = self.sb(st, "mv", [128, 4, 2], F32)
            rstd = self.sb(st, "rstd", [128, 4, 1], F32)
            nbias = self.sb(st, "nbias", [128, 1], F32)
            fw.dma(fw.sp, gB, self.wdram, gB[:], self.ln_g[l:l + 1, :].partition_broadcast(128))
            fw.dma(fw.sp, bB, self.wdram, bB[:], self.ln_b[l:l + 1, :].partition_broadcast(128))
            gi = 0
            pi = 0
            ti = 0
            for b in range(4):
                if b == 0:
                    for t in range(4):
                        fw.dma(fw.sp, z[t], src_b[t], z[t][:], src[t * 128:(t + 1) * 128, :])
                for t in range(4):
                    i = 4 * b + t
                    pb_ = pbf[t]
                    fw.dma(fw.pool, pb_, self.wdram, pb_[:], self.p_d[l, s, i * 128:(i + 1) * 128, :])
                    pt = self.ptb[ti % 2]; ti += 1

                    def f2(pt=pt, pb_=pb_):
                        for c in range(2):
                            ins = nc.tensor.transpose(pt[:, c * 128:(c + 1) * 128], pb_[:, c * 128:(c + 1) * 128],
                                                      self.ident[:])
                        return ins
                    fw.op(fw.pe, f2, [pb_, self.ident], [pt])
                    self.copy(fw.act, pT[:, :, t * 128:(t + 1) * 128],
                              pt[:, 0:256].rearrange("p (c t) -> p c t", c=2), [pt], [pT_b[t]])
                for n in range(4):
                    w = wb[gi % 2]; gi += 1
                    ns_ = slice(n * 512, (n + 1) * 512)
                    self.load_w(w, w_out[:, ns_], 512)
                    for t in range(4):
                        i = 4 * b + t
                        ps = self.pb[pi % 4]; pi += 1

                        def f(ps=ps, w=w, i=i):
                            for k in range(NT):
                                ins = nc.tensor.matmul(ps[:], lhsT=aoT[:, k, i * 128:(i + 1) * 128], rhs=w[:, k, :],
                                                       start=(k == 0), stop=(k == NT - 1))
                            return ins
                        fw.op(fw.pe, f, [w] + [aoT_b[k][i] for k in range(H)], [ps])
                        fw.op(fw.dve, lambda ps=ps, t=t, ns_=ns_: nc.vector.scalar_tensor_tensor(
                            out=z[t][:, ns_], in0=z[t][:, ns_], scalar=ALPHA, in1=ps[:], op0=ALU.mult, op1=ALU.add),
                            [z[t], ps], [z[t]])
                        fw.op(fw.dve, lambda t=t, n=n, ns_=ns_: nc.vector.bn_stats(out=stats[:, t, n, :], in_=z[t][:, ns_]),
                              [z[t]], [stats])
                w3 = wb[gi % 2]; gi += 1
                self.load_w(w3, self.w_gate[l][:, 0:512], 512)
                self.load_w(wp[0], self.w_ple[l][:, 0:512], 512)
                for t in range(4):
                    fw.op(fw.dve, lambda t=t: nc.vector.bn_aggr(out=mv[:, t, :], in_=stats[:, t].rearrange("p a b -> p (a b)")),
                          [stats], [mv])
                fw.op(fw.dve, lambda: nc.vector.tensor_scalar(out=rstd[:], in0=mv[:, :, 1:2], scalar1=1e-5, scalar2=None,
                                                              op0=ALU.add), [mv], [rstd])
                fw.op(fw.act, lambda: nc.scalar.sqrt(out=rstd[:], in_=rstd[:]), [rstd], [rstd])
                fw.op(fw.dve, lambda: nc.vector.reciprocal(out=rstd[:], in_=rstd[:]), [rstd], [rstd])
                for t in range(4):
                    i = 4 * b + t
                    zt = z[t]
                    fw.op(fw.dve, lambda zt=zt, t=t: nc.vector.scalar_tensor_tensor(
                        out=zt[:], in0=zt[:], scalar=mv[:, t, 0:1], in1=gB[:], op0=ALU.subtract, op1=ALU.mult),
                        [zt, mv, gB], [zt])
                    fw.op(fw.dve, lambda zt=zt, t=t: nc.vector.scalar_tensor_tensor(
                        out=zt[:], in0=zt[:], scalar=rstd[:, t, 0:1], in1=bB[:], op0=ALU.mult, op1=ALU.add),
                        [zt, rstd, bB], [zt])
                    x_ = xb[t % 2]
                    self.copy(fw.act, x_[:], zt[:], [zt], [x_])
                    for c4 in range(4):
                        pt = self.ptb[ti % 2]; ti += 1

                        def f(pt=pt, x_=x_, c4=c4):
                            for c in range(4):
                                k = c4 * 4 + c
                                ins = nc.tensor.transpose(pt[:, c * 128:(c + 1) * 128], x_[:, k * 128:(k + 1) * 128],
                                                          self.ident[:])
                            return ins
                        fw.op(fw.pe, f, [x_, self.ident], [pt])
                        self.copy(self.evac_eng(), xlnT[:, c4 * 4:(c4 + 1) * 4, t * 128:(t + 1) * 128],
                                  pt[:].rearrange("p (c t) -> p c t", c=4), [pt], [xlnT_b[t]])
                for n in range(4):
                    ns_ = slice(n * 512, (n + 1) * 512)
                    w2 = wp[n % 2]
                    if n == 0:
                        w = w3
                    else:
                        w = wb[gi % 2]; gi += 1
                        self.load_w(w, self.w_gate[l][:, ns_], 512)
                        self.load_w(w2, self.w_ple[l][:, ns_], 512)
                    for t in range(4):
                        zt = z[t]
                        ps = self.pb[pi % 4]; pi += 1
                        ps2 = self.pb[4 + (pi % 2)]

                        def f(ps=ps, w=w, t=t):
                            for k in range(NT):
                                ins = nc.tensor.matmul(ps[:], lhsT=xlnT[:, k, t * 128:(t + 1) * 128], rhs=w[:, k, :],
                                                       start=(k == 0), stop=(k == NT - 1))
                            return ins
                        fw.op(fw.pe, f, [w, xlnT_b[t]], [ps])

                        def f3(ps2=ps2, w2=w2, t=t):
                            for k in range(2):
                                ins = nc.tensor.matmul(ps2[:], lhsT=pT[:, k, t * 128:(t + 1) * 128], rhs=w2[:, k, :],
                                                       start=(k == 0), stop=(k == 1))
                            return ins
                        fw.op(fw.pe, f3, [w2, pT_b[t]], [ps2])
                        sg = sig[(n * 4 + t) % 2]
                        tm = tmp[(n * 4 + t) % 2]
                        fw.op(fw.act, lambda ps=ps, sg=sg: nc.scalar.activation(out=sg[:], in_=ps[:], func=AF.Sigmoid),
                              [ps], [sg])
                        fw.op(fw.dve, lambda ps2=ps2, sg=sg, tm=tm: nc.vector.tensor_tensor(out=tm[:], in0=ps2[:], in1=sg[:],
                                                                                            op=ALU.mult), [ps2, sg], [tm])
                        fw.op(fw.dve, lambda zt=zt, tm=tm, ns_=ns_: nc.vector.tensor_tensor(out=zt[:, ns_], in0=zt[:, ns_],
                                                                                            in1=tm[:], op=ALU.add),
                              [zt, tm], [zt])
                for t in range(4):
                    i = 4 * b + t
                    fw.dma(fw.sp, dst_b[i], z[t], dst[i * 128:(i + 1) * 128, :], z[t][:])
                    if b < 3:
                        i2 = i + 4
                        fw.dma(fw.sp, z[t], src_b[i2], z[t][:], src[i2 * 128:(i2 + 1) * 128, :])


_CACHE = {}


def _inv_freq_table():
    p = np.arange(128)
    c0 = (np.float32(10000.0) ** (-(p % 64).astype(np.float32) / np.float32(64))).astype(np.float32)
    c1 = (np.float32(10000.0) ** (-(p % 32).astype(np.float32) / np.float32(32))).astype(np.float32)
    return np.ascontiguousarray(np.stack([c0, c1], axis=1).astype(np.float32))


def kernel(x, p, positions, w_in_a, w_out_a, w_q_b, w_kv_b, w_out_b, ln_g, ln_b, w_ple, w_ple_gate):
    n = 8
    ns = 2
    if "nc" not in _CACHE:
        _CACHE["nc"] = Prog(nseq=ns, nlayers=4).build()
    nc = _CACHE["nc"]
    f32 = lambda a: np.ascontiguousarray(np.asarray(a), dtype=np.float32)
    x = f32(x); p = f32(p)
    pos = np.ascontiguousarray(np.asarray(positions), dtype=np.int32)
    shared = {"w_in_a": f32(w_in_a), "w_out_a": f32(w_out_a), "w_q_b": f32(w_q_b), "w_kv_b": f32(w_kv_b),
              "w_out_b": f32(w_out_b), "ln_g": f32(ln_g), "ln_b": f32(ln_b), "w_ple": f32(w_ple),
              "w_ple_gate": f32(w_ple_gate), "invf": _inv_freq_table()}
    in_maps = []
    for c in range(n):
        m = dict(shared)
        m["x"] = np.ascontiguousarray(x[c * ns:(c + 1) * ns])
        m["p"] = np.ascontiguousarray(p[:, c * ns:(c + 1) * ns])
        m["pos"] = np.ascontiguousarray(pos[c * ns:(c + 1) * ns])
        in_maps.append(m)
    res = run_bass_kernel_spmd(nc, in_maps, core_ids=list(range(n)))
    return np.concatenate([np.asarray(r["out"], dtype=np.float32) for r in res.results], axis=0)
```

```python
import math
from contextlib import ExitStack

import numpy as np
import concourse.bass as bass
import concourse.mybir as mybir
from concourse.bass_utils import run_bass_kernel_spmd

F32 = mybir.dt.float32
BF16 = mybir.dt.bfloat16
I32 = mybir.dt.int32
AF = mybir.ActivationFunctionType
ALU = mybir.AluOpType

S = 2048
D = 2048
NT = 16
H = 16
DH = 128
A_IN = 6224
ALPHA = float((2 * 4) ** 0.25)
SCALE = float(DH ** -0.5)
NEG = -30000.0
PI = math.pi


class Buf:
    __slots__ = ("t", "w", "r", "name")

    def __init__(self, t, name=""):
        self.t = t
        self.w = None
        self.r = {}
        self.name = name

    def __getitem__(self, idx):
        return self.t[idx]


class Eng:
    def __init__(self, fw, name, eng, sem, self_sync=True):
        self.fw = fw
        self.name = name
        self.eng = eng
        self.sem = sem
        self.count = 0
        self.seen = {}
        self.self_sync = self_sync

    def wait(self, tok):
        if tok is None:
            return
        key, val = tok
        if val <= 0:
            return
        if key == self.name and not self.self_sync:
            return
        if self.seen.get(key, 0) >= val:
            return
        self.seen[key] = val
        self.eng.wait_ge(self.fw.sems[key], val)
        self.fw.nwaits += 1


class FW:
    def __init__(self, nc, ndma=32):
        self.nc = nc
        self.sems = {}
        self.nwaits = 0
        self.nops = 0
        self.engs = {}
        self.dma_keys = []
        self.dma_val = {}
        self.dma_rr = 0
        self.ndma = ndma

    def setup(self, stack):
        nc = self.nc
        for name, eng, ss in (("pe", nc.tensor, False), ("act", nc.scalar, True),
                              ("dve", nc.vector, True), ("pool", nc.gpsimd, True),
                              ("sp", nc.sync, False)):
            sem = stack.enter_context(nc.semaphore("s_" + name))
            self.sems[name] = sem
            self.engs[name] = Eng(self, name, eng, sem, ss)
        self.rings = {"sp": [], "pool": []}
        self.ring_rr = {"sp": 0, "pool": 0}
        for qn, pre in (("sp", "d"), ("pool", "g")):
            for i in range(self.ndma // 2):
                k = "%s%d" % (pre, i)
                self.sems[k] = stack.enter_context(nc.semaphore("s_" + k))
                self.dma_keys.append(k)
                self.rings[qn].append(k)
                self.dma_val[k] = 0
        self.pe = self.engs["pe"]
        self.act = self.engs["act"]
        self.dve = self.engs["dve"]
        self.pool = self.engs["pool"]
        self.sp = self.engs["sp"]

    def _deps(self, E, reads, writes):
        for t in reads:
            E.wait(t.w)
        for t in writes:
            E.wait(t.w)
            for k, v in t.r.items():
                E.wait((k, v))

    def op(self, E, fn, reads, writes):
        self._deps(E, reads, writes)
        ins = fn()
        self.nops += 1
        E.count += 1
        ins.then_inc(E.sem, 1)
        tok = (E.name, E.count)
        for t in reads:
            if t.r.get(E.name, 0) < E.count:
                t.r[E.name] = E.count
        for t in writes:
            t.w = tok
            t.r = {}
        return tok

    def dma(self, Q, out_b, in_b, out_ap, in_ap, **kw):
        outs = out_b if isinstance(out_b, (list, tuple)) else [out_b]
        ins_ = in_b if isinstance(in_b, (list, tuple)) else [in_b]
        self._deps(Q, ins_, outs)
        ring = self.rings[Q.name]
        k = ring[self.ring_rr[Q.name]]
        self.ring_rr[Q.name] = (self.ring_rr[Q.name] + 1) % len(ring)
        Q.wait((k, self.dma_val[k]))
        ins = Q.eng.dma_start(out=out_ap, in_=in_ap, **kw)
        self.dma_val[k] += 16
        ins.then_inc(self.sems[k], 16)
        tok = (k, self.dma_val[k])
        for b in ins_:
            b.r[k] = self.dma_val[k]
        for b in outs:
            b.w = tok
            b.r = {}
        self.nops += 1
        return tok

    def barrier(self):
        for E in self.engs.values():
            for E2 in self.engs.values():
                if E2 is not E:
                    E.wait((E2.name, E2.count))
            for k in self.dma_keys:
                E.wait((k, self.dma_val[k]))

    def finish(self, bufs):
        for b in bufs:
            self.sp.wait(b.w)


class StopBuild(Exception):
    pass


class Prog:
    def stage(self, name):
        if self.stop == name:
            self.stopped = True
        return self.stopped

    def __init__(self, nseq=2, nlayers=4, debug=False, stop=None):
        self.stop = stop
        self.stopped = False
        self.nseq = nseq
        self.nlayers = nlayers
        self.debug = debug
        self.nc = bass.Bass("TRN2", target_bir_lowering=False)
        self.rr = 0

    def sb(self, st, name, shape, dt):
        self.uid += 1
        return Buf(st.enter_context(self.nc.sbuf_tensor("%s_%d" % (name, self.uid), shape, dt)), name)

    def din(self, name, shape, dt):
        return self.nc.dram_tensor(name, shape, dt, kind="ExternalInput").ap()

    def dscr(self, name, shape, dt):
        kind = "ExternalOutput" if self.debug else "Internal"
        return self.nc.dram_tensor(name, shape, dt, kind=kind).ap()

    def evac_eng(self):
        self.rr += 1
        return self.fw.act if (self.rr & 1) else self.fw.dve

    def copy(self, E, out_ap, in_ap, reads, writes):
        nc = self.nc
        if E is self.fw.act:
            return self.fw.op(E, lambda: nc.scalar.copy(out=out_ap, in_=in_ap), reads, writes)
        elif E is self.fw.dve:
            return self.fw.op(E, lambda: nc.vector.tensor_copy(out=out_ap, in_=in_ap), reads, writes)
        else:
            return self.fw.op(E, lambda: nc.gpsimd.tensor_copy(out=out_ap, in_=in_ap), reads, writes)

    def load_w(self, wb, src2d, ncols):
        kc = src2d.shape[0] // 128
        self.fw.dma(self.fw.pool, wb, self.wdram, wb[:, 0:kc, 0:ncols],
                    src2d.rearrange("(kc p) n -> p kc n", p=128))

    def build(self):
        nc = self.nc
        ns = self.nseq
        self.uid = 0
        self.x_d = self.din("x", [ns, S, D], F32)
        self.p_d = self.din("p", [4, ns, S, 256], F32)
        self.pos_d = self.din("pos", [ns, S], I32)
        self.w_in_a = self.din("w_in_a", [2, D, A_IN], F32)
        self.w_out_a = self.din("w_out_a", [2, D, D], F32)
        self.w_q_b = self.din("w_q_b", [2, D, 2 * D], F32)
        self.w_kv_b = self.din("w_kv_b", [D, 2 * D], F32)
        self.w_out_b = self.din("w_out_b", [2, D, D], F32)
        self.ln_g = self.din("ln_g", [4, D], F32)
        self.ln_b = self.din("ln_b", [4, D], F32)
        self.w_ple = self.din("w_ple", [4, 256, D], F32)
        self.w_gate = self.din("w_ple_gate", [4, D, D], F32)
        self.invf_d = self.din("invf", [128, 2], F32)
        self.out_d = nc.dram_tensor("out", [ns, S, D], F32, kind="ExternalOutput").ap()
        self.xres = [self.dscr("xres%d" % i, [S, D], F32) for i in range(2)]
        self.qT_s = self.dscr("qT_s", [H, 128, S], BF16)
        self.gT_s = self.dscr("gT_s", [H, 128, S], BF16)
        self.biasT_s = self.dscr("biasT_s", [NT, 128, S], BF16)
        self.kTb_s = self.dscr("kTb_s", [H, 128, S], BF16)
        self.vb_s = self.dscr("vb_s", [H, 128, NT, 128], BF16)
        self.wdram = Buf(None, "weights")
        self.xin_b = Buf(None, "xin")
        self.xres_b = [[Buf(None) for _ in range(NT)] for _ in range(2)]
        self.out_b = [[Buf(None) for _ in range(NT)] for _ in range(ns)]
        self.qT_b = [[Buf(None) for _ in range(4)] for _ in range(H)]
        self.gT_b = [[Buf(None) for _ in range(4)] for _ in range(H)]
        self.biasT_b = [[Buf(None) for _ in range(NT)] for _ in range(NT)]
        self.kTb_b = [[Buf(None) for _ in range(4)] for _ in range(H)]
        self.vb_b = [Buf(None) for _ in range(H)]

        with ExitStack() as st:
            self.fw = fw = FW(nc)
            fw.setup(st)
            pzt = st.enter_context(nc.psum_tensor("pz", [128, 2048], F32))
            self.pzt = pzt
            self.pb = [Buf(pzt[:, i * 512:(i + 1) * 512], "pz%d" % i) for i in range(4)]
            for i in range(2):
                t = st.enter_context(nc.psum_tensor("pb%d" % (4 + i), [128, 512], F32))
                self.pb.append(Buf(t, "pb%d" % (4 + i)))
            self.ptb = [Buf(st.enter_context(nc.psum_tensor("ptb%d" % i, [128, 512], BF16)), "ptb%d" % i)
                        for i in range(2)]
            self.identf = self.sb(st, "identf", [128, 128], F32)
            self.ident = self.sb(st, "ident", [128, 128], BF16)
            self.ones_bf = self.sb(st, "ones_bf", [128, 128], BF16)
            self.invf = self.sb(st, "invf", [128, 2], F32)
            self.cst = self.sb(st, "cst", [128, 4], F32)
            self.big = self.sb(st, "big", [128, NT, S], BF16)
            idf = self.identf
            self.reg_neg = nc.gpsimd.to_reg(-1e30)
            self.reg_zero = nc.gpsimd.to_reg(0.0)
            fw.op(fw.pool, lambda: nc.gpsimd.memset(idf[:], 1.0), [], [idf])
            fw.op(fw.pool, lambda: nc.gpsimd.affine_select(out=idf[:], in_=idf[:], pattern=[[-1, 128]],
                                                           compare_op=ALU.is_equal, fill=0.0, base=0,
                                                           channel_multiplier=1), [idf], [idf])
            self.copy(fw.dve, self.ident[:], idf[:], [idf], [self.ident])
            fw.op(fw.dve, lambda: nc.vector.memset(self.ones_bf[:], 1.0), [], [self.ones_bf])
            fw.op(fw.dve, lambda: nc.vector.memset(self.cst[:, 0:1], PI), [], [self.cst])
            fw.op(fw.dve, lambda: nc.vector.memset(self.cst[:, 1:2], -PI), [], [self.cst])
            fw.op(fw.dve, lambda: nc.vector.memset(self.cst[:, 2:3], 0.0), [], [self.cst])
            fw.op(fw.dve, lambda: nc.vector.memset(self.cst[:, 3:4], 1.0), [], [self.cst])
            fw.dma(fw.sp, self.invf, self.wdram, self.invf[:], self.invf_d[:, :])

            for s in range(ns):
                if not self.stopped:
                    self.run_seq(s)
            outs = [b for row in self.out_b for b in row]
            if self.nlayers < 4:
                outs += [b for row in self.xres_b for b in row]
            fw.barrier()
            fw.finish(outs)
        return nc

    def run_seq(self, s):
        fw = self.fw
        for l in range(self.nlayers):
            self.tag = "s%dL%d_" % (s, l)
            if self.stopped:
                return
            if l == 0:
                src = self.x_d[s]
                src_b = [self.xin_b] * NT
            else:
                src = self.xres[(l - 1) % 2]
                src_b = self.xres_b[(l - 1) % 2]
            if l == 3:
                dst = self.out_d[s]
                dst_b = self.out_b[s]
            else:
                dst = self.xres[l % 2]
                dst_b = self.xres_b[l % 2]
            if True:
                aoT = self.big
                aoT_b = [[Buf(aoT.t) for _ in range(NT)] for _ in range(H)]
                if l < 2:
                    self.layer_a(s, l, src, src_b, aoT, aoT_b)
                else:
                    self.layer_b(s, l, src, src_b, aoT, aoT_b)
                fw.barrier()
                if self.stopped:
                    return
                w_out = self.w_out_a[l] if l < 2 else self.w_out_b[l - 2]
                self.epilogue(s, l, src, src_b, dst, dst_b, aoT, aoT_b, w_out)
                fw.barrier()

    def phase_T(self, st, src, src_b):
        nc, fw = self.nc, self.fw
        xT = self.big
        xT_b = [Buf(xT.t) for _ in range(NT)]
        with ExitStack() as st2:
            st2.enter_context(self.nc.named_scope(self.tag + "T"))
            xb = [self.sb(st2, "T_xb%d" % i, [128, D], BF16) for i in range(2)]
            for i in range(NT):
                b = xb[i % 2]
                fw.dma(fw.pool, b, src_b[i], b[:], src[i * 128:(i + 1) * 128, :])
                for c4 in range(4):
                    pt = self.ptb[(i * 4 + c4) % 2]

                    def f(pt=pt, b=b, c4=c4):
                        for c in range(4):
                            k = c4 * 4 + c
                            ins = nc.tensor.transpose(pt[:, c * 128:(c + 1) * 128], b[:, k * 128:(k + 1) * 128],
                                                      self.ident[:])
                        return ins
                    fw.op(fw.pe, f, [b, self.ident], [pt])
                    self.copy(self.evac_eng(), xT[:, c4 * 4:(c4 + 1) * 4, i * 128:(i + 1) * 128],
                              pt[:].rearrange("p (c t) -> p c t", c=4), [pt], [xT_b[i]])
            fw.barrier()
        self.stage("T")
        return xT, xT_b

    def rope_tables(self, st, s, col, period):
        nc, fw = self.nc, self.fw
        cosT = self.sb(st, "cosT", [128, S], F32)
        sinT = self.sb(st, "sinT", [128, S], F32)
        with ExitStack() as st2:
            posi = self.sb(st2, "posi", [128, S], I32)
            ang = self.sb(st2, "ang", [128, S], F32)
            fw.dma(fw.sp, posi, self.wdram, posi[:], self.pos_d[s:s + 1, :].partition_broadcast(128))
            self.copy(fw.dve, ang[:], posi[:], [posi], [ang])
            fw.op(fw.dve, lambda: nc.vector.tensor_scalar(out=ang[:], in0=ang[:], scalar1=self.invf[:, col:col + 1],
                                                          scalar2=None, op0=ALU.mult), [ang, self.invf], [ang])
            tmp = posi
            tmpf = tmp[:].bitcast(F32)
            C1 = 6.28125
            C2 = 2 * PI - 6.28125
            TS = nc.vector.tensor_scalar
            STT = nc.vector.scalar_tensor_tensor
            fw.op(fw.dve, lambda: TS(out=posi[:], in0=ang[:], scalar1=1.0 / (2 * PI), scalar2=None, op0=ALU.mult),
                  [ang], [posi])
            self.copy(fw.dve, cosT[:], posi[:], [posi], [cosT])
            fw.op(fw.dve, lambda: STT(out=ang[:], in0=cosT[:], scalar=-C1, in1=ang[:], op0=ALU.mult, op1=ALU.add),
                  [cosT, ang], [ang])
            fw.op(fw.dve, lambda: STT(out=ang[:], in0=cosT[:], scalar=-C2, in1=ang[:], op0=ALU.mult, op1=ALU.add),
                  [cosT, ang], [ang])
            fw.op(fw.dve, lambda: TS(out=sinT[:], in0=ang[:], scalar1=PI, scalar2=-2 * PI, op0=ALU.is_gt, op1=ALU.mult),
                  [ang], [sinT])
            fw.op(fw.dve, lambda: nc.vector.tensor_tensor(out=ang[:], in0=ang[:], in1=sinT[:], op=ALU.add),
                  [ang, sinT], [ang])
            fw.op(fw.dve, lambda: TS(out=sinT[:], in0=ang[:], scalar1=-PI, scalar2=2 * PI, op0=ALU.is_lt, op1=ALU.mult),
                  [ang], [sinT])
            fw.op(fw.dve, lambda: nc.vector.tensor_tensor(out=ang[:], in0=ang[:], in1=sinT[:], op=ALU.add),
                  [ang, sinT], [ang])
            fw.op(fw.dve, lambda: TS(out=ang[:], in0=ang[:], scalar1=-PI, scalar2=PI, op0=ALU.max, op1=ALU.min),
                  [ang], [ang])
            fw.op(fw.dve, lambda: TS(out=tmpf, in0=ang[:], scalar1=0.5 * PI, scalar2=None, op0=ALU.add), [ang], [tmp])
            fw.op(fw.dve, lambda: TS(out=cosT[:], in0=tmpf, scalar1=PI, scalar2=-2 * PI, op0=ALU.is_gt, op1=ALU.mult),
                  [tmp], [cosT])
            fw.op(fw.dve, lambda: nc.vector.tensor_tensor(out=tmpf, in0=tmpf, in1=cosT[:], op=ALU.add), [tmp, cosT], [tmp])
            fw.op(fw.dve, lambda: TS(out=tmpf, in0=tmpf, scalar1=-PI, scalar2=PI, op0=ALU.max, op1=ALU.min), [tmp], [tmp])
            half = period // 2
            for g0 in range(0, 128, period):
                fw.op(fw.act, lambda g0=g0: nc.scalar.activation(out=sinT[g0:g0 + half, :], in_=ang[g0:g0 + half, :],
                                                                 func=AF.Sin, scale=1.0), [ang], [sinT])
                fw.op(fw.act, lambda g0=g0: nc.scalar.activation(out=sinT[g0 + half:g0 + period, :],
                                                                 in_=ang[g0 + half:g0 + period, :],
                                                                 func=AF.Sin, scale=-1.0), [ang], [sinT])
            fw.op(fw.act, lambda: nc.scalar.activation(out=cosT[:], in_=tmpf, func=AF.Sin, scale=1.0), [tmp], [cosT])
        return cosT, sinT

    def rope_post(self, ps, raw, t1, t2, cosT, sinT, tb, period, out_ap, out_bufs):
        nc, fw = self.nc, self.fw
        cs = slice(tb * 512, (tb + 1) * 512)
        self.copy(fw.act, raw[:], ps[:], [ps], [raw])
        fw.op(fw.dve, lambda: nc.vector.tensor_tensor(out=t1[:], in0=raw[:], in1=cosT[:, cs], op=ALU.mult),
              [raw, cosT], [t1])
        half = period // 2
        for g0 in range(0, 128, period):
            a0, a1, a2 = g0, g0 + half, g0 + period
            fw.op(fw.dve, lambda a0=a0, a1=a1, a2=a2: nc.vector.tensor_tensor(
                out=t2[a0:a1, :], in0=raw[a1:a2, :], in1=sinT[a1:a2, cs], op=ALU.mult), [raw, sinT], [t2])
            fw.op(fw.dve, lambda a0=a0, a1=a1, a2=a2: nc.vector.tensor_tensor(
                out=t2[a1:a2, :], in0=raw[a0:a1, :], in1=sinT[a0:a1, cs], op=ALU.mult), [raw, sinT], [t2])
        fw.op(fw.dve, lambda: nc.vector.tensor_tensor(out=out_ap, in0=t1[:], in1=t2[:], op=ALU.add),
              [t1, t2], out_bufs)

    def fm_chunk(self, ps, wb, c, xT, xT_b, tb):
        nc, fw = self.nc, self.fw

        def f():
            for k in range(NT):
                ins = nc.tensor.matmul(ps[:], lhsT=wb[:, k, c * 128:(c + 1) * 128],
                                       rhs=xT[:, k, tb * 512:(tb + 1) * 512], start=(k == 0), stop=(k == NT - 1))
            return ins
        fw.op(fw.pe, f, [wb] + xT_b[tb * 4:(tb + 1) * 4], [ps])

    def layer_a(self, s, l, src, src_b, aoT, aoT_b):
        nc, fw = self.nc, self.fw
        W = self.w_in_a[l]
        with ExitStack() as st_kv:
            kT = self.sb(st_kv, "kT", [128, 4, S], BF16)
            kT_b = [[Buf(kT.t) for _ in range(4)] for _ in range(4)]
            v = self.sb(st_kv, "v", [128, NT, 512], BF16)
            v_b = [Buf(v.t) for _ in range(NT)]
            with ExitStack() as st_idx:
                qiT = self.sb(st_idx, "qiT", [128, 8, S], BF16)
                qiT_b = [[Buf(qiT.t) for _ in range(4)] for _ in range(8)]
                kiT = self.sb(st_idx, "kiT", [128, S], BF16)
                kiT_b = [Buf(kiT.t) for _ in range(4)]
                wi = self.sb(st_idx, "wi", [128, NT, 16], F32)
                wi_b = [Buf(wi.t) for _ in range(NT)]
                with ExitStack() as st:
                    xT, xT_b = self.phase_T(st, src, src_b)
                    if self.stopped:
                        return
                    wb = [self.sb(st, "wb%d" % i, [128, NT, 256], BF16) for i in range(2)]
                    raw = [self.sb(st, "raw%d" % i, [128, 512], F32) for i in range(2)]
                    t1 = [self.sb(st, "t1%d" % i, [128, 512], F32) for i in range(2)]
                    t2 = [self.sb(st, "t2%d" % i, [128, 512], F32) for i in range(2)]
                    stg = [self.sb(st, "stg%d" % i, [128, 512], BF16) for i in range(4)]
                    st.enter_context(nc.named_scope(self.tag + "projA"))
                    gi = 0
                    pi = 0
                    si = 0
                    ri = 0
                    with ExitStack() as st_r:
                        cosT, sinT = self.rope_tables(st_r, s, 0, 128)
                        for grp in range(10):
                            col0 = grp * 256
                            w = wb[gi % 2]; gi += 1
                            self.load_w(w, W[:, col0:col0 + 256], 256)
                            for c in range(2):
                                ch = grp * 2 + c
                                for tb in range(4):
                                    ps = self.pb[pi % 4]; pi += 1
                                    self.fm_chunk(ps, w, c, xT, xT_b, tb)
                                    r = ri % 2; ri += 1
                                    if ch < 16:
                                        sg = stg[si % 4]; si += 1
                                        self.rope_post(ps, raw[r], t1[r], t2[r], cosT, sinT, tb, 128, sg[:], [sg])
                                        fw.dma(fw.sp, self.qT_b[ch][tb], sg, self.qT_s[ch, :, tb * 512:(tb + 1) * 512], sg[:])
                                    else:
                                        g = ch - 16
                                        self.rope_post(ps, raw[r], t1[r], t2[r], cosT, sinT, tb, 128,
                                                       kT[:, g, tb * 512:(tb + 1) * 512], [kT_b[g][tb]])
                        fw.barrier()
                        if self.stage("qk"):
                            return
                    with ExitStack() as st_r:
                        cosT, sinT = self.rope_tables(st_r, s, 1, 64)
                        for grp in range(4):
                            col0 = 5120 + grp * 256
                            w = wb[gi % 2]; gi += 1
                            self.load_w(w, W[:, col0:col0 + 256], 256)
                            for c in range(2):
                                ch = grp * 2 + c
                                for tb in range(4):
                                    ps = self.pb[pi % 4]; pi += 1
                                    self.fm_chunk(ps, w, c, xT, xT_b, tb)
                                    r = ri % 2; ri += 1
                                    self.rope_post(ps, raw[r], t1[r], t2[r], cosT, sinT, tb, 64,
                                                   qiT[:, ch, tb * 512:(tb + 1) * 512], [qiT_b[ch][tb]])
                        w = wb[gi % 2]; gi += 1
                        Wk = W[:, 6160:6224].rearrange("(kc p) n -> p kc n", p=128)
                        fw.dma(fw.pool, w, self.wdram, w[:, :, 0:64], Wk)
                        fw.dma(fw.pool, w, self.wdram, w[:, :, 64:128], Wk)
                        fw.dma(fw.pool, w, self.wdram, w[:, :, 128:144],
                               W[:, 6144:6160].rearrange("(kc p) n -> p kc n", p=128))
                        for tb in range(4):
                            ps = self.pb[pi % 4]; pi += 1
                            self.fm_chunk(ps, w, 0, xT, xT_b, tb)
                            r = ri % 2; ri += 1
                            self.rope_post(ps, raw[r], t1[r], t2[r], cosT, sinT, tb, 64,
                                           kiT[:, tb * 512:(tb + 1) * 512], [kiT_b[tb]])
                        for i in range(NT):
                            ps = self.pb[pi % 4]; pi += 1

                            def f(ps=ps, i=i, w=w):
                                for k in range(NT):
                                    ins = nc.tensor.matmul(ps[:, 0:16], lhsT=xT[:, k, i * 128:(i + 1) * 128],
                                                           rhs=w[:, k, 128:144], start=(k == 0), stop=(k == NT - 1))
                                return ins
                            fw.op(fw.pe, f, [w, xT_b[i]], [ps])
                            fw.op(fw.act, lambda ps=ps, i=i: nc.scalar.mul(out=wi[:, i, :], in_=ps[:, 0:16], mul=1.0 / 32.0),
                                  [ps], [wi_b[i]])
                        fw.barrier()
                    self.xT_keep = (xT, xT_b)
                    fw.barrier()
                if self.stage("proj"):
                    return
                self.indexer(qiT, qiT_b, kiT, kiT_b, wi, wi_b, W, v, v_b)
                fw.barrier()
                if self.stage("idx"):
                    return
            self.attn_a(kT, kT_b, v, v_b, aoT, aoT_b)
            fw.barrier()
            if self.stage("attn"):
                return

    def indexer(self, qiT, qiT_b, kiT, kiT_b, wi, wi_b, W, v, v_b):
        nc, fw = self.nc, self.fw
        with ExitStack() as st:
            st.enter_context(nc.named_scope(self.tag + "idx"))
            score = [self.sb(st, "score%d" % i, [128, S], F32) for i in range(2)]
            work = self.sb(st, "work", [128, S], F32)
            m8 = self.sb(st, "m8", [128, 8], F32)
            brow = [self.sb(st, "brow%d" % i, [128, S], BF16) for i in range(2)]
            rl = [self.sb(st, "rl%d" % i, [128, 512], BF16) for i in range(4)]
            diag = [self.sb(st, "diag%d" % i, [128, 16, 128], BF16) for i in range(2)]
            bst = [self.sb(st, "bst%d" % i, [128, 4, 128], BF16) for i in range(2)]
            cnt = {"ri": 0, "pi": 0, "bi": 0, "ti": 0, "di": 0, "si": 0}
            xT, xT_b = self.xT_keep
            wb2 = [self.sb(st, "iwb%d" % i, [128, NT, 128], BF16) for i in range(2)]
            stg2 = [self.sb(st, "istg%d" % i, [128, 512], BF16) for i in range(2)]
            extra = []

            def g_unit(ch, tb):
                w = wb2[ch % 2]
                if tb == 0:
                    self.load_w(w, W[:, 3072 + ch * 128:3072 + (ch + 1) * 128], 128)
                ps = self.pb[cnt["di"] % 4]; cnt["di"] += 1
                self.fm_chunk(ps, w, 0, xT, xT_b, tb)
                sg = stg2[cnt["si"] % 2]; cnt["si"] += 1
                fw.op(fw.act, lambda: nc.scalar.activation(out=sg[:], in_=ps[:], func=AF.Silu), [ps], [sg])
                fw.dma(fw.sp, self.gT_b[ch][tb], sg, self.gT_s[ch, :, tb * 512:(tb + 1) * 512], sg[:])

            def v_unit(vg, tq):
                w = wb2[vg % 2]
                if tq == 0:
                    self.load_w(w, W[:, 2560 + vg * 128:2560 + (vg + 1) * 128], 128)
                ps = self.pb[cnt["di"] % 4]; cnt["di"] += 1

                def f():
                    for tt in range(4):
                        it_ = tq * 4 + tt
                        for k in range(NT):
                            ins = nc.tensor.matmul(ps[:, tt * 128:(tt + 1) * 128], lhsT=xT[:, k, it_ * 128:(it_ + 1) * 128],
                                                   rhs=w[:, k, 0:128], start=(k == 0), stop=(k == NT - 1))
                    return ins
                fw.op(fw.pe, f, [w] + xT_b[tq * 4:tq * 4 + 4], [ps])
                self.copy(fw.act, v[:, tq * 4:tq * 4 + 4, vg * 128:(vg + 1) * 128],
                          ps[:].rearrange("p (c t) -> p c t", c=4), [ps], v_b[tq * 4:tq * 4 + 4])

            for vg in range(4):
                for tq in range(4):
                    extra.append(lambda vg=vg, tq=tq: v_unit(vg, tq))
            for ch in range(16):
                for tb in range(4):
                    extra.append(lambda ch=ch, tb=tb: g_unit(ch, tb))

            def score_tile(i):
                L = (i + 1) * 128
                dg = diag[i % 2]
                sc = score[i % 2]
                for h in range(16):
                    fw.op(fw.act, lambda h=h: nc.scalar.mul(out=dg[:, h, :], in_=self.identf[:], mul=wi[:, i, h:h + 1]),
                          [self.identf, wi_b[i]], [dg])
                nkb = (L + 511) // 512
                seq = [(kb, h) for kb in range(nkb) for h in range(16)]
                pscs = {}
                inflight = {}

                def issue_dots(q):
                    kb, h = seq[q]
                    n = min(512, L - kb * 512)
                    ks = slice(kb * 512, kb * 512 + n)
                    c, half = h // 2, h % 2
                    pd = self.pb[cnt["di"] % 4]; cnt["di"] += 1
                    p0 = half * 64
                    fw.op(fw.pe, lambda: nc.tensor.matmul(
                        pd[:, 0:n], lhsT=qiT[p0:p0 + 64, c, i * 128:(i + 1) * 128], rhs=kiT[p0:p0 + 64, ks],
                        start=True, stop=True), [qiT_b[c][i // 4], kiT_b[kb]], [pd])
                    r = rl[cnt["ri"] % 4]; cnt["ri"] += 1
                    fw.op(fw.act, lambda: nc.scalar.activation(out=r[:, 0:n], in_=pd[:, 0:n], func=AF.Relu), [pd], [r])
                    inflight[q] = r

                def issue_acc(q):
                    kb, h = seq[q]
                    n = min(512, L - kb * 512)
                    ks = slice(kb * 512, kb * 512 + n)
                    if h == 0:
                        pscs[kb] = self.pb[4 + (cnt["pi"] % 2)]; cnt["pi"] += 1
                    psc = pscs[kb]
                    r = inflight.pop(q)
                    fw.op(fw.pe, lambda: nc.tensor.matmul(
                        psc[:, 0:n], lhsT=dg[:, h, :], rhs=r[:, 0:n], start=(h == 0), stop=(h == 15)),
                        [dg, r], [psc])
                    if h == 15:
                        self.copy(fw.act, sc[:, ks], psc[:, 0:n], [psc], [sc])

                LA = 3
                for q in range(min(LA, len(seq))):
                    issue_dots(q)
                for q in range(len(seq)):
                    if q + LA < len(seq):
                        issue_dots(q + LA)
                    issue_acc(q)
                fw.op(fw.pool, lambda: nc.gpsimd.affine_select(
                    out=sc[:, i * 128:(i + 1) * 128], in_=sc[:, i * 128:(i + 1) * 128], pattern=[[-1, 128]],
                    compare_op=ALU.is_ge, fill=self.reg_neg, base=0, channel_multiplier=1), [sc], [sc])

            def topk_tile(i):
                L = (i + 1) * 128
                sc = score[i % 2]
                br = brow[i % 2]
                if i >= 2:
                    cur = sc
                    for it in range(32):
                        fw.op(fw.dve, lambda: nc.vector.max(out=m8[:], in_=cur[:, 0:L]), [cur], [m8])
                        if it < 31:
                            fw.op(fw.dve, lambda: nc.vector.match_replace(
                                out=work[:, 0:L], in_to_replace=m8[:], in_values=cur[:, 0:L], imm_value=-1e30),
                                [cur, m8], [work])
                            cur = work
                    fw.op(fw.dve, lambda: nc.vector.tensor_scalar(
                        out=br[:, 0:L], in0=sc[:, 0:L], scalar1=m8[:, 7:8], scalar2=None, op0=ALU.is_ge),
                        [sc, m8], [br])
                else:
                    fw.op(fw.dve, lambda: nc.vector.tensor_scalar(
                        out=br[:, 0:L], in0=sc[:, 0:L], scalar1=-1e29, scalar2=None, op0=ALU.is_ge),
                        [sc], [br])
                for j4 in range(0, i + 1, 4):
                    nj = min(4, i + 1 - j4)
                    pt = self.ptb[cnt["ti"] % 2]; cnt["ti"] += 1

                    def f():
                        for jj in range(nj):
                            j = j4 + jj
                            ins = nc.tensor.transpose(pt[:, jj * 128:(jj + 1) * 128], br[:, j * 128:(j + 1) * 128],
                                                      self.ident[:])
                        return ins
                    fw.op(fw.pe, f, [br, self.ident], [pt])
                    b = bst[cnt["bi"] % 2]; cnt["bi"] += 1
                    self.copy(fw.act, b[:, 0:nj, :], pt[:, 0:nj * 128].rearrange("p (c t) -> p c t", c=nj), [pt], [b])
                    fw.dma(fw.sp, [self.biasT_b[j4 + jj][i] for jj in range(nj)], b,
                           self.biasT_s[j4:j4 + nj, :, i * 128:(i + 1) * 128].rearrange("j p t -> p j t"),
                           b[:, 0:nj, :])

            score_tile(0)
            for i in range(NT):
                if i + 1 < NT:
                    score_tile(i + 1)
                rem_w = sum(range(i + 1, NT + 1))
                ne = len(extra) if i == NT - 1 else min(len(extra), (len(extra) * (i + 1) + rem_w - 1) // rem_w)
                for _ in range(ne):
                    extra.pop(0)()
                topk_tile(i)
            assert not extra

    def attn_a(self, kT, kT_b, v, v_b, aoT, aoT_b):
        nc, fw = self.nc, self.fw
        with ExitStack() as st:
            st.enter_context(nc.named_scope(self.tag + "attnA"))
            qblk = self.sb(st, "qblk", [128, H, 512], BF16)
            gblk = self.sb(st, "gblk", [128, H, 512], BF16)
            bblk = self.sb(st, "bblk", [128, NT, 512], BF16)
            qblk_b = [Buf(qblk.t) for _ in range(H)]
            gblk_b = [Buf(gblk.t) for _ in range(H)]
            bblk_b = [Buf(bblk.t) for _ in range(NT)]
            pT = [self.sb(st, "pT%d" % i, [128, 512], BF16) for i in range(8)]
            rden = self.sb(st, "rden", [128, 512], F32)
            o = self.sb(st, "o", [128, 512], F32)
            pi = 0
            for b in range(4):
                qs = slice(b * 512, (b + 1) * 512)
                nj = 4 * b + 4
                for h in range(H):
                    fw.dma(fw.sp, qblk_b[h], self.qT_b[h][b], qblk[:, h, :], self.qT_s[h, :, qs])
                    fw.dma(fw.sp, gblk_b[h], self.gT_b[h][b], gblk[:, h, :], self.gT_s[h, :, qs])
                for j in range(nj):
                    t0 = max(0, j - 4 * b)
                    fw.dma(fw.sp, bblk_b[j], [self.biasT_b[j][i] for i in range(4 * b + t0, 4 * b + 4)],
                           bblk[:, j, t0 * 128:512], self.biasT_s[j, :, b * 512 + t0 * 128:(b + 1) * 512])
                for h in range(H):
                    g = h // 4
                    pout = self.pb[4]
                    pden = self.pb[5]

                    def issue_s(j):
                        q0 = max(0, j - 4 * b) * 128
                        ps = self.pb[j % 4]

                        def f():
                            return nc.tensor.matmul(ps[:, q0:512], lhsT=kT[:, g, j * 128:(j + 1) * 128],
                                                    rhs=qblk[:, h, q0:512], start=True, stop=True)
                        fw.op(fw.pe, f, [kT_b[g][j // 4], qblk_b[h]], [ps])
                        pt = pT[j % 8]
                        fw.op(fw.act, lambda: nc.scalar.activation(out=pt[:, q0:512], in_=ps[:, q0:512], func=AF.Exp,
                                                                   scale=SCALE), [ps], [pt])
                        fw.op(fw.dve, lambda: nc.vector.tensor_tensor(out=pt[:, q0:512], in0=pt[:, q0:512],
                                                                      in1=bblk[:, j, q0:512], op=ALU.mult),
                              [pt, bblk_b[j]], [pt])

                    def issue_av(j):
                        q0 = max(0, j - 4 * b) * 128
                        pt = pT[j % 8]
                        fw.op(fw.pe, lambda: nc.tensor.matmul(pout[:, q0:512], lhsT=v[:, j, g * 128:(g + 1) * 128],
                                                              rhs=pt[:, q0:512], start=(j == 0), stop=(j == nj - 1)),
                              [v_b[j], pt], [pout])
                        fw.op(fw.pe, lambda: nc.tensor.matmul(pden[:, q0:512], lhsT=self.ones_bf[:],
                                                              rhs=pt[:, q0:512], start=(j == 0), stop=(j == nj - 1)),
                              [self.ones_bf, pt], [pden])
                    LA = 3
                    for j in range(min(LA, nj)):
                        issue_s(j)
                    for j in range(nj):
                        if j + LA < nj:
                            issue_s(j + LA)
                        issue_av(j)
                    fw.op(fw.act, lambda: nc.scalar.activation(out=rden[:], in_=pden[:], func=AF.Ln), [pden], [rden])
                    fw.op(fw.act, lambda: nc.scalar.activation(out=rden[:], in_=rden[:], func=AF.Exp, scale=-1.0),
                          [rden], [rden])
                    fw.op(fw.dve, lambda: nc.vector.tensor_tensor(out=o[:], in0=pout[:], in1=rden[:], op=ALU.mult),
                          [pout, rden], [o])
                    fw.op(fw.pool, lambda h=h: nc.gpsimd.tensor_tensor(out=aoT[:, h, qs], in0=o[:], in1=gblk[:, h, :],
                                                                       op=ALU.mult),
                          [o, gblk_b[h]], aoT_b[h][4 * b:4 * b + 4])

    def proj_fm_store(self, W, col0, nchunks, xT, xT_b, wb, stg, dst_s, dst_b, act_func, gi0=0):
        nc, fw = self.nc, self.fw
        gi = gi0
        pi = 0
        si = 0
        for grp in range(nchunks // 2):
            w = wb[gi % 2]; gi += 1
            self.load_w(w, W[:, col0 + grp * 256:col0 + grp * 256 + 256], 256)
            for c in range(2):
                ch = grp * 2 + c
                for tb in range(4):
                    ps = self.pb[pi % 4]; pi += 1
                    self.fm_chunk(ps, w, c, xT, xT_b, tb)
                    sg = stg[si % 4]; si += 1
                    if act_func is None:
                        self.copy(self.evac_eng(), sg[:], ps[:], [ps], [sg])
                    else:
                        fw.op(fw.act, lambda ps=ps, sg=sg: nc.scalar.activation(out=sg[:], in_=ps[:], func=act_func),
                              [ps], [sg])
                    fw.dma(fw.sp, dst_b[ch][tb], sg, dst_s[ch, :, tb * 512:(tb + 1) * 512], sg[:])
        return gi

    def layer_b(self, s, l, src, src_b, aoT, aoT_b):
        nc, fw = self.nc, self.fw
        jl = l - 2
        with ExitStack() as st:
            xT, xT_b = self.phase_T(st, src, src_b)
            wb = [self.sb(st, "wb%d" % i, [128, NT, 256], BF16) for i in range(2)]
            stg = [self.sb(st, "stg%d" % i, [128, 512], BF16) for i in range(4)]
            st.enter_context(nc.named_scope(self.tag + "projB"))
            gi = 0
            if l == 2:
                gi = self.proj_fm_store(self.w_kv_b, 0, 16, xT, xT_b, wb, stg, self.kTb_s, self.kTb_b, None, gi)
                vst = self.sb(st, "vst", [128, NT, 256], BF16)
                pi = 0
                for grp in range(8):
                    w = wb[gi % 2]; gi += 1
                    self.load_w(w, self.w_kv_b[:, 2048 + grp * 256:2048 + grp * 256 + 256], 256)
                    for i in range(NT):
                        ps = self.pb[pi % 4]; pi += 1

                        def f(ps=ps, i=i, w=w):
                            for k in range(NT):
                                ins = nc.tensor.matmul(ps[:, 0:256], lhsT=xT[:, k, i * 128:(i + 1) * 128],
                                                       rhs=w[:, k, 0:256], start=(k == 0), stop=(k == NT - 1))
                            return ins
                        fw.op(fw.pe, f, [w, xT_b[i]], [ps])
                        self.copy(self.evac_eng(), vst[:, i, :], ps[:, 0:256], [ps], [vst])
                    for c in range(2):
                        hh = grp * 2 + c
                        fw.dma(fw.sp, self.vb_b[hh], vst, self.vb_s[hh], vst[:, :, c * 128:(c + 1) * 128])
            W = self.w_q_b[jl]
            gi = self.proj_fm_store(W, 0, 16, xT, xT_b, wb, stg, self.qT_s, self.qT_b, None, gi)
            gi = self.proj_fm_store(W, 2048, 16, xT, xT_b, wb, stg, self.gT_s, self.gT_b, AF.Silu, gi)
            fw.barrier()
        self.attn_b(aoT, aoT_b)
        fw.barrier()

    def attn_b(self, aoT, aoT_b):
        nc, fw = self.nc, self.fw
        with ExitStack() as st:
            st.enter_context(nc.named_scope(self.tag + "attnB"))
            kh = [self.sb(st, "kh%d" % i, [128, S], BF16) for i in range(2)]
            vh = [self.sb(st, "vh%d" % i, [128, NT, 128], BF16) for i in range(2)]
            qh = [self.sb(st, "qh%d" % i, [128, S], BF16) for i in range(2)]
            gh = [self.sb(st, "gh%d" % i, [128, S], BF16) for i in range(2)]
            Eb = [self.sb(st, "Eb%d" % i, [128, S], F32) for i in range(3)]
            Sb = [self.sb(st, "Sb%d" % i, [128, S], F32) for i in range(3)]
            Cb = [self.sb(st, "Cb%d" % i, [128, S + 1], F32) for i in range(2)]
            ntot = [self.sb(st, "ntot%d" % i, [128, 1], F32) for i in range(2)]
            ones = self.sb(st, "ones", [128, S], F32)
            Ab = [self.sb(st, "Ab%d" % i, [128, S], BF16) for i in range(2)]
            AT = [self.sb(st, "AT%d" % i, [128, NT, 128], BF16) for i in range(2)]
            fw.op(fw.dve, lambda: nc.vector.memset(ones[:], 1.0), [], [ones])
            for c in Cb:
                fw.op(fw.dve, lambda c=c: nc.vector.memset(c[:, 0:1], 0.0), [], [c])
            iters = [(h, i) for h in range(H) for i in range(NT)]
            N = len(iters)
            pz = self.pzt
            st8 = {"cc": 0, "ti": 0}

            def loadhead(h):
                k_, v_, q_, g_ = kh[h % 2], vh[h % 2], qh[h % 2], gh[h % 2]
                fw.dma(fw.sp, k_, self.kTb_b[h], k_[:], self.kTb_s[h])
                fw.dma(fw.sp, q_, self.qT_b[h], q_[:], self.qT_s[h])
                fw.dma(fw.sp, v_, self.vb_b[h], v_[:], self.vb_s[h])
                fw.dma(fw.sp, g_, self.gT_b[h], g_[:], self.gT_s[h])

            def stageA(n):
                h, i = iters[n]
                if n == 0:
                    loadhead(0)
                if i == 8 and h + 1 < H:
                    loadhead(h + 1)
                k_, q_ = kh[h % 2], qh[h % 2]
                L = (i + 1) * 128
                e, sp = Eb[n % 3], Sb[n % 3]
                for c0 in range(0, L, 1024):
                    nn = min(1024, L - c0)
                    cc = st8["cc"]; st8["cc"] += 1
                    base = (cc % 2) * 1024
                    zb = self.pb[2 * (cc % 2):2 * (cc % 2) + (nn + 511) // 512]

                    def fz():
                        for s0 in range(0, nn, 512):
                            m = min(512, nn - s0)
                            ins = nc.tensor.matmul(pz[:, base + s0:base + s0 + m], lhsT=q_[:, i * 128:(i + 1) * 128],
                                                   rhs=k_[:, c0 + s0:c0 + s0 + m], start=True, stop=True)
                        return ins
                    fw.op(fw.pe, fz, [q_, k_], zb)
                    fw.op(fw.act, lambda: nc.scalar.activation(out=e[:, c0:c0 + nn], in_=pz[:, base:base + nn],
                                                               func=AF.Exp, scale=SCALE), zb, [e])
                fw.op(fw.act, lambda: nc.scalar.activation(out=sp[:, 0:L], in_=e[:, 0:L], func=AF.Ln,
                                                           bias=self.cst[:, 3:4]), [e, self.cst], [sp])
                d0 = i * 128
                fw.op(fw.pool, lambda: nc.gpsimd.affine_select(
                    out=sp[:, d0:d0 + 128], in_=sp[:, d0:d0 + 128], pattern=[[-1, 128]], compare_op=ALU.is_gt,
                    fill=self.reg_zero, base=0, channel_multiplier=1), [sp], [sp])

            def stageB(n):
                h, i = iters[n]
                L = (i + 1) * 128
                sp, cb, nt = Sb[n % 3], Cb[n % 2], ntot[n % 2]
                fw.op(fw.dve, lambda: nc.vector.tensor_tensor_scan(
                    out=cb[:, 1:L + 1], data0=ones[:, 0:L], data1=sp[:, 0:L], initial=0.0, op0=ALU.mult, op1=ALU.add),
                    [ones, sp], [cb])
                fw.op(fw.dve, lambda: nc.vector.tensor_scalar(out=nt[:], in0=cb[:, L:L + 1], scalar1=-1.0,
                                                              scalar2=None, op0=ALU.mult), [cb], [nt])
                fw.op(fw.act, lambda: nc.scalar.activation(out=sp[:, 0:L], in_=cb[:, 0:L], func=AF.Exp,
                                                           bias=nt[:, 0:1]), [cb, nt], [sp])

            def stageB2(n):
                h, i = iters[n]
                L = (i + 1) * 128
                e, sp, A = Eb[n % 3], Sb[n % 3], Ab[n % 2]
                fw.op(fw.dve, lambda: nc.vector.tensor_tensor(out=A[:, 0:L], in0=e[:, 0:L], in1=sp[:, 0:L], op=ALU.mult),
                      [e, sp], [A])
                d0 = i * 128
                fw.op(fw.pool, lambda: nc.gpsimd.affine_select(
                    out=A[:, d0:d0 + 128], in_=A[:, d0:d0 + 128], pattern=[[-1, 128]], compare_op=ALU.is_gt,
                    fill=self.reg_zero, base=0, channel_multiplier=1), [A], [A])

            def stageC(n):
                h, i = iters[n]
                A, at = Ab[n % 2], AT[n % 2]
                for j4 in range(0, i + 1, 4):
                    nj = min(4, i + 1 - j4)
                    pt = self.ptb[st8["ti"] % 2]; st8["ti"] += 1

                    def f():
                        for jj in range(nj):
                            j = j4 + jj
                            ins = nc.tensor.transpose(pt[:, jj * 128:(jj + 1) * 128], A[:, j * 128:(j + 1) * 128],
                                                      self.ident[:])
                        return ins
                    fw.op(fw.pe, f, [A, self.ident], [pt])
                    self.copy(fw.act if (st8["ti"] % 3 == 0) else fw.dve, at[:, j4:j4 + nj, :],
                              pt[:, 0:nj * 128].rearrange("p (c t) -> p c t", c=nj), [pt], [at])

            def stageC2(n):
                h, i = iters[n]
                v_, g_ = vh[h % 2], gh[h % 2]
                at = AT[n % 2]
                po = self.pb[4 + (n % 2)]

                def fo():
                    for j in range(i + 1):
                        ins = nc.tensor.matmul(po[:, 0:128], lhsT=v_[:, j, :], rhs=at[:, j, :],
                                               start=(j == 0), stop=(j == i))
                    return ins
                fw.op(fw.pe, fo, [v_, at], [po])
                fw.op(fw.dve, lambda: nc.vector.tensor_tensor(out=aoT[:, h, i * 128:(i + 1) * 128], in0=po[:, 0:128],
                                                              in1=g_[:, i * 128:(i + 1) * 128], op=ALU.mult),
                      [po, g_], [aoT_b[h][i]])

            for t in range(N + 4):
                if t < N:
                    stageA(t)
                if 0 <= t - 1 < N:
                    stageB(t - 1)
                if 0 <= t - 2 < N:
                    stageB2(t - 2)
                if 0 <= t - 3 < N:
                    stageC(t - 3)
                if 0 <= t - 4 < N:
                    stageC2(t - 4)

    def epilogue(self, s, l, src, src_b, dst, dst_b, aoT, aoT_b, w_out):
        nc, fw = self.nc, self.fw
        with ExitStack() as st:
            st.enter_context(nc.named_scope(self.tag + "epi"))
            wb = [self.sb(st, "ewb%d" % i, [128, NT, 512], BF16) for i in range(2)]
            wp = [self.sb(st, "ewp%d" % i, [128, 2, 512], BF16) for i in range(2)]
            z = [self.sb(st, "z%d" % i, [128, D], F32) for i in range(4)]
            gB = self.sb(st, "gB", [128, D], F32)
            bB = self.sb(st, "bB", [128, D], F32)
            xlnT = self.sb(st, "xlnT", [128, NT, 512], BF16)
            xlnT_b = [Buf(xlnT.t) for _ in range(4)]
            pT = self.sb(st, "ppT", [128, 2, 512], BF16)
            pT_b = [Buf(pT.t) for _ in range(4)]
            pbf = [self.sb(st, "pbf%d" % i, [128, 256], BF16) for i in range(4)]
            xb = [self.sb(st, "exb%d" % i, [128, D], BF16) for i in range(2)]
            sig = [self.sb(st, "sig%d" % i, [128, 512], F32) for i in range(2)]
            tmp = [self.sb(st, "tmp%d" % i, [128, 512], F32) for i in range(2)]
            stats = self.sb(st, "stats", [128, 4, 4, 6], F32)
            mv = self.sb(st, "mv", [128, 4, 2], F32)
            rstd = self.sb(st, "rstd", [128, 4, 1], F32)
            nbias = self.sb(st, "nbias", [128, 1], F32)
            fw.dma(fw.sp, gB, self.wdram, gB[:], self.ln_g[l:l + 1, :].partition_broadcast(128))
            fw.dma(fw.sp, bB, self.wdram, bB[:], self.ln_b[l:l + 1, :].partition_broadcast(128))
            gi = 0
            pi = 0
            ti = 0
            for b in range(4):
                if b == 0:
                    for t in range(4):
                        fw.dma(fw.sp, z[t], src_b[t], z[t][:], src[t * 128:(t + 1) * 128, :])
                for t in range(4):
                    i = 4 * b + t
                    pb_ = pbf[t]
                    fw.dma(fw.pool, pb_, self.wdram, pb_[:], self.p_d[l, s, i * 128:(i + 1) * 128, :])
                    pt = self.ptb[ti % 2]; ti += 1

                    def f2(pt=pt, pb_=pb_):
                        for c in range(2):
                            ins = nc.tensor.transpose(pt[:, c * 128:(c + 1) * 128], pb_[:, c * 128:(c + 1) * 128],
                                                      self.ident[:])
                        return ins
                    fw.op(fw.pe, f2, [pb_, self.ident], [pt])
                    self.copy(fw.act, pT[:, :, t * 128:(t + 1) * 128],
                              pt[:, 0:256].rearrange("p (c t) -> p c t", c=2), [pt], [pT_b[t]])
                for n in range(4):
                    w = wb[gi % 2]; gi += 1
                    ns_ = slice(n * 512, (n + 1) * 512)
                    self.load_w(w, w_out[:, ns_], 512)
                    for t in range(4):
                        i = 4 * b + t
                        ps = self.pb[pi % 4]; pi += 1

                        def f(ps=ps, w=w, i=i):
                            for k in range(NT):
                                ins = nc.tensor.matmul(ps[:], lhsT=aoT[:, k, i * 128:(i + 1) * 128], rhs=w[:, k, :],
                                                       start=(k == 0), stop=(k == NT - 1))
                            return ins
                        fw.op(fw.pe, f, [w] + [aoT_b[k][i] for k in range(H)], [ps])
                        fw.op(fw.dve, lambda ps=ps, t=t, ns_=ns_: nc.vector.scalar_tensor_tensor(
                            out=z[t][:, ns_], in0=z[t][:, ns_], scalar=ALPHA, in1=ps[:], op0=ALU.mult, op1=ALU.add),
                            [z[t], ps], [z[t]])
                        fw.op(fw.dve, lambda t=t, n=n, ns_=ns_: nc.vector.bn_stats(out=stats[:, t, n, :], in_=z[t][:, ns_]),
                              [z[t]], [stats])
                w3 = wb[gi % 2]; gi += 1
                self.load_w(w3, self.w_gate[l][:, 0:512], 512)
                self.load_w(wp[0], self.w_ple[l][:, 0:512], 512)
                for t in range(4):
                    fw.op(fw.dve, lambda t=t: nc.vector.bn_aggr(out=mv[:, t, :], in_=stats[:, t].rearrange("p a b -> p (a b)")),
                          [stats], [mv])
                fw.op(fw.dve, lambda: nc.vector.tensor_scalar(out=rstd[:], in0=mv[:, :, 1:2], scalar1=1e-5, scalar2=None,
                                                              op0=ALU.add), [mv], [rstd])
                fw.op(fw.act, lambda: nc.scalar.sqrt(out=rstd[:], in_=rstd[:]), [rstd], [rstd])
                fw.op(fw.dve, lambda: nc.vector.reciprocal(out=rstd[:], in_=rstd[:]), [rstd], [rstd])
                for t in range(4):
                    i = 4 * b + t
                    zt = z[t]
                    fw.op(fw.dve, lambda zt=zt, t=t: nc.vector.scalar_tensor_tensor(
                        out=zt[:], in0=zt[:], scalar=mv[:, t, 0:1], in1=gB[:], op0=ALU.subtract, op1=ALU.mult),
                        [zt, mv, gB], [zt])
                    fw.op(fw.dve, lambda zt=zt, t=t: nc.vector.scalar_tensor_tensor(
                        out=zt[:], in0=zt[:], scalar=rstd[:, t, 0:1], in1=bB[:], op0=ALU.mult, op1=ALU.add),
                        [zt, rstd, bB], [zt])
                    x_ = xb[t % 2]
                    self.copy(fw.act, x_[:], zt[:], [zt], [x_])
                    for c4 in range(4):
                        pt = self.ptb[ti % 2]; ti += 1

                        def f(pt=pt, x_=x_, c4=c4):
                            for c in range(4):
                                k = c4 * 4 + c
                                ins = nc.tensor.transpose(pt[:, c * 128:(c + 1) * 128], x_[:, k * 128:(k + 1) * 128],
                                                          self.ident[:])
                            return ins
                        fw.op(fw.pe, f, [x_, self.ident], [pt])
                        self.copy(fw.act, xlnT[:, c4 * 4:(c4 + 1) * 4, t * 128:(t + 1) * 128],
                                  pt[:].rearrange("p (c t) -> p c t", c=4), [pt], [xlnT_b[t]])
                for n in range(4):
                    ns_ = slice(n * 512, (n + 1) * 512)
                    w2 = wp[n % 2]
                    if n == 0:
                        w = w3
                    else:
                        w = wb[gi % 2]; gi += 1
                        self.load_w(w, self.w_gate[l][:, ns_], 512)
                        self.load_w(w2, self.w_ple[l][:, ns_], 512)
                    for t in range(4):
                        zt = z[t]
                        ps = self.pb[pi % 4]; pi += 1
                        ps2 = self.pb[4 + (pi % 2)]

                        def f(ps=ps, w=w, t=t):
                            for k in range(NT):
                                ins = nc.tensor.matmul(ps[:], lhsT=xlnT[:, k, t * 128:(t + 1) * 128], rhs=w[:, k, :],
                                                       start=(k == 0), stop=(k == NT - 1))
                            return ins
                        fw.op(fw.pe, f, [w, xlnT_b[t]], [ps])

                        def f3(ps2=ps2, w2=w2, t=t):
                            for k in range(2):
                                ins = nc.tensor.matmul(ps2[:], lhsT=pT[:, k, t * 128:(t + 1) * 128], rhs=w2[:, k, :],
                                                       start=(k == 0), stop=(k == 1))
                            return ins
                        fw.op(fw.pe, f3, [w2, pT_b[t]], [ps2])
                        sg = sig[(n * 4 + t) % 2]
                        tm = tmp[(n * 4 + t) % 2]
                        fw.op(fw.act, lambda ps=ps, sg=sg: nc.scalar.activation(out=sg[:], in_=ps[:], func=AF.Sigmoid),
                              [ps], [sg])
                        fw.op(fw.dve, lambda ps2=ps2, sg=sg, tm=tm: nc.vector.tensor_tensor(out=tm[:], in0=ps2[:], in1=sg[:],
                                                                                            op=ALU.mult), [ps2, sg], [tm])
                        fw.op(fw.dve, lambda zt=zt, tm=tm, ns_=ns_: nc.vector.tensor_tensor(out=zt[:, ns_], in0=zt[:, ns_],
                                                                                            in1=tm[:], op=ALU.add),
                              [zt, tm], [zt])
                for t in range(4):
                    i = 4 * b + t
                    fw.dma(fw.sp, dst_b[i], z[t], dst[i * 128:(i + 1) * 128, :], z[t][:])
                    if b < 3:
                        i2 = i + 4
                        fw.dma(fw.sp, z[t], src_b[i2], z[t][:], src[i2 * 128:(i2 + 1) * 128, :])


_CACHE = {}


def _inv_freq_table():
    p = np.arange(128)
    c0 = (np.float32(10000.0) ** (-(p % 64).astype(np.float32) / np.float32(64))).astype(np.float32)
    c1 = (np.float32(10000.0) ** (-(p % 32).astype(np.float32) / np.float32(32))).astype(np.float32)
    return np.ascontiguousarray(np.stack([c0, c1], axis=1).astype(np.float32))


def kernel(x, p, positions, w_in_a, w_out_a, w_q_b, w_kv_b, w_out_b, ln_g, ln_b, w_ple, w_ple_gate):
    n = 8
    ns = 2
    if "nc" not in _CACHE:
        _CACHE["nc"] = Prog(nseq=ns, nlayers=4).build()
    nc = _CACHE["nc"]
    f32 = lambda a: np.ascontiguousarray(np.asarray(a), dtype=np.float32)
    x = f32(x); p = f32(p)
    pos = np.ascontiguousarray(np.asarray(positions), dtype=np.int32)
    shared = {"w_in_a": f32(w_in_a), "w_out_a": f32(w_out_a), "w_q_b": f32(w_q_b), "w_kv_b": f32(w_kv_b),
              "w_out_b": f32(w_out_b), "ln_g": f32(ln_g), "ln_b": f32(ln_b), "w_ple": f32(w_ple),
              "w_ple_gate": f32(w_ple_gate), "invf": _inv_freq_table()}
    in_maps = []
    for c in range(n):
        m = dict(shared)
        m["x"] = np.ascontiguousarray(x[c * ns:(c + 1) * ns])
        m["p"] = np.ascontiguousarray(p[:, c * ns:(c + 1) * ns])
        m["pos"] = np.ascontiguousarray(pos[c * ns:(c + 1) * ns])
        in_maps.append(m)
    res = run_bass_kernel_spmd(nc, in_maps, core_ids=list(range(n)))
    return np.concatenate([np.asarray(r["out"], dtype=np.float32) for r in res.results], axis=0)
```
